# Optimizing a Trainium2 kernel written in Bass

```python
import math
import jax, jax.numpy as jnp
from jax import lax
import numpy as np

D_MODEL = 1024
BATCH = 1
SEQ = 16384
DEPTH = 4

CHUNK = 64
Q_BLOCK = 128
EPS = 1e-6
N_A_LAYERS = DEPTH // 2
N_B_LAYERS = DEPTH - N_A_LAYERS

A_HEADS = 6
A_HEAD_DIM = 128
A_WIDTH = A_HEADS * A_HEAD_DIM
CONV_K = 4

B_HEADS = 6
QK_NOPE = 128
QK_ROPE = 64
V_HEAD = 128
Q_LORA = 256
KV_LORA = 256
B_WIDTH = B_HEADS * V_HEAD
ROPE_THETA = 10000.0

N_MEM = 256
MEM_HEADS = 4
MEM_HEAD_DIM = 64
MEM_WIDTH = MEM_HEADS * MEM_HEAD_DIM

D_FF = 2816

MIX_WIDTH = A_WIDTH + MEM_WIDTH
A_IN = 4 * A_WIDTH + 2 * A_HEADS + MEM_WIDTH
B_IN = Q_LORA + MEM_WIDTH

kernel_name = "hybrid_gdn_mla_yoco_macaron"


def rms_norm(x, g):
    xf = x.astype(jnp.float32)
    y = xf * lax.rsqrt(jnp.mean(xf * xf, axis=-1, keepdims=True) + EPS)
    return (y * g.astype(jnp.float32)).astype(x.dtype)


def l2_norm(x):
    xf = x.astype(jnp.float32)
    return (xf * lax.rsqrt(jnp.sum(xf * xf, axis=-1, keepdims=True) + EPS)).astype(x.dtype)


def swiglu(x, w_gu, w_down):
    gate, up = jnp.split(x @ w_gu, 2, axis=-1)
    return (jax.nn.silu(gate) * up) @ w_down


def rope_tables(positions, dim):
    inv = ROPE_THETA ** (-jnp.arange(0, dim, 2, dtype=jnp.float32) / dim)
    ang = positions.astype(jnp.float32)[..., None] * inv
    return jnp.cos(ang), jnp.sin(ang)


def apply_rope(x, cos, sin):
    xf = x.astype(jnp.float32)
    x1, x2 = jnp.split(xf, 2, axis=-1)
    c, s = cos[:, :, None, :], sin[:, :, None, :]
    return jnp.concatenate([x1 * c - x2 * s, x1 * s + x2 * c], axis=-1).astype(x.dtype)


def causal_depthwise_conv(x, w):
    k = w.shape[0]
    return lax.conv_general_dilated(
        x, w[:, None, :].astype(x.dtype), window_strides=(1,), padding=[(k - 1, 0)],
        dimension_numbers=("NWC", "WIO", "NWC"), feature_group_count=x.shape[-1])


def chunked_gated_delta_rule(q, k, v, beta, g):
    b, s, h, dk = q.shape
    dv = v.shape[-1]
    n = s // CHUNK
    f32 = jnp.float32

    def chunks(t):
        return t.astype(f32).reshape(b, n, CHUNK, h, -1).transpose(0, 3, 1, 2, 4)

    q, k, v = chunks(q), chunks(k), chunks(v)
    beta = chunks(beta[..., None])[..., 0]
    g_cum = jnp.cumsum(chunks(g[..., None])[..., 0], axis=-1)
    causal = jnp.tril(jnp.ones((CHUNK, CHUNK), dtype=bool))
    strict = jnp.tril(jnp.ones((CHUNK, CHUNK), dtype=bool), -1)
    diff = g_cum[..., :, None] - g_cum[..., None, :]
    decay = jnp.where(causal, jnp.exp(jnp.where(causal, diff, 0.0)), 0.0)
    k_beta = k * beta[..., None]
    t_mat = jnp.where(strict, jnp.einsum("bhnid,bhnjd->bhnij", k_beta, k) * decay, 0.0) \
        + jnp.eye(CHUNK, dtype=f32)
    u = lax.linalg.triangular_solve(t_mat, v * beta[..., None], left_side=True,
                                    lower=True, unit_diagonal=True)
    w = lax.linalg.triangular_solve(t_mat, k_beta * jnp.exp(g_cum)[..., None], left_side=True,
                                    lower=True, unit_diagonal=True)
    qk = jnp.einsum("bhnid,bhnjd->bhnij", q, k) * decay
    g_last = g_cum[..., -1:]
    q_dec = q * jnp.exp(g_cum)[..., None]
    k_dec = k * jnp.exp(g_last - g_cum)[..., None]
    chunk_decay = jnp.exp(g_last[..., 0])
    xs = tuple(jnp.moveaxis(t, 2, 0) for t in (q_dec, k_dec, w, u, qk, chunk_decay))

    def step(state, inp):
        q_c, k_c, w_c, u_c, qk_c, d_c = inp
        v_new = u_c - jnp.einsum("bhcd,bhde->bhce", w_c, state)
        out = jnp.einsum("bhcd,bhde->bhce", q_c, state) + jnp.einsum("bhij,bhje->bhie", qk_c, v_new)
        state = state * d_c[..., None, None] + jnp.einsum("bhcd,bhce->bhde", k_c, v_new)
        return state, out

    _, o = lax.scan(step, jnp.zeros((b, h, dk, dv), f32), xs)
    return o.transpose(1, 0, 3, 2, 4).reshape(b, s, h, dv)


def gated_deltanet(qkv, gate, b_raw, a_raw, conv_w, A_log, dt_bias, out_gain):
    b, s, _ = qkv.shape
    qkv = jax.nn.silu(causal_depthwise_conv(qkv, conv_w))
    q, k, v = (t.reshape(b, s, A_HEADS, A_HEAD_DIM) for t in jnp.split(qkv, 3, axis=-1))
    q = l2_norm(q) * A_HEAD_DIM ** -0.5
    k = l2_norm(k)
    beta = jax.nn.sigmoid(b_raw.astype(jnp.float32))
    g = -jnp.exp(A_log.astype(jnp.float32)) * jax.nn.softplus(
        a_raw.astype(jnp.float32) + dt_bias.astype(jnp.float32))
    o = chunked_gated_delta_rule(q, k, v, beta, g).astype(qkv.dtype)
    o = rms_norm(o, out_gain) * jax.nn.silu(gate.reshape(b, s, A_HEADS, A_HEAD_DIM))
    return o.reshape(b, s, A_WIDTH)


def mla_attention(q_nope, q_rope, k_nope, k_rope, v):
    b, s, h, _ = q_nope.shape
    nb = s // Q_BLOCK
    scale = (QK_NOPE + QK_ROPE) ** -0.5
    key_chunk = jnp.arange(s) // CHUNK
    qn = q_nope.reshape(b, nb, Q_BLOCK, h, QK_NOPE).swapaxes(0, 1)
    qr = q_rope.reshape(b, nb, Q_BLOCK, h, QK_ROPE).swapaxes(0, 1)

    def block(args):
        i, qn_b, qr_b = args
        sc = (jnp.einsum("bqhd,bkhd->bhqk", qn_b, k_nope)
              + jnp.einsum("bqhd,bkd->bhqk", qr_b, k_rope)).astype(jnp.float32) * scale
        q_chunk = (i * Q_BLOCK + jnp.arange(Q_BLOCK)) // CHUNK
        mask = key_chunk[None, :] <= q_chunk[:, None]
        p = jax.nn.softmax(jnp.where(mask, sc, -jnp.inf), axis=-1).astype(v.dtype)
        return jnp.einsum("bhqk,bkhd->bqhd", p, v)

    out = lax.map(block, (jnp.arange(nb), qn, qr))
    return out.swapaxes(0, 1).reshape(b, s, h * V_HEAD)


def memory_attention(q, mem_kv):
    b, s, _ = q.shape
    q = q.reshape(b, s, MEM_HEADS, MEM_HEAD_DIM)
    k, v = (t.reshape(b, N_MEM, MEM_HEADS, MEM_HEAD_DIM) for t in jnp.split(mem_kv, 2, axis=-1))
    sc = jnp.einsum("bqhd,bmhd->bhqm", q, k).astype(jnp.float32) * MEM_HEAD_DIM ** -0.5
    p = jax.nn.softmax(sc, axis=-1).astype(v.dtype)
    return jnp.einsum("bhqm,bmhd->bqhd", p, v).reshape(b, s, MEM_WIDTH)


def setup_inputs(seed: int = 0) -> dict:
    key = jax.random.key(seed)
    ks = iter(jax.random.split(key, 40))
    f32 = jnp.float32

    def dense(shape, fan_in):
        return jax.random.normal(next(ks), shape, f32) * fan_in ** -0.5

    def gain(shape):
        return 1.0 + 0.02 * jax.random.normal(next(ks), shape, f32)

    x = jax.random.normal(next(ks), (BATCH, SEQ, D_MODEL), f32)
    mem = jax.random.normal(next(ks), (BATCH, N_MEM, D_MODEL), f32)
    offset = jax.random.randint(next(ks), (BATCH, 1), 0, 64, dtype=jnp.int32) * CHUNK
    positions = (offset + jnp.arange(SEQ, dtype=jnp.int32)[None, :]).astype(jnp.int32)

    ffn1_norm = gain((DEPTH, D_MODEL))
    ffn1_w_gu = dense((DEPTH, D_MODEL, 2 * D_FF), D_MODEL)
    ffn1_w_down = dense((DEPTH, D_FF, D_MODEL), D_FF)
    mix_norm = gain((DEPTH, D_MODEL))
    ffn2_norm = gain((DEPTH, D_MODEL))
    ffn2_w_gu = dense((DEPTH, D_MODEL, 2 * D_FF), D_MODEL)
    ffn2_w_down = dense((DEPTH, D_FF, D_MODEL), D_FF)
    w_out = dense((DEPTH, MIX_WIDTH, D_MODEL), MIX_WIDTH)
    mem_norm = gain((D_MODEL,))
    w_mem_kv = dense((DEPTH, D_MODEL, 2 * MEM_WIDTH), D_MODEL)

    a_w_in = dense((N_A_LAYERS, D_MODEL, A_IN), D_MODEL)
    a_conv = dense((N_A_LAYERS, CONV_K, 3 * A_WIDTH), CONV_K)
    a_A_log = jnp.log(jax.random.uniform(next(ks), (N_A_LAYERS, A_HEADS), f32, 1.0, 16.0))
    dt = jnp.exp(jax.random.uniform(next(ks), (N_A_LAYERS, A_HEADS), f32,
                                    math.log(1e-3), math.log(1e-1)))
    a_dt_bias = dt + jnp.log(-jnp.expm1(-dt))
    a_out_norm = gain((N_A_LAYERS, A_HEAD_DIM))

    b_w_in = dense((N_B_LAYERS, D_MODEL, B_IN), D_MODEL)
    b_q_norm = gain((N_B_LAYERS, Q_LORA))
    b_w_uq = dense((N_B_LAYERS, Q_LORA, B_HEADS * (QK_NOPE + QK_ROPE)), Q_LORA)

    kv_in_norm = gain((D_MODEL,))
    w_dkv = dense((D_MODEL, KV_LORA + QK_ROPE), D_MODEL)
    kv_lat_norm = gain((KV_LORA,))
    w_ukv = dense((KV_LORA, B_HEADS * (QK_NOPE + V_HEAD)), KV_LORA)
    final_norm = gain((D_MODEL,))

    return {"x": x, "mem": mem, "positions": positions,
            "ffn1_norm": ffn1_norm, "ffn1_w_gu": ffn1_w_gu, "ffn1_w_down": ffn1_w_down,
            "mix_norm": mix_norm,
            "ffn2_norm": ffn2_norm, "ffn2_w_gu": ffn2_w_gu, "ffn2_w_down": ffn2_w_down,
            "w_out": w_out, "mem_norm": mem_norm, "w_mem_kv": w_mem_kv,
            "a_w_in": a_w_in, "a_conv": a_conv, "a_A_log": a_A_log, "a_dt_bias": a_dt_bias,
            "a_out_norm": a_out_norm,
            "b_w_in": b_w_in, "b_q_norm": b_q_norm, "b_w_uq": b_w_uq,
            "kv_in_norm": kv_in_norm, "w_dkv": w_dkv, "kv_lat_norm": kv_lat_norm, "w_ukv": w_ukv,
            "final_norm": final_norm}


def reference(x, mem, positions, ffn1_norm, ffn1_w_gu, ffn1_w_down, mix_norm,
              ffn2_norm, ffn2_w_gu, ffn2_w_down, w_out, mem_norm, w_mem_kv,
              a_w_in, a_conv, a_A_log, a_dt_bias, a_out_norm,
              b_w_in, b_q_norm, b_w_uq, kv_in_norm, w_dkv, kv_lat_norm, w_ukv, final_norm):
    b, s, _ = x.shape
    mem_n = rms_norm(mem, mem_norm)
    cos, sin = rope_tables(positions, QK_ROPE)

    for i in range(N_A_LAYERS):
        l = i
        x = x + 0.5 * swiglu(rms_norm(x, ffn1_norm[l]), ffn1_w_gu[l], ffn1_w_down[l])
        h = rms_norm(x, mix_norm[l]) @ a_w_in[i]
        qkv, gate, b_raw, a_raw, q_mem = jnp.split(
            h, [3 * A_WIDTH, 4 * A_WIDTH, 4 * A_WIDTH + A_HEADS, 4 * A_WIDTH + 2 * A_HEADS], axis=-1)
        o_a = gated_deltanet(qkv, gate, b_raw, a_raw, a_conv[i], a_A_log[i], a_dt_bias[i], a_out_norm[i])
        o_m = memory_attention(q_mem, mem_n @ w_mem_kv[l])
        x = x + jnp.concatenate([o_a, o_m], axis=-1) @ w_out[l]
        x = x + 0.5 * swiglu(rms_norm(x, ffn2_norm[l]), ffn2_w_gu[l], ffn2_w_down[l])

    ckr = rms_norm(x, kv_in_norm) @ w_dkv
    c_kv = rms_norm(ckr[..., :KV_LORA], kv_lat_norm)
    k_rope = apply_rope(ckr[..., None, KV_LORA:], cos, sin)[:, :, 0]
    k_nope, v_mla = jnp.split((c_kv @ w_ukv).reshape(b, s, B_HEADS, QK_NOPE + V_HEAD), [QK_NOPE], axis=-1)

    for j in range(N_B_LAYERS):
        l = N_A_LAYERS + j
        x = x + 0.5 * swiglu(rms_norm(x, ffn1_norm[l]), ffn1_w_gu[l], ffn1_w_down[l])
        h = rms_norm(x, mix_norm[l]) @ b_w_in[j]
        cq, q_mem = jnp.split(h, [Q_LORA], axis=-1)
        q = (rms_norm(cq, b_q_norm[j]) @ b_w_uq[j]).reshape(b, s, B_HEADS, QK_NOPE + QK_ROPE)
        q_nope, q_rope = jnp.split(q, [QK_NOPE], axis=-1)
        q_rope = apply_rope(q_rope, cos, sin)
        o_b = mla_attention(q_nope, q_rope, k_nope, k_rope, v_mla)
        o_m = memory_attention(q_mem, mem_n @ w_mem_kv[l])
        x = x + jnp.concatenate([o_b, o_m], axis=-1) @ w_out[l]
        x = x + 0.5 * swiglu(rms_norm(x, ffn2_norm[l]), ffn2_w_gu[l], ffn2_w_down[l])

    return rms_norm(x, final_norm)
```

```python
import numpy as np
import concourse.bass as bass
import concourse.mybir as mybir
from concourse.bass_utils import run_bass_kernel_spmd

F32 = mybir.dt.float32
BF16 = mybir.dt.bfloat16
AF = mybir.ActivationFunctionType
ALU = mybir.AluOpType

NCORES = 8
S = 16384
D = 1024
T = S // NCORES
NTB = T // 512
KD = D // 128
DEPTH = 4
DFF = 2816
EPS = 1e-6
ENGS = ("pe", "act", "dve", "pool", "sp")


def _bank_of(key):
    n = key[0]
    if n == "ps":
        return key[1]
    if n == "ps1":
        return 1
    if n == "pst":
        return 2
    if n in ("psg", "pss"):
        return 3
    if n == "ps4":
        return 4
    if n == "ps5":
        return 5
    if n == "ps6":
        return 6
    if n in ("ps7", "lsum"):
        return 7
    if n == "olat":
        return 4 + key[1] // 2
    if n == "sT":
        return 2 * key[1] + (0 if key[2] == 0 else 1)
    return None


class Op:
    __slots__ = ("eng", "fn", "idx", "signal", "sigval", "dma_sem", "dma_val", "waits")


class Prog:
    def __init__(self, nc):
        self.nc = nc
        self.ops = {e: [] for e in ENGS}
        self.last_w = {}
        self.readers = {}
        self.waited = {e: {x: -1 for x in ENGS} for e in ENGS}
        self.waited_dma = {e: {} for e in ENGS}
        self.last_touch = {}
        self.dma_cnt = {}
        self.dma_inc = {}
        self.dma_last = {}

    def capture_begin(self):
        self._cap = []

    def capture_end(self):
        c, self._cap = self._cap, None
        return c

    def replay(self, a, b):
        i = j = 0
        while i < len(a) or j < len(b):
            if j >= len(b) or (i < len(a) and i * len(b) <= j * len(a)):
                self.op(*a[i])
                i += 1
            else:
                self.op(*b[j])
                j += 1

    def op(self, eng, fn, reads=(), writes=(), dma=None, sync_same=True):
        if getattr(self, "_cap", None) is not None:
            self._cap.append((eng, fn, list(reads), list(writes), dma, sync_same))
            return None
        o = Op()
        o.eng, o.fn, o.signal, o.sigval = eng, fn, False, 0
        o.idx = len(self.ops[eng])
        o.dma_sem, o.dma_val = None, 0
        if dma is not None:
            inc = 16
            if isinstance(dma, tuple):
                dma, inc = dma
            self.dma_inc[dma] = inc
            self.dma_cnt[dma] = self.dma_cnt.get(dma, 0) + 1
            o.dma_sem, o.dma_val = dma, inc * self.dma_cnt[dma]
            self.dma_last[dma] = o
        deps = []
        for k in reads:
            w = self.last_w.get(k)
            if w is not None:
                deps.append(w)
        for k in writes:
            w = self.last_w.get(k)
            if w is not None:
                deps.append(w)
            deps.extend(self.readers.get(k, ()))
        banks = set()
        for k in list(reads) + list(writes):
            b_ = _bank_of(k)
            if b_ is not None:
                banks.add(b_)
        for b_ in banks:
            t = self.last_touch.get(b_)
            if t is not None and t.eng != eng:
                deps.append(t)
            self.last_touch[b_] = o
        best = {}
        bestd = {}
        for d in deps:
            if d is o:
                continue
            if d.dma_sem is not None:
                if d.dma_val > self.waited_dma[eng].get(d.dma_sem, 0):
                    if d.dma_sem not in bestd or bestd[d.dma_sem].dma_val < d.dma_val:
                        bestd[d.dma_sem] = d
            else:
                if d.eng == eng and not sync_same:
                    continue
                if d.idx > self.waited[eng][d.eng]:
                    if d.eng not in best or best[d.eng].idx < d.idx:
                        best[d.eng] = d
        o.waits = list(best.values()) + list(bestd.values())
        for d in best.values():
            d.signal = True
            self.waited[eng][d.eng] = d.idx
        for d in bestd.values():
            self.waited_dma[eng][d.dma_sem] = d.dma_val
        for k in reads:
            self.readers.setdefault(k, []).append(o)
        for k in writes:
            self.last_w[k] = o
            self.readers[k] = []
        self.ops[eng].append(o)
        return o

    def mm(self, out, lhsT, rhs, start, stop, r, w, skip=False):
        if skip:
            return self.op("pe", lambda e: e.matmul(out, lhsT, rhs, start=start, stop=stop,
                                                    skip_group_check=True), r, w, sync_same=False)
        return self.op("pe", lambda e: e.matmul(out, lhsT, rhs, start=start, stop=stop),
                       r, w, sync_same=False)

    def act(self, out, in_, func, r, w, bias=None, scale=1.0, accum_out=None):
        kw = {}
        if bias is not None:
            kw["bias"] = bias
        if accum_out is not None:
            kw["accum_out"] = accum_out
        return self.op("act", lambda e: e.activation(out, in_, func, scale=scale, **kw), r, w)

    def tt(self, eng, out, in0, in1, op, r, w):
        return self.op(eng, lambda e: e.tensor_tensor(out, in0, in1, op), r, w)

    def ts(self, eng, out, in0, s1, s2, op0, op1, r, w, accum_out=None):
        if op1 is None:
            return self.op(eng, lambda e: e.tensor_scalar(out, in0, s1, None, op0), r, w)
        if accum_out is not None:
            return self.op(eng, lambda e: e.tensor_scalar(out, in0, s1, s2, op0, op1, accum_out), r, w)
        return self.op(eng, lambda e: e.tensor_scalar(out, in0, s1, s2, op0, op1), r, w)

    def stt(self, eng, out, in0, scalar, in1, op0, op1, r, w):
        return self.op(eng, lambda e: e.scalar_tensor_tensor(out, in0, scalar, in1, op0, op1), r, w)

    def copy(self, eng, out, in_, r, w):
        if eng == "act":
            return self.op(eng, lambda e: e.copy(out, in_), r, w)
        return self.op(eng, lambda e: e.tensor_copy(out, in_), r, w)

    def memset(self, eng, ap, val, w):
        return self.op(eng, lambda e: e.memset(ap, val), (), w)

    def dma(self, q, out, in_, sem, r, w):
        return self.op(q, lambda e: e.dma_start(out=out, in_=in_), r, w, dma=sem)

    def wait_all(self, eng, keys):
        return self.op(eng, None, keys, ())

    def cc(self, kind, ins, outs, sem, r, w):
        return self.op("pool", lambda e: e.collective_compute(
            kind, ALU.bypass, replica_groups=[list(range(NCORES))], ins=ins, outs=outs),
            r, w, dma=(sem, 1))

    def barrier(self):
        keys = []
        for e in ENGS:
            last = None
            for o in reversed(self.ops[e]):
                if o.dma_sem is None and o.fn is not None:
                    last = o
                    break
            if last is not None:
                self.last_w[("__bar__", e)] = last
                self.readers[("__bar__", e)] = []
                keys.append(("__bar__", e))
        for n, o in self.dma_last.items():
            self.last_w[("__bard__", n)] = o
            self.readers[("__bard__", n)] = []
            keys.append(("__bard__", n))
        for e in ENGS:
            self.op(e, None, keys, ())

    def emit(self):
        nc = self.nc
        for e in ENGS:
            c = 0
            for o in self.ops[e]:
                if o.signal:
                    c += 1
                o.sigval = c
        import contextlib
        with contextlib.ExitStack() as st:
            esem = {e: st.enter_context(nc.semaphore("s_" + e)) for e in ENGS}
            dsem = {n: st.enter_context(nc.semaphore("d_" + n)) for n in self.dma_cnt}
            block = st.enter_context(nc.Block())

            def run(ename, eng):
                for o in self.ops[ename]:
                    for d in o.waits:
                        if d.dma_sem is not None:
                            eng.wait_ge(dsem[d.dma_sem], d.dma_val)
                        else:
                            eng.wait_ge(esem[d.eng], d.sigval)
                    if o.fn is None:
                        continue
                    ins = o.fn(eng)
                    if o.dma_sem is not None:
                        ins.then_inc(dsem[o.dma_sem], self.dma_inc[o.dma_sem])
                    elif o.signal:
                        ins.then_inc(esem[ename], 1)

            @block.tensor
            def _(eng):
                run("pe", eng)

            @block.scalar
            def _(eng):
                run("act", eng)

            @block.vector
            def _(eng):
                run("dve", eng)

            @block.gpsimd
            def _(eng):
                run("pool", eng)

            @block.sync
            def _(eng):
                run("sp", eng)


H = 6
NEG = -30000.0
SCALE_B = 192.0 ** -0.5
TWO_PI = 6.283185307179586


def _col(v, k):
    return np.ascontiguousarray(np.asarray(v, np.float32).reshape(k, 128).T)


def _bc(v):
    v = np.atleast_1d(np.asarray(v, np.float32))
    return np.ascontiguousarray(np.broadcast_to(v[None, :], (128, v.shape[0])))


class SmallPack:
    def __init__(self):
        self.cols, self.off, self.n = [], {}, 0

    def add(self, name, arr):
        arr = np.asarray(arr, np.float32)
        assert arr.shape[0] == 128, (name, arr.shape)
        self.off[name] = (self.n, arr.shape[1])
        self.cols.append(arr)
        self.n += arr.shape[1]

    def build(self):
        return np.ascontiguousarray(np.concatenate(self.cols, axis=1))


def pack_small(inp, c):
    sp = SmallPack()
    for l in range(DEPTH):
        sp.add(f"ffn1_norm{l}", _col(inp["ffn1_norm"][l], 8))
        sp.add(f"mix_norm{l}", _col(inp["mix_norm"][l], 8))
        sp.add(f"ffn2_norm{l}", _col(inp["ffn2_norm"][l], 8))
    sp.add("kv_in_norm", _col(inp["kv_in_norm"], 8))
    sp.add("final_norm", _col(inp["final_norm"], 8))
    sp.add("mem_norm", _col(inp["mem_norm"], 8))
    h = c % H
    for i in range(2):
        cw = np.asarray(inp["a_conv"][i], np.float32)
        cols = []
        for cc in range(3):
            for j in range(4):
                cols.append(cw[j, cc * 768 + h * 128: cc * 768 + (h + 1) * 128])
        sp.add(f"conv{i}", np.stack(cols, axis=1))
        sp.add(f"alog{i}", _bc(inp["a_A_log"][i][h]))
        sp.add(f"dtb{i}", _bc(inp["a_dt_bias"][i][h]))
        sp.add(f"ognorm{i}", _bc(inp["a_out_norm"][i]))
    for j in range(2):
        sp.add(f"bqnorm{j}", _col(inp["b_q_norm"][j], 2))
    sp.add("kvlat", _col(inp["kv_lat_norm"], 2))
    inv = 10000.0 ** (-np.arange(0, 64, 2, dtype=np.float32) / 64.0)
    invf = np.zeros((128, 1), np.float32)
    invf[0:32, 0] = inv
    invf[32:64, 0] = inv
    sp.add("invf", invf)
    sgn = np.zeros((128, 1), np.float32)
    sgn[0:32] = -1.0
    sgn[32:64] = 1.0
    sp.add("sgn", sgn)
    oh = np.zeros((128, 8), np.float32)
    oh[:, c] = 1.0
    sp.add("onehot", oh)
    p = np.arange(128)
    same = (p[:, None] // 64) == (p[None, :] // 64)
    sp.add("ident", np.eye(128, dtype=np.float32))
    sp.add("ltri", (same & (p[:, None] <= p[None, :])).astype(np.float32))
    sp.add("bones", same.astype(np.float32))
    low = same & (p[:, None] >= p[None, :])
    sp.add("mneg", np.where(low, 0.0, NEG).astype(np.float32))
    sp.add("mnegT", np.where(low.T, 0.0, NEG).astype(np.float32))
    sp.add("sneg", np.where(same & (p[:, None] > p[None, :]), -1.0, 0.0).astype(np.float32))
    cind = np.zeros((128, 2), np.float32)
    cind[0:64, 0] = 1.0
    cind[64:128, 1] = 1.0
    sp.add("cind", cind)
    vis = np.zeros((128, 16, 128, 2), np.float32)
    for i in range(16):
        gi = 16 * c + i
        kb = np.arange(128)
        for qc in range(2):
            qchunk = 2 * gi + qc
            vis[0:64, i, :, qc] = np.where(2 * kb <= qchunk, 0.0, NEG)[None, :]
            vis[64:128, i, :, qc] = np.where(2 * kb + 1 <= qchunk, 0.0, NEG)[None, :]
    return sp, np.ascontiguousarray(vis.reshape(128, -1))


def _swap_rope(w):
    return np.concatenate([w[..., 32:64], w[..., 0:32]], axis=-1)


def host_inputs(inp, c):
    f = lambda a: np.ascontiguousarray(np.asarray(a, np.float32))
    m = {}
    x = np.asarray(inp["x"], np.float32)[0]
    m["xT_in"] = f(x[c * T:(c + 1) * T, :].T)
    m["memT"] = f(np.asarray(inp["mem"], np.float32)[0].T)
    pos = np.asarray(inp["positions"])[0, c * T:(c + 1) * T].astype(np.int32)
    m["pos"] = np.ascontiguousarray(np.broadcast_to(pos[None, :], (128, T)))
    for l in range(DEPTH):
        for which in (1, 2):
            gu = np.asarray(inp[f"ffn{which}_w_gu"][l], np.float32)[c * 128:(c + 1) * 128, :]
            dn = np.zeros((3072, D), np.float32)
            dn[0:DFF] = np.asarray(inp[f"ffn{which}_w_down"][l], np.float32)
            dn = dn[c * 384:(c + 1) * 384]
            m[f"wp_f{which}_{l}"] = f(np.concatenate([gu.ravel(), dn.ravel()]).reshape(128, -1))
        parts = [np.asarray(inp["w_out"][l], np.float32)[c * 128:(c + 1) * 128].ravel(),
                 np.asarray(inp["w_mem_kv"][l], np.float32)[c * 128:(c + 1) * 128].ravel()]
        if l >= 2:
            parts.append(np.asarray(inp["b_w_in"][l - 2], np.float32)[c * 128:(c + 1) * 128].ravel())
        m[f"wp_mx_{l}"] = f(np.concatenate(parts).reshape(128, -1))
    h = c % H
    aw = []
    for i in range(2):
        w = np.asarray(inp["a_w_in"][i], np.float32)
        aw.append(np.concatenate([
            w[:, h * 128:(h + 1) * 128], w[:, 768 + h * 128:768 + (h + 1) * 128],
            w[:, 1536 + h * 128:1536 + (h + 1) * 128], w[:, 2304 + h * 128:2304 + (h + 1) * 128],
            w[:, 3072 + h:3073 + h], w[:, 3078 + h:3079 + h], w[:, 3084:3340]], axis=1))
    m["awin"] = f(np.stack(aw))
    wq = []
    for j in range(2):
        w = np.asarray(inp["b_w_uq"][j], np.float32)
        cols = []
        for hh in range(H):
            nope = w[:, hh * 192:hh * 192 + 128]
            rope = w[:, hh * 192 + 128:hh * 192 + 192]
            cols += [nope, rope, _swap_rope(rope)]
        wq.append(np.concatenate(cols, axis=1))
    m["wuq"] = f(np.stack(wq))
    wd = np.asarray(inp["w_dkv"], np.float32)
    m["wdkv"] = f(np.concatenate([wd[:, 0:256], wd[:, 256:320], _swap_rope(wd[:, 256:320])], axis=1))
    wk = np.asarray(inp["w_ukv"], np.float32)
    m["wukT"] = f(np.stack([wk[:, hh * 256:hh * 256 + 128].T for hh in range(H)]))
    m["wuv"] = f(np.concatenate([wk[:, hh * 256 + 128:hh * 256 + 256] for hh in range(H)], axis=1))
    sp, vis = pack_small(inp, c)
    m["small"] = sp.build()
    m["visb"] = vis
    return m


class Rot:
    def __init__(self, name, tiles):
        self.name, self.tiles, self.i = name, tiles, 0

    def next(self):
        j = self.i % len(self.tiles)
        self.i += 1
        return self.tiles[j], (self.name, j)


NF1 = 128 * 5632 + 384 * D
OFF_DN = 128 * 5632


class Builder:
    def __init__(self, small_off, n_small, stop_after=None, dbg=False):
        self.stop_after = stop_after
        nc = bass.Bass("TRN2", target_bir_lowering=False)
        self.nc = nc
        self.P = Prog(nc)
        self.small_off = small_off
        dt = nc.dram_tensor
        ext = lambda name, shape, d=F32: dt(name, shape, d, kind="ExternalInput").ap()
        self.xT_in = ext("xT_in", [D, T])
        self.memT_in = ext("memT", [D, 256])
        self.pos_in = ext("pos", [128, T], mybir.dt.int32)
        self.small_in = ext("small", [128, n_small])
        self.visb_in = ext("visb", [128, 4096])
        self.awin = ext("awin", [2, D, 770])
        self.wuq_in = ext("wuq", [2, 256, 1536])
        self.wdkv_in = ext("wdkv", [D, 384])
        self.wukT_in = ext("wukT", [H, 128, 256])
        self.wuv_in = ext("wuv", [256, 768])
        self.wp = {}
        self.wfull = {}
        self.wsb = {}
        self.gsize = {}
        for l in range(DEPTH):
            for part in ("f1", "mx", "f2"):
                n = NF1 if part != "mx" else (128 * 1024 + 128 * 512 + (128 * 512 if l >= 2 else 0))
                g = f"{part}_{l}"
                self.gsize[g] = n
                self.wp[g] = ext(f"wp_{g}", [128, n // 128])
                self.wsb[g] = dt(f"wsb_{g}", [128, n // 128], BF16)
                self.wfull[g] = dt(f"wfull_{g}", [1024, n // 128], BF16)
        self.outT = dt("outT", [D, T], F32, kind="ExternalOutput").ap()
        self.xn_loc = dt("xn_loc", [D, T], BF16)
        self.xn_all = dt("xn_all", [NCORES * D, T], BF16)
        self.mix_loc = dt("mix_loc", [128, S], BF16)
        self.mix_all = dt("mix_all", [NCORES * 128, S], BF16)
        self.kT_loc = dt("kT_loc", [384, T], BF16)
        self.kT_all = dt("kT_all", [NCORES * 384, T], BF16)
        self.c_loc = dt("c_loc", [T, 256], BF16)
        self.c_all = dt("c_all", [S, 256], BF16)

        A = nc.alloc_sbuf_tensor
        self.xT = A("xT", [128, KD, T], F32)
        self.xn = A("xn", [128, KD, T], BF16)
        self.small = A("small_sb", [128, n_small], F32)
        self.ones_bf = A("ones_bf", [128, 128], BF16)
        self.ones_f = A("ones_f", [128, 128], F32)
        self.ident_bf = A("ident_bf", [128, 128], BF16)
        self.sq = A("sq", [128, KD, 512], BF16)
        self.rstd = A("rstd", [128, 512], F32)
        self.memn = A("memn", [128, KD, 256], BF16)
        self.ARENA = 88 * 1024
        self.arena = A("arena", [128, self.ARENA // 4], F32)
        self.aoff = 0
        self.ps2 = [nc.alloc_psum_tensor(f"pp{i}", [128, 1024], F32) for i in range(4)]
        self.ps = [self.ps2[i // 2][:, (i % 2) * 512:(i % 2 + 1) * 512] for i in range(8)]
        self.ffn_groups = [(g * 512, 4) for g in range(5)] + [(2560, 2)]
        self.dbg_outs = {}
        self.dbg = dbg

    def sm(self, name):
        o, w = self.small_off[name]
        return self.small[:, o:o + w]

    def arena_reset(self):
        self.P.barrier()
        self.aoff = 0

    def carve(self, shape, dtype):
        esz = 2 if dtype == BF16 else 4
        n = 1
        for s_ in shape[1:]:
            n *= s_
        nbytes = (n * esz + 31) // 32 * 32
        assert self.aoff + nbytes <= self.ARENA, (self.aoff, nbytes)
        a = self.arena[:, self.aoff // 4:(self.aoff + nbytes) // 4]
        self.aoff += nbytes
        if dtype != F32:
            a = a.bitcast(dtype)
        a = a[:, 0:n]
        if len(shape) == 3:
            a = a.rearrange("p (a b) -> p a b", a=shape[1])
        elif len(shape) == 4:
            a = a.rearrange("p (a b c) -> p a b c", a=shape[1], b=shape[2])
        return a

    def dump(self, name, ap, shape, keys, dtype=F32):
        if not self.dbg:
            return
        t = self.nc.dram_tensor("dbg_" + name, shape, dtype, kind="ExternalOutput").ap()
        self.P.dma("sp", t, ap, "dbg", keys, [("dbgout", name)])
        self.dbg_outs[name] = t

    def wview(self, g):
        n = self.gsize[g]
        return self.wfull[g].ap().rearrange("(r a) b -> r (a b)", a=128)

    def w_cast(self, g):
        n = self.gsize[g] // 128
        step = (n + 3) // 4
        for a in range(0, n, step):
            b = min(n, a + step)
            self.P.dma("pool", self.wsb[g][:, a:b], self.wp[g][:, a:b], "wc_" + g, [], [("wsb", g, a)])
        self.wsb_keys = getattr(self, "wsb_keys", {})
        self.wsb_keys[g] = [("wsb", g, a) for a in range(0, n, step)]

    def w_gather(self, g):
        self.P.cc("AllGather", [self.wsb[g].ap().opt()], [self.wfull[g].ap().opt()], "wag",
                  self.wsb_keys[g], [("wfull", g)])

    def norm_fm(self, src, skeys, nk, ncol, dn, gain, dst, dkeys, psb=6):
        P = self.P
        psn = self.ps[psb]
        for k in range(nk):
            P.act(self.sq[:, k, 0:ncol], src(k), AF.Square, skeys(k), [("sq", k)])
        for k in range(nk):
            P.mm(psn[:, 0:ncol], self.ones_bf[:, :], self.sq[:, k, 0:ncol], k == 0, k == nk - 1,
                 [("sq", k), ("ones",)], [("ps", psb)])
        P.ts("dve", self.rstd[:, 0:ncol], psn[:, 0:ncol], 1.0 / dn, EPS, ALU.mult, ALU.add,
             [("ps", psb)], [("rstd",)])
        P.act(self.rstd[:, 0:ncol], self.rstd[:, 0:ncol], AF.Ln, [("rstd",)], [("rstd",)])
        P.act(self.rstd[:, 0:ncol], self.rstd[:, 0:ncol], AF.Exp, [("rstd",)], [("rstd",)], scale=-0.5)
        for k in range(nk):
            P.stt("dve", dst(k), src(k), gain[:, k:k + 1], self.rstd[:, 0:ncol],
                  ALU.mult, ALU.mult, skeys(k) + [("rstd",), ("small",)], dkeys(k))

    def norm_x(self, gname):
        for tb in range(NTB):
            c0 = tb * 512
            self.norm_fm(lambda k: self.xT[:, k, c0:c0 + 512], lambda k: [("x", k, tb)], KD, 512, D,
                         self.sm(gname), lambda k: self.xn[:, k, c0:c0 + 512],
                         lambda k: [("xn", k, tb)])

    def ffn_alloc(self):
        self.NSLOT = 2
        self.wg = [self.carve([128, KD, 512], BF16) for _ in range(self.NSLOT)]
        self.wu = [self.carve([128, KD, 512], BF16) for _ in range(self.NSLOT)]
        self.wd = [self.carve([128, 4, D], BF16) for _ in range(self.NSLOT)]
        self.aT = [self.carve([128, 4, T], BF16) for _ in range(2)]
        self.sil = [self.carve([128, 512], F32) for _ in range(2)]
        self.grp_ctr = 0
        self.sil_ctr = 0

    def ffn_load(self, g, gi):
        P = self.P
        f0, nch = self.ffn_groups[gi]
        n = nch * 128
        s = (self.grp_ctr + gi) % self.NSLOT
        v = self.wview(g)
        gu = v[:, 0:OFF_DN].rearrange("r (p f) -> p r f", p=128)
        P.dma("sp", self.wg[s][:, :, 0:n], gu[:, :, f0:f0 + n], f"wg{s}", [("wfull", g)], [("wg", s)])
        P.dma("sp", self.wu[s][:, :, 0:n], gu[:, :, DFF + f0:DFF + f0 + n], f"wu{s}",
              [("wfull", g)], [("wu", s)])
        for jj in range(nch):
            j = f0 // 128 + jj
            r, jr = j // 3, j % 3
            src = v[r, OFF_DN + jr * 128 * D:OFF_DN + (jr + 1) * 128 * D].rearrange("(p d) -> p d", p=128)
            P.dma("sp", self.wd[s][:, jj, :], src, f"wd{s}", [("wfull", g)], [("wd", s, jj)])

    def ffn_gu(self, gi):
        P = self.P
        f0, nch = self.ffn_groups[gi]
        s = (self.grp_ctr + gi) % self.NSLOT
        a = (self.grp_ctr + gi) % 2
        for tb in range(NTB):
            c0 = tb * 512
            for j in range(nch):
                pb = 2 * (self.sil_ctr % 2)
                psg, psu = self.ps[pb], self.ps[pb + 1]
                for k in range(KD):
                    P.mm(psg[:, :], self.wg[s][:, k, j * 128:(j + 1) * 128], self.xn[:, k, c0:c0 + 512],
                         k == 0, k == KD - 1, [("wg", s), ("xn", k, tb)], [("ps", pb)])
                for k in range(KD):
                    P.mm(psu[:, :], self.wu[s][:, k, j * 128:(j + 1) * 128], self.xn[:, k, c0:c0 + 512],
                         k == 0, k == KD - 1, [("wu", s), ("xn", k, tb)], [("ps", pb + 1)])
                st = self.sil[self.sil_ctr % 2]
                sk = ("sil", self.sil_ctr % 2)
                P.act(st[:, :], psg[:, :], AF.Silu, [("ps", pb)], [sk])
                P.tt("dve", self.aT[a][:, j, c0:c0 + 512], psu[:, :], st[:, :], ALU.mult,
                     [("ps", pb + 1), sk], [("aT", a, j, tb)])
                self.sil_ctr += 1

    def ffn_down(self, gi):
        P = self.P
        f0, nch = self.ffn_groups[gi]
        s = (self.grp_ctr + gi) % self.NSLOT
        a = (self.grp_ctr + gi) % 2
        for tb in range(NTB):
            c0 = tb * 512
            for i in range(KD):
                pb = 4 + (i % 2)
                psy = self.ps[pb]
                for j in range(nch):
                    P.mm(psy[:, :], self.wd[s][:, j, i * 128:(i + 1) * 128], self.aT[a][:, j, c0:c0 + 512],
                         j == 0, j == nch - 1, [("wd", s, jj) for jj in range(nch)] + [("aT", a, j, tb)],
                         [("ps", pb)])
                P.stt("dve", self.xT[:, i, c0:c0 + 512], psy[:, :], 0.5, self.xT[:, i, c0:c0 + 512],
                      ALU.mult, ALU.add, [("ps", pb), ("x", i, tb)], [("x", i, tb)])

    def ffn(self, l, which):
        self.arena_reset()
        self.ffn_alloc()
        g = f"f{which}_{l}"
        G = len(self.ffn_groups)
        self.ffn_load(g, 0)
        self.ffn_load(g, 1)
        self.norm_x(f"ffn{which}_norm{l}")
        self.ffn_gu(0)
        for gi in range(G):
            if gi + 1 < G:
                self.ffn_gu(gi + 1)
            self.ffn_down(gi)
            if gi + 2 < G:
                self.ffn_load(g, gi + 2)
        self.grp_ctr += G

    def outproj(self, l, mixfn, mkeys):
        P = self.P
        g = f"mx_{l}"
        wout = self.carve([128, KD, D], BF16)
        src = self.wview(g)[:, 0:128 * D].rearrange("r (p f) -> p r f", p=128)
        P.dma("sp", wout, src, "wout", [("wfull", g)], [("wout",)])
        for tb in range(NTB):
            c0 = tb * 512
            for i in range(KD):
                pb = 4 + (i % 2)
                psy = self.ps[pb]
                for c in range(KD):
                    P.mm(psy[:, :], wout[:, c, i * 128:(i + 1) * 128], mixfn(c, tb),
                         c == 0, c == KD - 1, [("wout",)] + mkeys(c, tb), [("ps", pb)])
                P.tt("dve", self.xT[:, i, c0:c0 + 512], psy[:, :], self.xT[:, i, c0:c0 + 512], ALU.add,
                     [("ps", pb), ("x", i, tb)], [("x", i, tb)])

    def mem_attn(self, l, qmT, qkeys, dstfn):
        P = self.P
        g = f"mx_{l}"
        wmkv = self.carve([128, KD, 512], BF16)
        src = self.wview(g)[:, 128 * D:128 * D + 128 * 512].rearrange("r (p f) -> p r f", p=128)
        P.dma("sp", wmkv, src, "wmkv", [("wfull", g)], [("wmkv",)])
        mkT = self.carve([128, 2, 256], BF16)
        vpad = self.carve([128, 4, 2, 128], BF16)
        opad = self.carve([128, 2, 128], BF16)
        P.memset("dve", vpad, 0.0, [("vpad",)])
        P.memset("dve", opad, 0.0, [("opad",)])
        for hp in range(2):
            P.memset("dve", opad[:, hp, hp * 64:(hp + 1) * 64], 1.0, [("opad",)])
        for m2 in range(2):
            ps = self.ps[0]
            for k in range(KD):
                P.mm(ps[:, 0:256], wmkv[:, k, m2 * 128:(m2 + 1) * 128], self.memn[:, k, :],
                     k == 0, k == KD - 1, [("wmkv",), ("memn",)], [("ps", 0)])
            P.ts("dve", mkT[:, m2, :], ps[:, 0:256], 0.125, None, ALU.mult, None, [("ps", 0)], [("mkT",)])
        for mb in range(2):
            ps = self.ps[1]
            for k in range(KD):
                P.mm(ps[:, 0:256], self.memn[:, k, mb * 128:(mb + 1) * 128], wmkv[:, k, 256:512],
                     k == 0, k == KD - 1, [("wmkv",), ("memn",)], [("ps", 1)])
            for h in range(4):
                P.copy("act", vpad[:, h, mb, (h % 2) * 64:(h % 2) * 64 + 64], ps[:, h * 64:(h + 1) * 64],
                       [("ps", 1)], [("vpad",)])
        ptR = Rot("mpt", [self.carve([128, 512], BF16) for _ in range(8)])
        rl = self.carve([128, 512], F32)
        sctr = 0
        for tb in range(NTB):
            c0 = tb * 512
            for m2 in range(2):
                pts = []
                for hp in range(2):
                    for mb in range(2):
                        pb = sctr % 2
                        sctr += 1
                        ps = self.ps[pb]
                        P.mm(ps[:, :], mkT[hp * 64:(hp + 1) * 64, m2, mb * 128:(mb + 1) * 128],
                             qmT[hp * 64:(hp + 1) * 64, m2, c0:c0 + 512], True, True,
                             [("mkT",)] + qkeys(m2, tb), [("ps", pb)])
                        pt, pk = ptR.next()
                        P.act(pt, ps[:, :], AF.Exp, [("ps", pb)], [pk])
                        pts.append((pt, pk, hp, mb))
                pso, psl = self.ps[2], self.ps[3]
                for n_, (pt, pk, hp, mb) in enumerate(pts):
                    P.mm(pso[:, :], vpad[:, 2 * m2 + hp, mb, :], pt, n_ == 0, n_ == 3,
                         [("vpad",), pk], [("ps", 2)])
                for n_, (pt, pk, hp, mb) in enumerate(pts):
                    P.mm(psl[:, :], opad[:, hp, :], pt, n_ == 0, n_ == 3, [("opad",), pk], [("ps", 3)])
                P.op("dve", lambda e, a=rl, b=psl: e.reciprocal(a, b[:, :]), [("ps", 3)], [("mrl",)])
                P.tt("dve", dstfn(m2, tb), pso[:, :], rl, ALU.mult,
                     [("ps", 2), ("mrl",)], [("mix", 6 + m2, tb)])

    def rope_tables(self):
        P = self.P
        self.rope_dram = self.nc.dram_tensor("rope_dram", [2, 64, T], F32)
        CCt = self.carve([128, T], F32)
        SSt = self.carve([128, T], F32)
        posi = self.carve([128, T], mybir.dt.int32)
        xf = self.carve([128, T], F32)
        ki = self.carve([128, T], mybir.dt.int32)
        kf = self.carve([128, T], F32)
        msk = self.carve([128, T], F32)
        P.dma("sp", posi, self.pos_in[:, :], "pos", [], [("posi",)])
        R = slice(0, 64)
        P.copy("dve", xf[R], posi[R], [("posi",)], [("rx",)])
        P.ts("dve", xf[R], xf[R], self.sm("invf")[R, 0:1], 1.0 / TWO_PI, ALU.mult, ALU.mult,
             [("rx",), ("small",)], [("rx",)])

        def wrap(r_key):
            P.copy("dve", ki[R], xf[R], [r_key], [("rk",)])
            P.copy("dve", kf[R], ki[R], [("rk",)], [("rkf",)])
            P.tt("dve", xf[R], xf[R], kf[R], ALU.subtract, [r_key, ("rkf",)], [r_key])
            P.ts("dve", msk[R], xf[R], 0.5, None, ALU.is_gt, None, [r_key], [("rm",)])
            P.tt("dve", xf[R], xf[R], msk[R], ALU.subtract, [r_key, ("rm",)], [r_key])
            P.ts("dve", msk[R], xf[R], -0.5, None, ALU.is_lt, None, [r_key], [("rm",)])
            P.tt("dve", xf[R], xf[R], msk[R], ALU.add, [r_key, ("rm",)], [r_key])

        wrap(("rx",))
        P.act(SSt[R], xf[R], AF.Sin, [("rx",)], [("SS",)], scale=TWO_PI * (1.0 - 1e-6))
        P.ts("dve", SSt[R], SSt[R], self.sm("sgn")[R, 0:1], None, ALU.mult, None,
             [("SS",), ("small",)], [("SS",)])
        P.ts("dve", xf[R], xf[R], 0.25, None, ALU.add, None, [("rx",)], [("rx",)])
        wrap(("rx",))
        P.act(CCt[R], xf[R], AF.Sin, [("rx",)], [("CC",)], scale=TWO_PI * (1.0 - 1e-6))
        P.dma("sp", self.rope_dram[0], CCt[R], "rope0", [("CC",)], [("rope_dram", 0)])
        P.dma("sp", self.rope_dram[1], SSt[R], "rope1", [("SS",)], [("rope_dram", 1)])

    def a_mixer(self, i):
        P = self.P
        l = i
        self.arena_reset()
        self.norm_x(f"mix_norm{l}")
        for k in range(KD):
            P.dma("sp", self.xn_loc[k * 128:(k + 1) * 128, :], self.xn[:, k, :], f"xo{k % 4}",
                  [("xn", k, tb) for tb in range(NTB)], [("xn_loc", k)])
        P.cc("AllGather", [self.xn_loc.ap().opt()], [self.xn_all.ap().opt()], "ag",
             [("xn_loc", k) for k in range(KD)], [("xn_all",)])
        self.w_gather(f"f2_{l}")
        if l + 1 < DEPTH:
            self.w_gather(f"f1_{l + 1}")
            self.w_gather(f"mx_{l + 1}")

        mixmem = self.carve([128, 2, T], BF16)
        wqkv = self.carve([128, KD, 384], BF16)
        wgba = self.carve([128, KD, 130], BF16)
        wqm = self.carve([128, KD, 256], BF16)
        P.dma("pool", wqkv, self.awin[i, :, 0:384].rearrange("(k p) c -> p k c", p=128), "aw_qkv", [], [("wqkv",)])
        P.dma("pool", wgba, self.awin[i, :, 384:514].rearrange("(k p) c -> p k c", p=128), "aw_gba", [], [("wgba",)])
        P.dma("pool", wqm, self.awin[i, :, 514:770].rearrange("(k p) c -> p k c", p=128), "aw_qm", [], [("wqm",)])

        mark = self.aoff
        qmT = self.carve([128, 2, T], BF16)
        for tb in range(NTB):
            c0 = tb * 512
            for m2 in range(2):
                ps = self.ps[4 + m2]
                for k in range(KD):
                    P.mm(ps[:, :], wqm[:, k, m2 * 128:(m2 + 1) * 128], self.xn[:, k, c0:c0 + 512],
                         k == 0, k == KD - 1, [("wqm",), ("xn", k, tb)], [("ps", 4 + m2)])
                P.copy("act", qmT[:, m2, c0:c0 + 512], ps[:, :], [("ps", 4 + m2)], [("qm", m2, tb)])
        self.mem_attn(l, qmT, lambda m2, tb: [("qm", m2, tb)],
                      lambda m2, tb: mixmem[:, m2, tb * 512:(tb + 1) * 512])
        P.barrier()
        self.aoff = mark

        conv = self.sm(f"conv{i}")
        ident_f = self.sm("ident")
        ltri, bones = self.sm("ltri"), self.sm("bones")
        mneg, mnegT, sneg = self.sm("mneg"), self.sm("mnegT"), self.sm("sneg")
        cind = self.sm("cind")
        ogn = self.sm(f"ognorm{i}")
        cv = lambda shape, d: self.carve(shape, d)
        negA = cv([128, 1], F32)
        P.act(negA, self.sm(f"alog{i}"), AF.Exp, [("small",)], [("negA",)])
        P.ts("dve", negA, negA, -1.0, None, ALU.mult, None, [("negA",)], [("negA",)])
        S_f = cv([128, 128], F32)
        S_b = cv([128, 128], BF16)
        P.memset("dve", S_f, 0.0, [("S_f",)])
        P.memset("dve", S_b, 0.0, [("S_b",)])
        pre = [cv([128, 3, 515], BF16) for _ in range(2)]
        P.memset("dve", pre[0][:, :, 0:3], 0.0, [("pre", 0, c) for c in range(3)])
        xbR = Rot("xb", [cv([128, KD, 512], BF16) for _ in range(2)])
        acc = cv([128, 512], F32)
        sgt = cv([128, 512], F32)
        qkvT = cv([128, 3, 512], BF16)
        stageR = Rot("stg", [cv([128, 512], BF16) for _ in range(2)])
        junk = cv([128, 128], BF16)
        scR = Rot("sc", [cv([128, 32], F32) for _ in range(2)])
        dSR = Rot("dS", [cv([128, 2], F32) for _ in range(2)])
        t128 = lambda n, d, nm: Rot(nm, [cv([128, 128], d) for _ in range(n)])
        KbR, VbR, QsTR, QdTR, KnTR = (t128(2, BF16, "Kb"), t128(2, BF16, "Vb"), t128(2, BF16, "QsT"),
                                      t128(2, BF16, "QdT"), t128(2, BF16, "KnT"))
        KnR, QsR, QdR = t128(2, BF16, "Kn"), t128(2, BF16, "Qs"), t128(2, BF16, "Qd")
        Kd0R, Kd1R = t128(2, BF16, "Kd0"), t128(2, BF16, "Kd1")
        for t_, _k in [Kd0R.next() for _ in range(2)]:
            P.memset("dve", t_, 0.0, [_k])
        for t_, _k in [Kd1R.next() for _ in range(2)]:
            P.memset("dve", t_, 0.0, [_k])
        diagR, tmpR, decR, decTR = (t128(2, F32, "diag"), t128(2, F32, "tmp"), t128(2, F32, "dec"),
                                    t128(2, F32, "decT"))
        XR, XTR, PTR = t128(3, F32, "X"), t128(3, F32, "XT"), t128(3, F32, "PT")
        qkTR, PTbR, wTR, uR, vnR = (t128(2, BF16, "qkT"), t128(2, BF16, "PTb"), t128(2, BF16, "wT"),
                                    t128(2, F32, "u"), t128(2, BF16, "vn"))
        oR, sgR, yR, mtR = t128(2, F32, "o"), t128(2, F32, "sg"), t128(2, F32, "y"), t128(2, BF16, "mt")
        gsbR = t128(2, F32, "gsb")
        prev_st2 = []

        xall = self.xn_all.ap().rearrange("(r k p) t -> r p k t", r=NCORES, k=KD)
        NB = S // 512
        for b in range(NB):
            r, t0 = b // 4, (b % 4) * 512
            xb, xbk = xbR.next()
            P.dma("sp", xb, xall[r, :, :, t0:t0 + 512], f"xb{(b % 2)}", [("xn_all",)], [xbk])
            pr, prn = pre[b % 2], pre[(b + 1) % 2]
            for c in range(3):
                ps = self.ps[0]
                for k in range(KD):
                    P.mm(ps[:, :], wqkv[:, k, c * 128:(c + 1) * 128], xb[:, k, :], k == 0, k == KD - 1,
                         [("wqkv",), xbk], [("ps", 0)])
                P.copy("act", pr[:, c, 3:515], ps[:, :], [("ps", 0)], [("pre", b % 2, c)])
                pk = [("pre", b % 2, c)]
                P.ts("dve", acc, pr[:, c, 0:512], conv[:, 4 * c:4 * c + 1], None, ALU.mult, None,
                     pk + [("small",)], [("acc",)])
                for j in range(1, 4):
                    P.stt("dve", acc, pr[:, c, j:j + 512], conv[:, 4 * c + j:4 * c + j + 1], acc,
                          ALU.mult, ALU.add, pk + [("acc",), ("small",)], [("acc",)])
                P.copy("dve", prn[:, c, 0:3], pr[:, c, 512:515], pk, [("pre", (b + 1) % 2, c)])
                P.act(sgt, acc, AF.Exp, [("acc",)], [("sgt",)], scale=-1.0)
                P.ts("dve", sgt, sgt, 1.0, None, ALU.add, None, [("sgt",)], [("sgt",)])
                P.op("dve", lambda e, a=sgt: e.reciprocal(a, a), [("sgt",)], [("sgt",)])
                P.tt("dve", qkvT[:, c, :], acc, sgt, ALU.mult, [("acc",), ("sgt",)], [("qkvT", c)])
            stg, stgk = stageR.next()
            for tl in range(4):
                c0 = tl * 128
                P.capture_begin()
                sc, sck = scR.next()
                col = lambda n: sc[:, n:n + 1]
                SK = [sck]
                pst = self.ps[2]
                for c in range(3):
                    P.mm(pst[:, c * 128:(c + 1) * 128], qkvT[:, c, c0:c0 + 128], self.ident_bf[:, :],
                         True, True, [("qkvT", c), ("identb",)], [("pst", c)])
                psg = self.ps[3]
                for k in range(KD):
                    P.mm(psg[:, 0:130], xb[:, k, c0:c0 + 128], wgba[:, k, :], k == 0, k == KD - 1,
                         [xbk, ("wgba",)], [("psg",)])
                P.act(junk, pst[:, 0:128], AF.Square, [("pst", 0)], [("junk",), sck], accum_out=col(0))
                P.act(junk, pst[:, 128:256], AF.Square, [("pst", 1)], [("junk",), sck], accum_out=col(1))
                P.ts("dve", sc[:, 2:4], sc[:, 0:2], EPS, None, ALU.add, None, SK, SK)
                P.act(sc[:, 2:4], sc[:, 2:4], AF.Ln, SK, SK)
                P.act(sc[:, 2:4], sc[:, 2:4], AF.Exp, SK, SK, scale=-0.5)
                P.ts("dve", col(15), col(2), 128.0 ** -0.5, None, ALU.mult, None, SK, SK)
                P.act(col(4), psg[:, 128:129], AF.Exp, [("psg",)], SK, scale=-1.0)
                P.ts("dve", col(4), col(4), 1.0, None, ALU.add, None, SK, SK)
                P.op("dve", lambda e, a=col(4): e.reciprocal(a, a), SK, SK)
                P.ts("dve", col(5), psg[:, 129:130], self.sm(f"dtb{i}")[:, 0:1], None, ALU.add, None,
                     [("psg",), ("small",)], SK)
                P.act(col(6), col(5), AF.Abs, SK, SK)
                P.act(col(6), col(6), AF.Exp, SK, SK, scale=-1.0)
                P.act(col(6), col(6), AF.Ln, SK, SK, bias=1.0)
                P.stt("dve", col(7), col(5), 0.0, col(6), ALU.max, ALU.add, SK, SK)
                P.tt("dve", col(8), col(7), negA, ALU.mult, SK + [("negA",)], SK)
                pss = self.ps[3]
                P.mm(pss[:, 256:257], ltri, col(8), True, True, SK + [("small",)], [("pss", 0)])
                P.mm(pss[:, 257:258], bones, col(8), True, True, SK + [("small",)], [("pss", 1)])
                P.copy("dve", sc[:, 9:11], pss[:, 256:258], [("pss", 0), ("pss", 1)], SK)
                P.ts("dve", col(11), col(9), -1.0, None, ALU.mult, None, SK, SK)
                P.act(col(12), col(9), AF.Exp, SK, SK)
                P.act(col(13), col(9), AF.Exp, SK, SK, scale=-1.0, bias=col(10))
                P.tt("dve", col(14), col(4), col(12), ALU.mult, SK, SK)
                gsel = sc[:, 16:18]
                P.ts("dve", gsel, cind, col(8), None, ALU.mult, None, SK + [("small",)], SK)
                P.mm(pss[:, 258:260], self.ones_f[:, :], gsel, True, True, SK + [("onesf",)], [("pss", 2)])
                dS, dSk = dSR.next()
                P.act(dS, pss[:, 258:260], AF.Exp, [("pss", 2)], [dSk])
                Kn, Knk = KnR.next()
                Kb, Kbk = KbR.next()
                Kd0, Kd0k = Kd0R.next()
                Kd1, Kd1k = Kd1R.next()
                Qs, Qsk = QsR.next()
                Qd, Qdk = QdR.next()
                Vb, Vbk = VbR.next()
                kps, qps, vps = pst[:, 128:256], pst[:, 0:128], pst[:, 256:384]
                P.ts("dve", Kn, kps, col(3), None, ALU.mult, None, [("pst", 1)] + SK, [Knk])
                P.ts("dve", Kb, kps, col(3), col(14), ALU.mult, ALU.mult, [("pst", 1)] + SK, [Kbk])
                P.ts("dve", Kd0[0:64, :], kps[0:64, :], sc[0:64, 3:4], sc[0:64, 13:14], ALU.mult, ALU.mult,
                     [("pst", 1)] + SK, [Kd0k])
                P.ts("dve", Kd1[64:128, :], kps[64:128, :], sc[64:128, 3:4], sc[64:128, 13:14], ALU.mult,
                     ALU.mult, [("pst", 1)] + SK, [Kd1k])
                P.ts("dve", Qs, qps, col(15), None, ALU.mult, None, [("pst", 0)] + SK, [Qsk])
                P.ts("dve", Qd, qps, col(15), col(12), ALU.mult, ALU.mult, [("pst", 0)] + SK, [Qdk])
                P.ts("dve", Vb, vps, col(4), None, ALU.mult, None, [("pst", 2)] + SK, [Vbk])
                ps4 = self.ps[4]
                KnT, KnTk = KnTR.next()
                QsT, QsTk = QsTR.next()
                QdT, QdTk = QdTR.next()
                for n_, (src_, sk_, dst_, dk_) in enumerate(((Kn, Knk, KnT, KnTk), (Qs, Qsk, QsT, QsTk),
                                                              (Qd, Qdk, QdT, QdTk))):
                    P.mm(ps4[:, n_ * 128:(n_ + 1) * 128], src_, self.ident_bf[:, :], True, True,
                         [sk_, ("identb",)], [("ps4", n_)])
                    P.copy("act", dst_, ps4[:, n_ * 128:(n_ + 1) * 128], [("ps4", n_)], [dk_])
                ps5 = self.ps[5]
                P.mm(ps5[:, 0:128], KnT, KnT, True, True, [KnTk], [("ps5", 0)])
                P.mm(ps5[:, 128:256], KnT, QsT, True, True, [KnTk, QsTk], [("ps5", 1)])
                dg, dgk = diagR.next()
                P.ts("dve", dg, ident_f, col(9), None, ALU.mult, None, SK + [("small",)], [dgk])
                P.mm(ps5[:, 256:384], self.ones_f[:, :], dg, True, True, [dgk, ("onesf",)], [("ps5", 2)])
                tmp, tmpk = tmpR.next()
                dec, deck = decR.next()
                P.stt("dve", tmp, ps5[:, 256:384], -1.0, mneg, ALU.mult, ALU.add, [("ps5", 2), ("small",)], [tmpk])
                P.act(dec, tmp, AF.Exp, [tmpk] + SK, [deck], bias=col(9))
                tmp2, tmp2k = tmpR.next()
                decT, decTk = decTR.next()
                P.tt("dve", tmp2, ps5[:, 256:384], mnegT, ALU.add, [("ps5", 2), ("small",)], [tmp2k])
                P.act(decT, tmp2, AF.Exp, [tmp2k] + SK, [decTk], bias=col(11))
                X0, X0k = XR.next()
                P.stt("dve", X0, ps5[:, 0:128], col(4), dec, ALU.mult, ALU.mult, [("ps5", 0), deck] + SK, [X0k])
                P.tt("dve", X0, X0, sneg, ALU.mult, [X0k, ("small",)], [X0k])
                qkT, qkTk = qkTR.next()
                P.tt("dve", qkT, ps5[:, 128:256], decT, ALU.mult, [("ps5", 1), decTk], [qkTk])
                ps6 = self.ps[6]
                ps7 = self.ps[7]
                P.mm(ps6[:, 0:128], X0, ident_f, True, True, [X0k, ("small",)], [("ps6", 0)])
                X0T, X0Tk = XTR.next()
                P.copy("act", X0T, ps6[:, 0:128], [("ps6", 0)], [X0Tk])
                PT, PTk = PTR.next()
                P.tt("dve", PT, ps6[:, 0:128], ident_f, ALU.add, [("ps6", 0), ("small",)], [PTk])
                Xp, Xpk, XpT, XpTk = X0, X0k, X0T, X0Tk
                for kk in range(1, 6):
                    Xn, Xnk = XR.next()
                    P.mm(ps6[:, 128:256], XpT, Xp, True, True, [XpTk, Xpk], [("ps6", 1)])
                    P.copy("act", Xn, ps6[:, 128:256], [("ps6", 1)], [Xnk])
                    if kk < 5:
                        XnT, XnTk = XTR.next()
                        P.mm(ps6[:, 256:384], Xp, XpT, True, True, [XpTk, Xpk], [("ps6", 2)])
                        P.copy("act", XnT, ps6[:, 256:384], [("ps6", 2)], [XnTk])
                    P.mm(ps7[:, 0:128], Xn, PT, True, True, [Xnk, PTk], [("ps7", 0)])
                    PTn, PTnk = PTR.next()
                    P.tt("dve", PTn, ps7[:, 0:128], PT, ALU.add, [("ps7", 0), PTk], [PTnk])
                    PT, PTk = PTn, PTnk
                    Xp, Xpk = Xn, Xnk
                    if kk < 5:
                        XpT, XpTk = XnT, XnTk
                PTb, PTbk = PTbR.next()
                P.copy("act", PTb, PT, [PTk], [PTbk])
                wT, wTk = wTR.next()
                u, uk = uR.next()
                P.mm(ps7[:, 128:256], Kb, PTb, True, True, [Kbk, PTbk], [("ps7", 1)])
                P.copy("act", wT, ps7[:, 128:256], [("ps7", 1)], [wTk])
                P.mm(ps7[:, 256:384], PTb, Vb, True, True, [Vbk, PTbk], [("ps7", 2)])
                P.copy("act", u, ps7[:, 256:384], [("ps7", 2)], [uk])
                gsb, gsbk = gsbR.next()
                P.copy("act", gsb, psg[:, 0:128], [("psg",)], [gsbk])
                st1 = P.capture_end()
                P.capture_begin()
                o, ok_ = oR.next()
                ps1 = self.ps[1]
                for cch in range(2):
                    rows = slice(cch * 64, cch * 64 + 64)
                    Kd, Kdk = (Kd0, Kd0k) if cch == 0 else (Kd1, Kd1k)
                    vn, vnk = vnR.next()
                    P.mm(ps1[:, 0:128], wT, S_b, True, True, [wTk, ("S_b",)], [("ps1", 0)])
                    P.tt("dve", vn, u, ps1[:, 0:128], ALU.subtract, [uk, ("ps1", 0)], [vnk])
                    P.mm(ps1[:, 128:256], QdT, S_b, True, False, [QdTk, ("S_b",)], [("ps1", 1)])
                    P.mm(ps1[:, 128:256], qkT, vn, False, True, [qkTk, vnk], [("ps1", 1)])
                    P.copy("act", o[rows, :], ps1[rows, 128:256], [("ps1", 1)], [ok_])
                    P.mm(ps1[:, 256:384], Kd, vn, True, True, [Kdk, vnk], [("ps1", 2)])
                    P.stt("dve", S_f, S_f, dS[:, cch:cch + 1], ps1[:, 256:384], ALU.mult, ALU.add,
                          [("S_f",), dSk, ("ps1", 2)], [("S_f",)])
                    P.copy("act", S_b, S_f, [("S_f",)], [("S_b",)])
                P.act(junk, o, AF.Square, [ok_], [("junk",), sck], accum_out=col(20))
                P.ts("dve", col(21), col(20), 1.0 / 128.0, EPS, ALU.mult, ALU.add, SK, SK)
                P.act(col(21), col(21), AF.Ln, SK, SK)
                P.act(col(21), col(21), AF.Exp, SK, SK, scale=-0.5)
                sg, sgk = sgR.next()
                P.act(sg, gsb, AF.Exp, [gsbk], [sgk], scale=-1.0)
                P.ts("dve", sg, sg, 1.0, None, ALU.add, None, [sgk], [sgk])
                P.op("dve", lambda e, a=sg: e.reciprocal(a, a), [sgk], [sgk])
                P.tt("dve", sg, gsb, sg, ALU.mult, [gsbk, sgk], [sgk])
                y, yk = yR.next()
                P.stt("dve", y, o, col(21), ogn, ALU.mult, ALU.mult, [ok_, ("small",)] + SK, [yk])
                mt, mtk = mtR.next()
                P.tt("dve", mt, y, sg, ALU.mult, [yk, sgk], [mtk])
                P.mm(ps1[:, 384:512], mt, self.ident_bf[:, :], True, True, [mtk, ("identb",)], [("ps1", 3)])
                P.copy("act", stg[:, c0:c0 + 128], ps1[:, 384:512], [("ps1", 3)], [stgk])
                if tl == 3:
                    P.dma("sp", self.mix_loc[:, b * 512:(b + 1) * 512], stg, f"mo{b % 2}", [stgk],
                          [("mix_loc", b)])
                st2 = P.capture_end()
                P.replay(prev_st2, st1)
                prev_st2 = st2
        P.replay(prev_st2, [])
        P.cc("AllGather", [self.mix_loc.ap().opt()], [self.mix_all.ap().opt()], "ag",
             [("mix_loc", b) for b in range(NB)], [("mix_all",)])
        oh = self.sm("onehot")
        mall = self.mix_all.ap().rearrange("(h p) (r t) -> h p r t", p=128, r=NCORES)
        P.barrier()
        self.aoff = mark
        mixsel = self.carve([128, H, T], BF16)
        candR = Rot("cand", [self.carve([128, NCORES, 512], BF16) for _ in range(2)])
        sacc = self.carve([128, 512], F32)
        n_ = 0
        for tb in range(NTB):
            for ch in range(H):
                cand, ck = candR.next()
                P.dma("sp", cand, mall[ch, :, :, tb * 512:(tb + 1) * 512], f"cand{n_ % 2}", [("mix_all",)], [ck])
                n_ += 1
                P.ts("dve", sacc, cand[:, 0, :], oh[:, 0:1], None, ALU.mult, None, [ck, ("small",)], [("sacc",)])
                for r in range(1, NCORES):
                    last = r == NCORES - 1
                    dst = mixsel[:, ch, tb * 512:(tb + 1) * 512] if last else sacc
                    P.stt("dve", dst, cand[:, r, :], oh[:, r:r + 1], sacc, ALU.mult, ALU.add,
                          [ck, ("sacc",), ("small",)], [("mix", ch, tb)] if last else [("sacc",)])
        self.outproj(l, lambda c, tb: (mixsel[:, c, tb * 512:(tb + 1) * 512] if c < H
                                       else mixmem[:, c - H, tb * 512:(tb + 1) * 512]),
                     lambda c, tb: [("mix", c, tb)])

    def rope_load(self, dstC, dstS, c0, n, key):
        P = self.P
        P.dma("sp", dstC[0:64, 0:n], self.rope_dram[0][:, c0:c0 + n], "ropeld", [("rope_dram", 0)], [key])
        P.dma("sp", dstS[0:64, 0:n], self.rope_dram[1][:, c0:c0 + n], "ropeld", [("rope_dram", 1)], [key])

    def kv_build(self):
        P = self.P
        self.arena_reset()
        self.rope_tables()
        self.arena_reset()
        self.norm_x("kv_in_norm")
        wdkv = self.carve([128, KD, 384], BF16)
        P.dma("pool", wdkv, self.wdkv_in.rearrange("(k p) c -> p k c", p=128), "aw_dkv", [], [("wdkv",)])
        kst = self.carve([128, 3, T], BF16)
        P.memset("dve", kst[64:128, 2, :], 0.0, [("kst2",)])
        P.memset("dve", kst[64:65, 2, :], 1.0, [("kst2",)])
        cst = self.carve([128, 16, 256], BF16)
        ckf = self.carve([128, 2, 512], F32)
        CCb = self.carve([128, 512], F32)
        SSb = self.carve([128, 512], F32)
        t1 = self.carve([128, 512], F32)
        t2 = self.carve([128, 512], F32)
        kvl = self.sm("kvlat")
        for tb in range(NTB):
            c0 = tb * 512
            self.rope_load(CCb, SSb, c0, 512, ("ropeb",))
            for c in range(2):
                ps = self.ps[c]
                for k in range(KD):
                    P.mm(ps[:, :], wdkv[:, k, c * 128:(c + 1) * 128], self.xn[:, k, c0:c0 + 512],
                         k == 0, k == KD - 1, [("wdkv",), ("xn", k, tb)], [("ps", c)])
                P.copy("act", ckf[:, c, :], ps[:, :], [("ps", c)], [("ckf", c)])
            for c in range(2):
                ps = self.ps[2 + c]
                for k in range(KD):
                    P.mm(ps[0:64, :], wdkv[:, k, 256 + c * 64:320 + c * 64], self.xn[:, k, c0:c0 + 512],
                         k == 0, k == KD - 1, [("wdkv",), ("xn", k, tb)], [("ps", 2 + c)])
            self.norm_fm(lambda k: ckf[:, k, :], lambda k: [("ckf", k)], 2, 512, 256.0, kvl,
                         lambda k: kst[:, k, c0:c0 + 512], lambda k: [("kst", k, tb)])
            P.tt("dve", t1[0:64, :], self.ps[2][0:64, :], CCb[0:64, :], ALU.mult, [("ps", 2), ("ropeb",)], [("t1",)])
            P.tt("dve", t2[0:64, :], self.ps[3][0:64, :], SSb[0:64, :], ALU.mult, [("ps", 3), ("ropeb",)], [("t2",)])
            P.tt("dve", kst[0:64, 2, c0:c0 + 512], t1[0:64, :], t2[0:64, :], ALU.add, [("t1",), ("t2",)],
                 [("kst", 2, tb)])
            for tl in range(4):
                ti = tb * 4 + tl
                ps = self.ps[4 + (tl % 2)]
                for c in range(2):
                    P.mm(ps[:, c * 128:(c + 1) * 128], kst[:, c, ti * 128:(ti + 1) * 128], self.ident_bf[:, :],
                         True, True, [("kst", c, tb), ("identb",)], [("ps", 4 + (tl % 2))])
                P.copy("act", cst[:, ti, :], ps[:, 0:256], [("ps", 4 + (tl % 2))], [("cst", ti)])
        for c in range(3):
            P.dma("sp", self.kT_loc[c * 128:(c + 1) * 128, :], kst[:, c, :], "kvo",
                  [("kst", c, tb) for tb in range(NTB)] + [("kst2",)], [("kT_loc", c)])
        P.dma("sp", self.c_loc.ap().rearrange("(n p) c -> p n c", p=128), cst, "kvo_c",
              [("cst", ti) for ti in range(16)], [("c_loc",)])
        P.cc("AllGather", [self.kT_loc.ap().opt()], [self.kT_all.ap().opt()], "ag",
             [("kT_loc", c) for c in range(3)], [("kT_all",)])
        P.cc("AllGather", [self.c_loc.ap().opt()], [self.c_all.ap().opt()], "ag", [("c_loc",)], [("c_all",)])

    def b_mixer(self, j):
        P = self.P
        l = 2 + j
        g = f"mx_{l}"
        self.arena_reset()
        self.norm_x(f"mix_norm{l}")
        self.w_gather(f"f2_{l}")
        if l + 1 < DEPTH:
            self.w_gather(f"f1_{l + 1}")
            self.w_gather(f"mx_{l + 1}")
        mixT = self.carve([128, KD, T], BF16)
        cqn = self.carve([128, 2, T], BF16)
        mark = self.aoff
        wbin = self.carve([128, KD, 512], BF16)
        src = self.wview(g)[:, 128 * D + 128 * 512:128 * D + 2 * 128 * 512].rearrange("r (p f) -> p r f", p=128)
        P.dma("sp", wbin, src, "wbin", [("wfull", g)], [("wbin",)])
        qmT = self.carve([128, 2, T], BF16)
        cqf = self.carve([128, 2, 512], F32)
        for tb in range(NTB):
            c0 = tb * 512
            for c in range(4):
                ps = self.ps[c % 2]
                for k in range(KD):
                    P.mm(ps[:, :], wbin[:, k, c * 128:(c + 1) * 128], self.xn[:, k, c0:c0 + 512],
                         k == 0, k == KD - 1, [("wbin",), ("xn", k, tb)], [("ps", c % 2)])
                if c < 2:
                    P.copy("act", cqf[:, c, :], ps[:, :], [("ps", c % 2)], [("cqf", c)])
                else:
                    P.copy("act", qmT[:, c - 2, c0:c0 + 512], ps[:, :], [("ps", c % 2)], [("qm", c - 2, tb)])
            self.norm_fm(lambda k: cqf[:, k, :], lambda k: [("cqf", k)], 2, 512, 256.0, self.sm(f"bqnorm{j}"),
                         lambda k: cqn[:, k, c0:c0 + 512], lambda k: [("cqn", k, tb)])
        self.mem_attn(l, qmT, lambda m2, tb: [("qm", m2, tb)],
                      lambda m2, tb: mixT[:, 6 + m2, tb * 512:(tb + 1) * 512])
        P.barrier()
        self.aoff = mark
        wuq = self.carve([128, 2, 1536], BF16)
        wukT = self.carve([128, H, 256], BF16)
        wuv = self.carve([128, 2, 768], BF16)
        visR = Rot("vis", [self.carve([128, 256], F32) for _ in range(2)])
        P.dma("pool", wuq, self.wuq_in[j].rearrange("(k p) c -> p k c", p=128), "aw_uq", [], [("wuq",)])
        P.dma("pool", wukT, self.wukT_in.rearrange("h p c -> p h c"), "aw_ukT", [], [("wukT",)])
        P.dma("pool", wuv, self.wuv_in.rearrange("(k p) c -> p k c", p=128), "aw_uv", [], [("wuv",)])
        QaR = Rot("Qa", [self.carve([128, 3, 768], BF16) for _ in range(2)])
        for qa, qk_ in [QaR.next() for _ in range(2)]:
            P.memset("dve", qa[64:128, 2, :], 0.0, [qk_])
        qn = Rot("qn", [self.carve([128, 128], BF16) for _ in range(2)])
        CCq = self.carve([128, 128], F32)
        SSq = self.carve([128, 128], F32)
        r1 = self.carve([128, 128], F32)
        r2 = self.carve([128, 128], F32)
        kTR = Rot("kT4", [self.carve([128, 3, 512], BF16) for _ in range(2)])
        c4R = Rot("c4", [self.carve([128, 4, 256], BF16) for _ in range(2)])
        ptR = Rot("pt", [self.carve([128, 768], BF16) for _ in range(3)])
        rl = self.carve([128, H], F32)
        olnR = Rot("oln", [self.carve([128, 256], BF16) for _ in range(2)])
        oltR = Rot("olt", [self.carve([128, 2, 128], BF16) for _ in range(2)])
        sT = [self.ps2[0], self.ps2[1]]
        lsum = self.ps[7][:, 0:H]
        kall = self.kT_all.ap().rearrange("(r c p) t -> r p c t", r=NCORES, c=3)
        call = self.c_all.ap().rearrange("(g n p) c -> g p n c", n=4, p=128)
        bctr = 0
        for i in range(16):
            c0 = i * 128
            qa, qak = QaR.next()
            visb, visk = visR.next()
            P.dma("sp", visb, self.visb_in[:, i * 256:(i + 1) * 256], f"vis{visk[1]}", [], [visk])
            self.rope_load(CCq, SSq, c0, 128, ("ropeq",))
            ps7 = self.ps[7]
            for h in range(H):
                qnt, qnk = qn.next()
                for k2 in range(2):
                    P.mm(ps7[:, 0:128], wuq[:, k2, h * 256:h * 256 + 128], cqn[:, k2, c0:c0 + 128],
                         k2 == 0, k2 == 1, [("wuq",)] + [("cqn", k2, i // 4)], [("ps7", 0)])
                P.copy("act", qnt, ps7[:, 0:128], [("ps7", 0)], [qnk])
                for rr in range(2):
                    for k2 in range(2):
                        P.mm(ps7[0:64, 128 + rr * 128:256 + rr * 128],
                             wuq[:, k2, h * 256 + 128 + rr * 64:h * 256 + 192 + rr * 64],
                             cqn[:, k2, c0:c0 + 128], k2 == 0, k2 == 1,
                             [("wuq",)] + [("cqn", k2, i // 4)], [("ps7", 1 + rr)])
                P.tt("dve", r1[0:64, :], ps7[0:64, 128:256], CCq[0:64, :], ALU.mult, [("ps7", 1), ("ropeq",)], [("r1",)])
                P.stt("dve", r2[0:64, :], ps7[0:64, 256:384], SCALE_B, SSq[0:64, :], ALU.mult, ALU.mult,
                      [("ps7", 2), ("ropeq",)], [("r2",)])
                P.stt("dve", qa[0:64, 2, h * 128:(h + 1) * 128], r1[0:64, :], SCALE_B, r2[0:64, :],
                      ALU.mult, ALU.add, [("r1",), ("r2",)], [qak])
                for m in range(2):
                    P.mm(ps7[:, 384:512], wukT[:, h, m * 128:(m + 1) * 128], qnt, True, True,
                         [("wukT",), qnk], [("ps7", 3)])
                    P.ts("dve", qa[:, m, h * 128:(h + 1) * 128], ps7[:, 384:512], SCALE_B, None, ALU.mult, None,
                         [("ps7", 3)], [qak])
            nkb = min(128, 113 + i)
            ngr = (nkb + 3) // 4
            for gq in range(ngr):
                kT4, kTk = kTR.next()
                c4, c4k = c4R.next()
                r = gq // 4
                P.dma("sp", kT4, kall[r, :, :, (gq % 4) * 512:(gq % 4 + 1) * 512], f"kT{kTk[1]}", [("kT_all",)], [kTk])
                P.dma("sp", c4, call[gq], f"c4{c4k[1]}", [("c_all",)], [c4k])
                for kk in range(4):
                    kb = gq * 4 + kk
                    if kb >= nkb:
                        break
                    sb = bctr % 2
                    bctr += 1
                    st = sT[sb]
                    for (a0, a1) in ((0, 512), (512, 768)):
                        for c in range(3):
                            rows = slice(0, 128) if c < 2 else slice(0, 64)
                            P.mm(st[:, a0:a1], kT4[rows, c, kk * 128:(kk + 1) * 128], qa[rows, c, a0:a1],
                                 c == 0, c == 2, [kTk, qak], [("sT", sb, a0)])
                    pt, ptk = ptR.next()
                    for qc in range(2):
                        vcol = kb * 2 + qc
                        P.act(pt.rearrange("p (h q) -> p h q", h=H)[:, :, qc * 64:(qc + 1) * 64],
                              st[:, 0:768].rearrange("p (h q) -> p h q", h=H)[:, :, qc * 64:(qc + 1) * 64],
                              AF.Exp, [("sT", sb, 0), ("sT", sb, 512), visk], [ptk],
                              bias=visb[:, vcol:vcol + 1])
                    first, last = kb == 0, kb == nkb - 1
                    for h in range(H):
                        acc = self.ps[4 + h // 2][:, (h % 2) * 256:(h % 2) * 256 + 256]
                        P.mm(acc, pt[:, h * 128:(h + 1) * 128], c4[:, kk, :], first and h % 2 == 0, last,
                             [ptk, c4k], [("olat", h)], skip=True)
                        P.mm(lsum[:, h:h + 1], pt[:, h * 128:(h + 1) * 128], self.ones_bf[:, 0:1],
                             first and h == 0, last, [ptk, ("ones",)], [("lsum",)], skip=True)
            P.op("dve", lambda e, a=rl, b=lsum: e.reciprocal(a, b[:, 0:H]), [("lsum",)], [("rl",)])
            for h in range(H):
                acc = self.ps[4 + h // 2][:, (h % 2) * 256:(h % 2) * 256 + 256]
                oln, olnk = olnR.next()
                P.ts("dve", oln, acc, rl[:, h:h + 1], None, ALU.mult, None, [("olat", h), ("rl",)], [olnk])
                olt, oltk = oltR.next()
                for m in range(2):
                    P.mm(ps7[:, 128 + m * 128:256 + m * 128], oln[:, m * 128:(m + 1) * 128], self.ident_bf[:, :],
                         True, True, [olnk, ("identb",)], [("ps7", 1 + m)])
                    P.copy("act", olt[:, m, :], ps7[:, 128 + m * 128:256 + m * 128], [("ps7", 1 + m)], [oltk])
                for m in range(2):
                    P.mm(ps7[:, 384:512], wuv[:, m, h * 128:(h + 1) * 128], olt[:, m, :], m == 0, m == 1,
                         [("wuv",), oltk], [("ps7", 3)])
                P.copy("act", mixT[:, h, c0:c0 + 128], ps7[:, 384:512], [("ps7", 3)], [("mixq", h, i)])
        P.barrier()
        self.aoff = mark
        self.outproj(l, lambda c, tb: mixT[:, c, tb * 512:(tb + 1) * 512],
                     lambda c, tb: ([("mix", c, tb)] if c >= H else [("mixq", c, 4 * tb + q_) for q_ in range(4)]))

    def final_out(self):
        P = self.P
        self.arena_reset()
        xo = Rot("xo", [self.carve([128, 512], F32) for _ in range(8)])
        for tb in range(NTB):
            c0 = tb * 512
            outs = {}

            def dstf(k):
                t, kk = xo.next()
                outs[k] = (t, kk)
                return t
            self.norm_fm(lambda k: self.xT[:, k, c0:c0 + 512], lambda k: [("x", k, tb)], KD, 512, D,
                         self.sm("final_norm"), dstf, lambda k: [outs[k][1]])
            for k in range(KD):
                P.dma("sp", self.outT[k * 128:(k + 1) * 128, c0:c0 + 512], outs[k][0], f"out{k % 4}",
                      [outs[k][1]], [("out", k, tb)])
        P.wait_all("sp", [("out", k, tb) for k in range(KD) for tb in range(NTB)] + getattr(self, "dbg_keys", []))

    def dump_stage(self, name):
        if not self.dbg:
            return
        t = self.nc.dram_tensor("dbg_" + name, [D, T], F32, kind="ExternalOutput").ap()
        for k in range(KD):
            self.P.dma("sp", t[k * 128:(k + 1) * 128, :], self.xT[:, k, :], f"dbg{k % 4}",
                       [("x", k, tb) for tb in range(NTB)], [("dbgout", name, k)])
        self.dbg_keys = getattr(self, "dbg_keys", []) + [("dbgout", name, k) for k in range(KD)]

    def dump_x(self):
        P = self.P
        for k in range(KD):
            P.dma("sp", self.outT[k * 128:(k + 1) * 128, :], self.xT[:, k, :], f"out{k % 4}",
                  [("x", k, tb) for tb in range(NTB)], [("out", k)])
        P.wait_all("sp", [("out", k) for k in range(KD)])

    def build(self):
        P = self.P
        sa = self.stop_after
        P.dma("sp", self.small[:, :], self.small_in[:, :], "small", [], [("small",)])
        for k in range(KD):
            P.dma("sp", self.xT[:, k, :], self.xT_in[k * 128:(k + 1) * 128, :], f"xin{k}", [],
                  [("x", k, tb) for tb in range(NTB)])
        P.memset("dve", self.ones_bf[:, :], 1.0, [("ones",)])
        P.memset("dve", self.ones_f[:, :], 1.0, [("onesf",)])
        P.copy("dve", self.ident_bf[:, :], self.sm("ident"), [("small",)], [("identb",)])
        for l in range(DEPTH):
            for part in ("f1", "mx", "f2"):
                self.w_cast(f"{part}_{l}")
        self.w_gather("f1_0")
        self.w_gather("mx_0")
        self.aoff = 0
        memf = self.carve([128, KD, 256], F32)
        P.dma("sp", memf, self.memT_in.rearrange("(k p) m -> p k m", p=128), "memin", [], [("memf",)])
        self.norm_fm(lambda k: memf[:, k, :], lambda k: [("memf",)], KD, 256, D, self.sm("mem_norm"),
                     lambda k: self.memn[:, k, :], lambda k: [("memn",)])
        done = False
        for l in range(DEPTH):
            self.ffn(l, 1)
            self.dump_stage(f"x_ffn1_{l}")
            if sa == ("ffn1", l):
                done = True
                break
            if l < 2:
                self.a_mixer(l)
            else:
                if l == 2:
                    pass
                self.b_mixer(l - 2)
            self.dump_stage(f"x_mix_{l}")
            if sa == ("mix", l):
                done = True
                break
            self.ffn(l, 2)
            self.dump_stage(f"x_l_{l}")
            if sa == ("ffn2", l):
                done = True
                break
            if l == 1:
                self.kv_build()
        if done:
            self.arena_reset()
            self.dump_x()
        else:
            self.final_out()
        P.emit()
        return self.nc


def _build(small_off, n_small, stop_after=None, dbg=False):
    b = Builder(small_off, n_small, stop_after, dbg)
    nc = b.build()
    return nc, b


def kernel(_stop_after=None, _dbg=False, **inp):
    sp0, _ = pack_small(inp, 0)
    nc, b = _build(sp0.off, sp0.n, _stop_after, _dbg)
    in_maps = []
    for c in range(NCORES):
        m = host_inputs(inp, c)
        in_maps.append(m)
    res = run_bass_kernel_spmd(nc, in_maps, core_ids=list(range(NCORES)))
    out = np.concatenate([np.asarray(r["outT"]).T for r in res.results], axis=0)
    out = np.ascontiguousarray(out.reshape(1, S, D).astype(np.float32))
    if _dbg:
        return out, res.results
    return out
```

```python
import numpy as np
import concourse.bass as bass
import concourse.mybir as mybir
from concourse.bass_utils import run_bass_kernel_spmd

F32 = mybir.dt.float32
BF16 = mybir.dt.bfloat16
AF = mybir.ActivationFunctionType
ALU = mybir.AluOpType

NCORES = 8
S = 16384
D = 1024
T = S // NCORES
NTB = T // 512
KD = D // 128
DEPTH = 4
DFF = 2816
EPS = 1e-6
ENGS = ("pe", "act", "dve", "pool", "sp")


def _bank_of(key):
    n = key[0]
    if n == "ps":
        return key[1]
    if n == "ps1":
        return 1
    if n == "pst":
        return 2
    if n in ("psg", "pss"):
        return 3
    if n == "ps4":
        return 4
    if n == "ps5":
        return 5
    if n == "ps6":
        return 6
    if n in ("ps7", "lsum"):
        return 7
    if n == "olat":
        return 4 + key[1] // 2
    if n == "sT":
        return 2 * key[1] + (0 if key[2] == 0 else 1)
    return None


class Op:
    __slots__ = ("eng", "fn", "idx", "signal", "sigval", "dma_sem", "dma_val", "waits")


class Prog:
    def __init__(self, nc):
        self.nc = nc
        self.ops = {e: [] for e in ENGS}
        self.last_w = {}
        self.readers = {}
        self.waited = {e: {x: -1 for x in ENGS} for e in ENGS}
        self.waited_dma = {e: {} for e in ENGS}
        self.last_touch = {}
        self.dma_cnt = {}
        self.dma_inc = {}
        self.dma_last = {}

    def capture_begin(self):
        self._cap = []

    def capture_end(self):
        c, self._cap = self._cap, None
        return c

    def replay(self, a, b):
        i = j = 0
        while i < len(a) or j < len(b):
            if j >= len(b) or (i < len(a) and i * len(b) <= j * len(a)):
                self.op(*a[i])
                i += 1
            else:
                self.op(*b[j])
                j += 1

    def op(self, eng, fn, reads=(), writes=(), dma=None, sync_same=True):
        if getattr(self, "_cap", None) is not None:
            self._cap.append((eng, fn, list(reads), list(writes), dma, sync_same))
            return None
        o = Op()
        o.eng, o.fn, o.signal, o.sigval = eng, fn, False, 0
        o.idx = len(self.ops[eng])
        o.dma_sem, o.dma_val = None, 0
        if dma is not None:
            inc = 16
            if isinstance(dma, tuple):
                dma, inc = dma
            self.dma_inc[dma] = inc
            self.dma_cnt[dma] = self.dma_cnt.get(dma, 0) + 1
            o.dma_sem, o.dma_val = dma, inc * self.dma_cnt[dma]
            self.dma_last[dma] = o
        deps = []
        for k in reads:
            w = self.last_w.get(k)
            if w is not None:
                deps.append(w)
        for k in writes:
            w = self.last_w.get(k)
            if w is not None:
                deps.append(w)
            deps.extend(self.readers.get(k, ()))
        banks = set()
        for k in list(reads) + list(writes):
            b_ = _bank_of(k)
            if b_ is not None:
                banks.add(b_)
        for b_ in banks:
            t = self.last_touch.get(b_)
            if t is not None and t.eng != eng:
                deps.append(t)
            self.last_touch[b_] = o
        best = {}
        bestd = {}
        for d in deps:
            if d is o:
                continue
            if d.dma_sem is not None:
                if d.dma_val > self.waited_dma[eng].get(d.dma_sem, 0):
                    if d.dma_sem not in bestd or bestd[d.dma_sem].dma_val < d.dma_val:
                        bestd[d.dma_sem] = d
            else:
                if d.eng == eng and not sync_same:
                    continue
                if d.idx > self.waited[eng][d.eng]:
                    if d.eng not in best or best[d.eng].idx < d.idx:
                        best[d.eng] = d
        o.waits = list(best.values()) + list(bestd.values())
        for d in best.values():
            d.signal = True
            self.waited[eng][d.eng] = d.idx
        for d in bestd.values():
            self.waited_dma[eng][d.dma_sem] = d.dma_val
        for k in reads:
            self.readers.setdefault(k, []).append(o)
        for k in writes:
            self.last_w[k] = o
            self.readers[k] = []
        self.ops[eng].append(o)
        return o

    def mm(self, out, lhsT, rhs, start, stop, r, w, skip=False):
        if skip:
            return self.op("pe", lambda e: e.matmul(out, lhsT, rhs, start=start, stop=stop,
                                                    skip_group_check=True), r, w, sync_same=False)
        return self.op("pe", lambda e: e.matmul(out, lhsT, rhs, start=start, stop=stop),
                       r, w, sync_same=False)

    def act(self, out, in_, func, r, w, bias=None, scale=1.0, accum_out=None):
        kw = {}
        if bias is not None:
            kw["bias"] = bias
        if accum_out is not None:
            kw["accum_out"] = accum_out
        return self.op("act", lambda e: e.activation(out, in_, func, scale=scale, **kw), r, w)

    def tt(self, eng, out, in0, in1, op, r, w):
        return self.op(eng, lambda e: e.tensor_tensor(out, in0, in1, op), r, w)

    def ts(self, eng, out, in0, s1, s2, op0, op1, r, w, accum_out=None):
        if op1 is None:
            return self.op(eng, lambda e: e.tensor_scalar(out, in0, s1, None, op0), r, w)
        if accum_out is not None:
            return self.op(eng, lambda e: e.tensor_scalar(out, in0, s1, s2, op0, op1, accum_out), r, w)
        return self.op(eng, lambda e: e.tensor_scalar(out, in0, s1, s2, op0, op1), r, w)

    def stt(self, eng, out, in0, scalar, in1, op0, op1, r, w):
        return self.op(eng, lambda e: e.scalar_tensor_tensor(out, in0, scalar, in1, op0, op1), r, w)

    def copy(self, eng, out, in_, r, w):
        if eng == "act":
            return self.op(eng, lambda e: e.copy(out, in_), r, w)
        return self.op(eng, lambda e: e.tensor_copy(out, in_), r, w)

    def memset(self, eng, ap, val, w):
        return self.op(eng, lambda e: e.memset(ap, val), (), w)

    def dma(self, q, out, in_, sem, r, w):
        return self.op(q, lambda e: e.dma_start(out=out, in_=in_), r, w, dma=sem)

    def wait_all(self, eng, keys):
        return self.op(eng, None, keys, ())

    def cc(self, kind, ins, outs, sem, r, w):
        return self.op("pool", lambda e: e.collective_compute(
            kind, ALU.bypass, replica_groups=[list(range(NCORES))], ins=ins, outs=outs),
            r, w, dma=(sem, 1))

    def barrier(self):
        keys = []
        for e in ENGS:
            last = None
            for o in reversed(self.ops[e]):
                if o.dma_sem is None and o.fn is not None:
                    last = o
                    break
            if last is not None:
                self.last_w[("__bar__", e)] = last
                self.readers[("__bar__", e)] = []
                keys.append(("__bar__", e))
        for n, o in self.dma_last.items():
            self.last_w[("__bard__", n)] = o
            self.readers[("__bard__", n)] = []
            keys.append(("__bard__", n))
        for e in ENGS:
            self.op(e, None, keys, ())

    def emit(self):
        nc = self.nc
        for e in ENGS:
            c = 0
            for o in self.ops[e]:
                if o.signal:
                    c += 1
                o.sigval = c
        import contextlib
        with contextlib.ExitStack() as st:
            esem = {e: st.enter_context(nc.semaphore("s_" + e)) for e in ENGS}
            dsem = {n: st.enter_context(nc.semaphore("d_" + n)) for n in self.dma_cnt}
            block = st.enter_context(nc.Block())

            def run(ename, eng):
                for o in self.ops[ename]:
                    for d in o.waits:
                        if d.dma_sem is not None:
                            eng.wait_ge(dsem[d.dma_sem], d.dma_val)
                        else:
                            eng.wait_ge(esem[d.eng], d.sigval)
                    if o.fn is None:
                        continue
                    ins = o.fn(eng)
                    if o.dma_sem is not None:
                        ins.then_inc(dsem[o.dma_sem], self.dma_inc[o.dma_sem])
                    elif o.signal:
                        ins.then_inc(esem[ename], 1)

            @block.tensor
            def _(eng):
                run("pe", eng)

            @block.scalar
            def _(eng):
                run("act", eng)

            @block.vector
            def _(eng):
                run("dve", eng)

            @block.gpsimd
            def _(eng):
                run("pool", eng)

            @block.sync
            def _(eng):
                run("sp", eng)


H = 6
NEG = -30000.0
SCALE_B = 192.0 ** -0.5
TWO_PI = 6.283185307179586


def _col(v, k):
    return np.ascontiguousarray(np.asarray(v, np.float32).reshape(k, 128).T)


def _bc(v):
    v = np.atleast_1d(np.asarray(v, np.float32))
    return np.ascontiguousarray(np.broadcast_to(v[None, :], (128, v.shape[0])))


class SmallPack:
    def __init__(self):
        self.cols, self.off, self.n = [], {}, 0

    def add(self, name, arr):
        arr = np.asarray(arr, np.float32)
        assert arr.shape[0] == 128, (name, arr.shape)
        self.off[name] = (self.n, arr.shape[1])
        self.cols.append(arr)
        self.n += arr.shape[1]

    def build(self):
        return np.ascontiguousarray(np.concatenate(self.cols, axis=1))


def pack_small(inp, c):
    sp = SmallPack()
    for l in range(DEPTH):
        sp.add(f"ffn1_norm{l}", _col(inp["ffn1_norm"][l], 8))
        sp.add(f"mix_norm{l}", _col(inp["mix_norm"][l], 8))
        sp.add(f"ffn2_norm{l}", _col(inp["ffn2_norm"][l], 8))
    sp.add("kv_in_norm", _col(inp["kv_in_norm"], 8))
    sp.add("final_norm", _col(inp["final_norm"], 8))
    sp.add("mem_norm", _col(inp["mem_norm"], 8))
    h = c % H
    for i in range(2):
        cw = np.asarray(inp["a_conv"][i], np.float32)
        cols = []
        for cc in range(3):
            for j in range(4):
                cols.append(cw[j, cc * 768 + h * 128: cc * 768 + (h + 1) * 128])
        sp.add(f"conv{i}", np.stack(cols, axis=1))
        sp.add(f"alog{i}", _bc(inp["a_A_log"][i][h]))
        sp.add(f"dtb{i}", _bc(inp["a_dt_bias"][i][h]))
        sp.add(f"ognorm{i}", _bc(inp["a_out_norm"][i]))
    for j in range(2):
        sp.add(f"bqnorm{j}", _col(inp["b_q_norm"][j], 2))
    sp.add("kvlat", _col(inp["kv_lat_norm"], 2))
    inv = 10000.0 ** (-np.arange(0, 64, 2, dtype=np.float32) / 64.0)
    invf = np.zeros((128, 1), np.float32)
    invf[0:32, 0] = inv
    invf[32:64, 0] = inv
    sp.add("invf", invf)
    sgn = np.zeros((128, 1), np.float32)
    sgn[0:32] = -1.0
    sgn[32:64] = 1.0
    sp.add("sgn", sgn)
    oh = np.zeros((128, 8), np.float32)
    oh[:, c] = 1.0
    sp.add("onehot", oh)
    p = np.arange(128)
    same = (p[:, None] // 64) == (p[None, :] // 64)
    sp.add("ident", np.eye(128, dtype=np.float32))
    sp.add("ltri", (same & (p[:, None] <= p[None, :])).astype(np.float32))
    sp.add("bones", same.astype(np.float32))
    low = same & (p[:, None] >= p[None, :])
    sp.add("mneg", np.where(low, 0.0, NEG).astype(np.float32))
    sp.add("mnegT", np.where(low.T, 0.0, NEG).astype(np.float32))
    sp.add("sneg", np.where(same & (p[:, None] > p[None, :]), -1.0, 0.0).astype(np.float32))
    cind = np.zeros((128, 2), np.float32)
    cind[0:64, 0] = 1.0
    cind[64:128, 1] = 1.0
    sp.add("cind", cind)
    vis = np.zeros((128, 16, 128, 2), np.float32)
    for i in range(16):
        gi = (NCORES * (i // 4) + c) * 4 + i % 4
        kb = np.arange(128)
        for qc in range(2):
            qchunk = 2 * gi + qc
            vis[0:64, i, :, qc] = np.where(2 * kb <= qchunk, 0.0, NEG)[None, :]
            vis[64:128, i, :, qc] = np.where(2 * kb + 1 <= qchunk, 0.0, NEG)[None, :]
    return sp, np.ascontiguousarray(vis.reshape(128, -1))


def _swap_rope(w):
    return np.concatenate([w[..., 32:64], w[..., 0:32]], axis=-1)


def host_inputs(inp, c):
    f = lambda a: np.ascontiguousarray(np.asarray(a, np.float32))
    m = {}
    x = np.asarray(inp["x"], np.float32)[0]
    m["xT_in"] = f(x[_tok_idx(c), :].T)
    m["memT"] = f(np.asarray(inp["mem"], np.float32)[0].T)
    pos = np.asarray(inp["positions"])[0, _tok_idx(c)].astype(np.int32)
    m["pos"] = np.ascontiguousarray(np.broadcast_to(pos[None, :], (128, T)))
    for l in range(DEPTH):
        for which in (1, 2):
            gu = np.asarray(inp[f"ffn{which}_w_gu"][l], np.float32)[c * 128:(c + 1) * 128, :]
            dn = np.zeros((3072, D), np.float32)
            dn[0:DFF] = np.asarray(inp[f"ffn{which}_w_down"][l], np.float32)
            dn = dn[c * 384:(c + 1) * 384]
            m[f"wp_f{which}_{l}"] = f(np.concatenate([gu.ravel(), dn.ravel()]).reshape(128, -1))
        parts = [np.asarray(inp["w_out"][l], np.float32)[c * 128:(c + 1) * 128].ravel(),
                 np.asarray(inp["w_mem_kv"][l], np.float32)[c * 128:(c + 1) * 128].ravel()]
        if l >= 2:
            parts.append(np.asarray(inp["b_w_in"][l - 2], np.float32)[c * 128:(c + 1) * 128].ravel())
        m[f"wp_mx_{l}"] = f(np.concatenate(parts).reshape(128, -1))
    h = c % H
    aw = []
    for i in range(2):
        w = np.asarray(inp["a_w_in"][i], np.float32)
        aw.append(np.concatenate([
            w[:, h * 128:(h + 1) * 128], w[:, 768 + h * 128:768 + (h + 1) * 128],
            w[:, 1536 + h * 128:1536 + (h + 1) * 128], w[:, 2304 + h * 128:2304 + (h + 1) * 128],
            w[:, 3072 + h:3073 + h], w[:, 3078 + h:3079 + h], w[:, 3084:3340]], axis=1))
    m["awin"] = f(np.stack(aw))
    wq = []
    for j in range(2):
        w = np.asarray(inp["b_w_uq"][j], np.float32)
        cols = []
        for hh in range(H):
            nope = w[:, hh * 192:hh * 192 + 128]
            rope = w[:, hh * 192 + 128:hh * 192 + 192]
            cols += [nope, rope, _swap_rope(rope)]
        wq.append(np.concatenate(cols, axis=1))
    m["wuq"] = f(np.stack(wq))
    wd = np.asarray(inp["w_dkv"], np.float32)
    m["wdkv"] = f(np.concatenate([wd[:, 0:256], wd[:, 256:320], _swap_rope(wd[:, 256:320])], axis=1))
    wk = np.asarray(inp["w_ukv"], np.float32)
    m["wukT"] = f(np.stack([wk[:, hh * 256:hh * 256 + 128].T for hh in range(H)]))
    m["wuv"] = f(np.concatenate([wk[:, hh * 256 + 128:hh * 256 + 256] for hh in range(H)], axis=1))
    sp, vis = pack_small(inp, c)
    m["small"] = sp.build()
    m["visb"] = vis
    return m


def _tok_idx(c):
    return ((np.arange(4)[:, None] * NCORES + c) * 512 + np.arange(512)[None, :]).reshape(-1)


class Rot:
    def __init__(self, name, tiles):
        self.name, self.tiles, self.i = name, tiles, 0

    def next(self):
        j = self.i % len(self.tiles)
        self.i += 1
        return self.tiles[j], (self.name, j)


NF1 = 128 * 5632 + 384 * D
OFF_DN = 128 * 5632


class Builder:
    def __init__(self, small_off, n_small, stop_after=None, dbg=False):
        self.stop_after = stop_after
        nc = bass.Bass("TRN2", target_bir_lowering=False)
        self.nc = nc
        self.P = Prog(nc)
        self.small_off = small_off
        dt = nc.dram_tensor
        ext = lambda name, shape, d=F32: dt(name, shape, d, kind="ExternalInput").ap()
        self.xT_in = ext("xT_in", [D, T])
        self.memT_in = ext("memT", [D, 256])
        self.pos_in = ext("pos", [128, T], mybir.dt.int32)
        self.small_in = ext("small", [128, n_small])
        self.visb_in = ext("visb", [128, 4096])
        self.awin = ext("awin", [2, D, 770])
        self.wuq_in = ext("wuq", [2, 256, 1536])
        self.wdkv_in = ext("wdkv", [D, 384])
        self.wukT_in = ext("wukT", [H, 128, 256])
        self.wuv_in = ext("wuv", [256, 768])
        self.wp = {}
        self.wfull = {}
        self.wsb = {}
        self.gsize = {}
        for l in range(DEPTH):
            for part in ("f1", "mx", "f2"):
                n = NF1 if part != "mx" else (128 * 1024 + 128 * 512 + (128 * 512 if l >= 2 else 0))
                g = f"{part}_{l}"
                self.gsize[g] = n
                self.wp[g] = ext(f"wp_{g}", [128, n // 128])
                self.wsb[g] = dt(f"wsb_{g}", [128, n // 128], BF16)
                self.wfull[g] = dt(f"wfull_{g}", [1024, n // 128], BF16)
        self.outT = dt("outT", [D, T], F32, kind="ExternalOutput").ap()
        self.xn_loc = dt("xn_loc", [D, T], BF16)
        self.xn_all = dt("xn_all", [NCORES * D, T], BF16)
        self.mix_loc = dt("mix_loc", [128, S], BF16)
        self.mix_all = dt("mix_all", [NCORES * 128, S], BF16)
        self.kT_loc = dt("kT_loc", [384, T], BF16)
        self.kT_all = dt("kT_all", [NCORES * 384, T], BF16)
        self.c_loc = dt("c_loc", [T, 256], BF16)
        self.c_all = dt("c_all", [S, 256], BF16)

        A = nc.alloc_sbuf_tensor
        self.xT = A("xT", [128, KD, T], F32)
        self.xn = A("xn", [128, KD, T], BF16)
        self.small = A("small_sb", [128, n_small], F32)
        self.ones_bf = A("ones_bf", [128, 128], BF16)
        self.ones_f = A("ones_f", [128, 128], F32)
        self.ident_bf = A("ident_bf", [128, 128], BF16)
        self.sq = A("sq", [128, KD, 512], BF16)
        self.rstd = A("rstd", [128, 512], F32)
        self.memn = A("memn", [128, KD, 256], BF16)
        self.ARENA = 88 * 1024
        self.arena = A("arena", [128, self.ARENA // 4], F32)
        self.aoff = 0
        self.ps2 = [nc.alloc_psum_tensor(f"pp{i}", [128, 1024], F32) for i in range(4)]
        self.ps = [self.ps2[i // 2][:, (i % 2) * 512:(i % 2 + 1) * 512] for i in range(8)]
        self.ffn_groups = [(g * 512, 4) for g in range(5)] + [(2560, 2)]
        self.dbg_outs = {}
        self.dbg = dbg

    def sm(self, name):
        o, w = self.small_off[name]
        return self.small[:, o:o + w]

    def arena_reset(self):
        self.P.barrier()
        self.aoff = 0

    def carve(self, shape, dtype):
        esz = 2 if dtype == BF16 else 4
        n = 1
        for s_ in shape[1:]:
            n *= s_
        nbytes = (n * esz + 31) // 32 * 32
        assert self.aoff + nbytes <= self.ARENA, (self.aoff, nbytes)
        a = self.arena[:, self.aoff // 4:(self.aoff + nbytes) // 4]
        self.aoff += nbytes
        if dtype != F32:
            a = a.bitcast(dtype)
        a = a[:, 0:n]
        if len(shape) == 3:
            a = a.rearrange("p (a b) -> p a b", a=shape[1])
        elif len(shape) == 4:
            a = a.rearrange("p (a b c) -> p a b c", a=shape[1], b=shape[2])
        return a

    def dump(self, name, ap, shape, keys, dtype=F32):
        if not self.dbg:
            return
        t = self.nc.dram_tensor("dbg_" + name, shape, dtype, kind="ExternalOutput").ap()
        self.P.dma("sp", t, ap, "dbg", keys, [("dbgout", name)])
        self.dbg_outs[name] = t

    def wview(self, g):
        n = self.gsize[g]
        return self.wfull[g].ap().rearrange("(r a) b -> r (a b)", a=128)

    def w_cast(self, g):
        n = self.gsize[g] // 128
        step = (n + 3) // 4
        for a in range(0, n, step):
            b = min(n, a + step)
            self.P.dma("pool", self.wsb[g][:, a:b], self.wp[g][:, a:b], "wc_" + g, [], [("wsb", g, a)])
        self.wsb_keys = getattr(self, "wsb_keys", {})
        self.wsb_keys[g] = [("wsb", g, a) for a in range(0, n, step)]

    def w_gather(self, g):
        self.P.cc("AllGather", [self.wsb[g].ap().opt()], [self.wfull[g].ap().opt()], "wag",
                  self.wsb_keys[g], [("wfull", g)])

    def norm_fm(self, src, skeys, nk, ncol, dn, gain, dst, dkeys, psb=6):
        P = self.P
        psn = self.ps[psb]
        for k in range(nk):
            P.act(self.sq[:, k, 0:ncol], src(k), AF.Square, skeys(k), [("sq", k)])
        for k in range(nk):
            P.mm(psn[:, 0:ncol], self.ones_bf[:, :], self.sq[:, k, 0:ncol], k == 0, k == nk - 1,
                 [("sq", k), ("ones",)], [("ps", psb)])
        P.ts("dve", self.rstd[:, 0:ncol], psn[:, 0:ncol], 1.0 / dn, EPS, ALU.mult, ALU.add,
             [("ps", psb)], [("rstd",)])
        P.act(self.rstd[:, 0:ncol], self.rstd[:, 0:ncol], AF.Ln, [("rstd",)], [("rstd",)])
        P.act(self.rstd[:, 0:ncol], self.rstd[:, 0:ncol], AF.Exp, [("rstd",)], [("rstd",)], scale=-0.5)
        for k in range(nk):
            P.stt("dve", dst(k), src(k), gain[:, k:k + 1], self.rstd[:, 0:ncol],
                  ALU.mult, ALU.mult, skeys(k) + [("rstd",), ("small",)], dkeys(k))

    def norm_x(self, gname):
        for tb in range(NTB):
            c0 = tb * 512
            self.norm_fm(lambda k: self.xT[:, k, c0:c0 + 512], lambda k: [("x", k, tb)], KD, 512, D,
                         self.sm(gname), lambda k: self.xn[:, k, c0:c0 + 512],
                         lambda k: [("xn", k, tb)])

    def ffn_alloc(self):
        self.NSLOT = 2
        self.wg = [self.carve([128, KD, 512], BF16) for _ in range(self.NSLOT)]
        self.wu = [self.carve([128, KD, 512], BF16) for _ in range(self.NSLOT)]
        self.wd = [self.carve([128, 4, D], BF16) for _ in range(self.NSLOT)]
        self.aT = [self.carve([128, 4, T], BF16) for _ in range(2)]
        self.sil = [self.carve([128, 512], F32) for _ in range(2)]
        self.grp_ctr = 0
        self.sil_ctr = 0

    def ffn_load(self, g, gi):
        P = self.P
        f0, nch = self.ffn_groups[gi]
        n = nch * 128
        s = (self.grp_ctr + gi) % self.NSLOT
        v = self.wview(g)
        gu = v[:, 0:OFF_DN].rearrange("r (p f) -> p r f", p=128)
        P.dma("sp", self.wg[s][:, :, 0:n], gu[:, :, f0:f0 + n], f"wg{s}", [("wfull", g)], [("wg", s)])
        P.dma("sp", self.wu[s][:, :, 0:n], gu[:, :, DFF + f0:DFF + f0 + n], f"wu{s}",
              [("wfull", g)], [("wu", s)])
        for jj in range(nch):
            j = f0 // 128 + jj
            r, jr = j // 3, j % 3
            src = v[r, OFF_DN + jr * 128 * D:OFF_DN + (jr + 1) * 128 * D].rearrange("(p d) -> p d", p=128)
            P.dma("sp", self.wd[s][:, jj, :], src, f"wd{s}", [("wfull", g)], [("wd", s, jj)])

    def ffn_gu(self, gi):
        P = self.P
        f0, nch = self.ffn_groups[gi]
        s = (self.grp_ctr + gi) % self.NSLOT
        a = (self.grp_ctr + gi) % 2
        for tb in range(NTB):
            c0 = tb * 512
            for j in range(nch):
                pb = 2 * (self.sil_ctr % 2)
                psg, psu = self.ps[pb], self.ps[pb + 1]
                for k in range(KD):
                    P.mm(psg[:, :], self.wg[s][:, k, j * 128:(j + 1) * 128], self.xn[:, k, c0:c0 + 512],
                         k == 0, k == KD - 1, [("wg", s), ("xn", k, tb)], [("ps", pb)])
                for k in range(KD):
                    P.mm(psu[:, :], self.wu[s][:, k, j * 128:(j + 1) * 128], self.xn[:, k, c0:c0 + 512],
                         k == 0, k == KD - 1, [("wu", s), ("xn", k, tb)], [("ps", pb + 1)])
                st = self.sil[self.sil_ctr % 2]
                sk = ("sil", self.sil_ctr % 2)
                P.act(st[:, :], psg[:, :], AF.Silu, [("ps", pb)], [sk])
                P.tt("dve", self.aT[a][:, j, c0:c0 + 512], psu[:, :], st[:, :], ALU.mult,
                     [("ps", pb + 1), sk], [("aT", a, j, tb)])
                self.sil_ctr += 1

    def ffn_down(self, gi):
        P = self.P
        f0, nch = self.ffn_groups[gi]
        s = (self.grp_ctr + gi) % self.NSLOT
        a = (self.grp_ctr + gi) % 2
        for tb in range(NTB):
            c0 = tb * 512
            for i in range(KD):
                pb = 4 + (i % 2)
                psy = self.ps[pb]
                for j in range(nch):
                    P.mm(psy[:, :], self.wd[s][:, j, i * 128:(i + 1) * 128], self.aT[a][:, j, c0:c0 + 512],
                         j == 0, j == nch - 1, [("wd", s, jj) for jj in range(nch)] + [("aT", a, j, tb)],
                         [("ps", pb)])
                P.stt("dve", self.xT[:, i, c0:c0 + 512], psy[:, :], 0.5, self.xT[:, i, c0:c0 + 512],
                      ALU.mult, ALU.add, [("ps", pb), ("x", i, tb)], [("x", i, tb)])

    def ffn(self, l, which):
        self.arena_reset()
        self.ffn_alloc()
        g = f"f{which}_{l}"
        G = len(self.ffn_groups)
        self.ffn_load(g, 0)
        self.ffn_load(g, 1)
        self.norm_x(f"ffn{which}_norm{l}")
        self.ffn_gu(0)
        for gi in range(G):
            if gi + 1 < G:
                self.ffn_gu(gi + 1)
            self.ffn_down(gi)
            if gi + 2 < G:
                self.ffn_load(g, gi + 2)
        self.grp_ctr += G

    def outproj(self, l, mixfn, mkeys):
        P = self.P
        g = f"mx_{l}"
        wout = self.carve([128, KD, D], BF16)
        src = self.wview(g)[:, 0:128 * D].rearrange("r (p f) -> p r f", p=128)
        P.dma("sp", wout, src, "wout", [("wfull", g)], [("wout",)])
        for tb in range(NTB):
            c0 = tb * 512
            for i in range(KD):
                pb = 4 + (i % 2)
                psy = self.ps[pb]
                for c in range(KD):
                    P.mm(psy[:, :], wout[:, c, i * 128:(i + 1) * 128], mixfn(c, tb),
                         c == 0, c == KD - 1, [("wout",)] + mkeys(c, tb), [("ps", pb)])
                P.tt("dve", self.xT[:, i, c0:c0 + 512], psy[:, :], self.xT[:, i, c0:c0 + 512], ALU.add,
                     [("ps", pb), ("x", i, tb)], [("x", i, tb)])

    def mem_attn(self, l, qmT, qkeys, dstfn):
        P = self.P
        g = f"mx_{l}"
        wmkv = self.carve([128, KD, 512], BF16)
        src = self.wview(g)[:, 128 * D:128 * D + 128 * 512].rearrange("r (p f) -> p r f", p=128)
        P.dma("sp", wmkv, src, "wmkv", [("wfull", g)], [("wmkv",)])
        mkT = self.carve([128, 2, 256], BF16)
        vpad = self.carve([128, 4, 2, 128], BF16)
        opad = self.carve([128, 2, 128], BF16)
        P.memset("dve", vpad, 0.0, [("vpad",)])
        P.memset("dve", opad, 0.0, [("opad",)])
        for hp in range(2):
            P.memset("dve", opad[:, hp, hp * 64:(hp + 1) * 64], 1.0, [("opad",)])
        for m2 in range(2):
            ps = self.ps[0]
            for k in range(KD):
                P.mm(ps[:, 0:256], wmkv[:, k, m2 * 128:(m2 + 1) * 128], self.memn[:, k, :],
                     k == 0, k == KD - 1, [("wmkv",), ("memn",)], [("ps", 0)])
            P.ts("dve", mkT[:, m2, :], ps[:, 0:256], 0.125, None, ALU.mult, None, [("ps", 0)], [("mkT",)])
        for mb in range(2):
            ps = self.ps[1]
            for k in range(KD):
                P.mm(ps[:, 0:256], self.memn[:, k, mb * 128:(mb + 1) * 128], wmkv[:, k, 256:512],
                     k == 0, k == KD - 1, [("wmkv",), ("memn",)], [("ps", 1)])
            for h in range(4):
                P.copy("act", vpad[:, h, mb, (h % 2) * 64:(h % 2) * 64 + 64], ps[:, h * 64:(h + 1) * 64],
                       [("ps", 1)], [("vpad",)])
        ptR = Rot("mpt", [self.carve([128, 512], BF16) for _ in range(8)])
        rl = self.carve([128, 512], F32)
        sctr = 0
        for tb in range(NTB):
            c0 = tb * 512
            for m2 in range(2):
                pts = []
                for hp in range(2):
                    for mb in range(2):
                        pb = sctr % 2
                        sctr += 1
                        ps = self.ps[pb]
                        P.mm(ps[:, :], mkT[hp * 64:(hp + 1) * 64, m2, mb * 128:(mb + 1) * 128],
                             qmT[hp * 64:(hp + 1) * 64, m2, c0:c0 + 512], True, True,
                             [("mkT",)] + qkeys(m2, tb), [("ps", pb)])
                        pt, pk = ptR.next()
                        P.act(pt, ps[:, :], AF.Exp, [("ps", pb)], [pk])
                        pts.append((pt, pk, hp, mb))
                pso, psl = self.ps[2], self.ps[3]
                for n_, (pt, pk, hp, mb) in enumerate(pts):
                    P.mm(pso[:, :], vpad[:, 2 * m2 + hp, mb, :], pt, n_ == 0, n_ == 3,
                         [("vpad",), pk], [("ps", 2)])
                for n_, (pt, pk, hp, mb) in enumerate(pts):
                    P.mm(psl[:, :], opad[:, hp, :], pt, n_ == 0, n_ == 3, [("opad",), pk], [("ps", 3)])
                P.op("dve", lambda e, a=rl, b=psl: e.reciprocal(a, b[:, :]), [("ps", 3)], [("mrl",)])
                P.tt("dve", dstfn(m2, tb), pso[:, :], rl, ALU.mult,
                     [("ps", 2), ("mrl",)], [("mix", 6 + m2, tb)])

    def rope_tables(self):
        P = self.P
        self.rope_dram = self.nc.dram_tensor("rope_dram", [2, 64, T], F32)
        CCt = self.carve([128, T], F32)
        SSt = self.carve([128, T], F32)
        posi = self.carve([128, T], mybir.dt.int32)
        xf = self.carve([128, T], F32)
        ki = self.carve([128, T], mybir.dt.int32)
        kf = self.carve([128, T], F32)
        msk = self.carve([128, T], F32)
        P.dma("sp", posi, self.pos_in[:, :], "pos", [], [("posi",)])
        R = slice(0, 64)
        P.copy("dve", xf[R], posi[R], [("posi",)], [("rx",)])
        P.ts("dve", xf[R], xf[R], self.sm("invf")[R, 0:1], 1.0 / TWO_PI, ALU.mult, ALU.mult,
             [("rx",), ("small",)], [("rx",)])

        def wrap(r_key):
            P.copy("dve", ki[R], xf[R], [r_key], [("rk",)])
            P.copy("dve", kf[R], ki[R], [("rk",)], [("rkf",)])
            P.tt("dve", xf[R], xf[R], kf[R], ALU.subtract, [r_key, ("rkf",)], [r_key])
            P.ts("dve", msk[R], xf[R], 0.5, None, ALU.is_gt, None, [r_key], [("rm",)])
            P.tt("dve", xf[R], xf[R], msk[R], ALU.subtract, [r_key, ("rm",)], [r_key])
            P.ts("dve", msk[R], xf[R], -0.5, None, ALU.is_lt, None, [r_key], [("rm",)])
            P.tt("dve", xf[R], xf[R], msk[R], ALU.add, [r_key, ("rm",)], [r_key])

        wrap(("rx",))
        P.act(SSt[R], xf[R], AF.Sin, [("rx",)], [("SS",)], scale=TWO_PI * (1.0 - 1e-6))
        P.ts("dve", SSt[R], SSt[R], self.sm("sgn")[R, 0:1], None, ALU.mult, None,
             [("SS",), ("small",)], [("SS",)])
        P.ts("dve", xf[R], xf[R], 0.25, None, ALU.add, None, [("rx",)], [("rx",)])
        wrap(("rx",))
        P.act(CCt[R], xf[R], AF.Sin, [("rx",)], [("CC",)], scale=TWO_PI * (1.0 - 1e-6))
        P.dma("sp", self.rope_dram[0], CCt[R], "rope0", [("CC",)], [("rope_dram", 0)])
        P.dma("sp", self.rope_dram[1], SSt[R], "rope1", [("SS",)], [("rope_dram", 1)])

    def a_mixer(self, i):
        P = self.P
        l = i
        self.arena_reset()
        self.norm_x(f"mix_norm{l}")
        for k in range(KD):
            P.dma("sp", self.xn_loc[k * 128:(k + 1) * 128, :], self.xn[:, k, :], f"xo{k % 4}",
                  [("xn", k, tb) for tb in range(NTB)], [("xn_loc", k)])
        P.cc("AllGather", [self.xn_loc.ap().opt()], [self.xn_all.ap().opt()], "ag",
             [("xn_loc", k) for k in range(KD)], [("xn_all",)])
        self.w_gather(f"f2_{l}")
        if l + 1 < DEPTH:
            self.w_gather(f"f1_{l + 1}")
            self.w_gather(f"mx_{l + 1}")

        mixmem = self.carve([128, 2, T], BF16)
        wqkv = self.carve([128, KD, 384], BF16)
        wgba = self.carve([128, KD, 130], BF16)
        wqm = self.carve([128, KD, 256], BF16)
        P.dma("pool", wqkv, self.awin[i, :, 0:384].rearrange("(k p) c -> p k c", p=128), "aw_qkv", [], [("wqkv",)])
        P.dma("pool", wgba, self.awin[i, :, 384:514].rearrange("(k p) c -> p k c", p=128), "aw_gba", [], [("wgba",)])
        P.dma("pool", wqm, self.awin[i, :, 514:770].rearrange("(k p) c -> p k c", p=128), "aw_qm", [], [("wqm",)])

        mark = self.aoff
        qmT = self.carve([128, 2, T], BF16)
        for tb in range(NTB):
            c0 = tb * 512
            for m2 in range(2):
                ps = self.ps[4 + m2]
                for k in range(KD):
                    P.mm(ps[:, :], wqm[:, k, m2 * 128:(m2 + 1) * 128], self.xn[:, k, c0:c0 + 512],
                         k == 0, k == KD - 1, [("wqm",), ("xn", k, tb)], [("ps", 4 + m2)])
                P.copy("act", qmT[:, m2, c0:c0 + 512], ps[:, :], [("ps", 4 + m2)], [("qm", m2, tb)])
        self.mem_attn(l, qmT, lambda m2, tb: [("qm", m2, tb)],
                      lambda m2, tb: mixmem[:, m2, tb * 512:(tb + 1) * 512])
        P.barrier()
        self.aoff = mark

        conv = self.sm(f"conv{i}")
        ident_f = self.sm("ident")
        ltri, bones = self.sm("ltri"), self.sm("bones")
        mneg, mnegT, sneg = self.sm("mneg"), self.sm("mnegT"), self.sm("sneg")
        cind = self.sm("cind")
        ogn = self.sm(f"ognorm{i}")
        cv = lambda shape, d: self.carve(shape, d)
        negA = cv([128, 1], F32)
        P.act(negA, self.sm(f"alog{i}"), AF.Exp, [("small",)], [("negA",)])
        P.ts("dve", negA, negA, -1.0, None, ALU.mult, None, [("negA",)], [("negA",)])
        S_f = cv([128, 128], F32)
        S_b = cv([128, 128], BF16)
        P.memset("dve", S_f, 0.0, [("S_f",)])
        P.memset("dve", S_b, 0.0, [("S_b",)])
        pre = [cv([128, 3, 515], BF16) for _ in range(2)]
        P.memset("dve", pre[0][:, :, 0:3], 0.0, [("pre", 0, c) for c in range(3)])
        xbR = Rot("xb", [cv([128, KD, 512], BF16) for _ in range(2)])
        acc = cv([128, 512], F32)
        sgt = cv([128, 512], F32)
        qkvT = cv([128, 3, 512], BF16)
        stageR = Rot("stg", [cv([128, 512], BF16) for _ in range(2)])
        junk = cv([128, 128], BF16)
        scR = Rot("sc", [cv([128, 32], F32) for _ in range(2)])
        dSR = Rot("dS", [cv([128, 2], F32) for _ in range(2)])
        t128 = lambda n, d, nm: Rot(nm, [cv([128, 128], d) for _ in range(n)])
        KbR, VbR, QsTR, QdTR, KnTR = (t128(2, BF16, "Kb"), t128(2, BF16, "Vb"), t128(2, BF16, "QsT"),
                                      t128(2, BF16, "QdT"), t128(2, BF16, "KnT"))
        KnR, QsR, QdR = t128(2, BF16, "Kn"), t128(2, BF16, "Qs"), t128(2, BF16, "Qd")
        Kd0R, Kd1R = t128(2, BF16, "Kd0"), t128(2, BF16, "Kd1")
        for t_, _k in [Kd0R.next() for _ in range(2)]:
            P.memset("dve", t_, 0.0, [_k])
        for t_, _k in [Kd1R.next() for _ in range(2)]:
            P.memset("dve", t_, 0.0, [_k])
        diagR, tmpR, decR, decTR = (t128(2, F32, "diag"), t128(2, F32, "tmp"), t128(2, F32, "dec"),
                                    t128(2, F32, "decT"))
        XR, XTR, PTR = t128(3, F32, "X"), t128(3, F32, "XT"), t128(3, F32, "PT")
        qkTR, PTbR, wTR, uR, vnR = (t128(2, BF16, "qkT"), t128(2, BF16, "PTb"), t128(2, BF16, "wT"),
                                    t128(2, F32, "u"), t128(2, BF16, "vn"))
        oR, sgR, yR, mtR = t128(2, F32, "o"), t128(2, F32, "sg"), t128(2, F32, "y"), t128(2, BF16, "mt")
        gsbR = t128(2, F32, "gsb")
        prev_st2 = []

        xall = self.xn_all.ap().rearrange("(r k p) t -> r p k t", r=NCORES, k=KD)
        NB = S // 512
        for b in range(NB):
            r, t0 = b % NCORES, (b // NCORES) * 512
            xb, xbk = xbR.next()
            P.dma("sp", xb, xall[r, :, :, t0:t0 + 512], f"xb{(b % 2)}", [("xn_all",)], [xbk])
            pr, prn = pre[b % 2], pre[(b + 1) % 2]
            for c in range(3):
                ps = self.ps[0]
                for k in range(KD):
                    P.mm(ps[:, :], wqkv[:, k, c * 128:(c + 1) * 128], xb[:, k, :], k == 0, k == KD - 1,
                         [("wqkv",), xbk], [("ps", 0)])
                P.copy("act", pr[:, c, 3:515], ps[:, :], [("ps", 0)], [("pre", b % 2, c)])
                pk = [("pre", b % 2, c)]
                P.ts("dve", acc, pr[:, c, 0:512], conv[:, 4 * c:4 * c + 1], None, ALU.mult, None,
                     pk + [("small",)], [("acc",)])
                for j in range(1, 4):
                    P.stt("dve", acc, pr[:, c, j:j + 512], conv[:, 4 * c + j:4 * c + j + 1], acc,
                          ALU.mult, ALU.add, pk + [("acc",), ("small",)], [("acc",)])
                P.copy("dve", prn[:, c, 0:3], pr[:, c, 512:515], pk, [("pre", (b + 1) % 2, c)])
                P.act(sgt, acc, AF.Exp, [("acc",)], [("sgt",)], scale=-1.0)
                P.ts("dve", sgt, sgt, 1.0, None, ALU.add, None, [("sgt",)], [("sgt",)])
                P.op("dve", lambda e, a=sgt: e.reciprocal(a, a), [("sgt",)], [("sgt",)])
                P.tt("dve", qkvT[:, c, :], acc, sgt, ALU.mult, [("acc",), ("sgt",)], [("qkvT", c)])
            stg, stgk = stageR.next()
            for tl in range(4):
                c0 = tl * 128
                P.capture_begin()
                sc, sck = scR.next()
                col = lambda n: sc[:, n:n + 1]
                SK = [sck]
                pst = self.ps[2]
                for c in range(3):
                    P.mm(pst[:, c * 128:(c + 1) * 128], qkvT[:, c, c0:c0 + 128], self.ident_bf[:, :],
                         True, True, [("qkvT", c), ("identb",)], [("pst", c)])
                psg = self.ps[3]
                for k in range(KD):
                    P.mm(psg[:, 0:130], xb[:, k, c0:c0 + 128], wgba[:, k, :], k == 0, k == KD - 1,
                         [xbk, ("wgba",)], [("psg",)])
                P.act(junk, pst[:, 0:128], AF.Square, [("pst", 0)], [("junk",), sck], accum_out=col(0))
                P.act(junk, pst[:, 128:256], AF.Square, [("pst", 1)], [("junk",), sck], accum_out=col(1))
                P.ts("dve", sc[:, 2:4], sc[:, 0:2], EPS, None, ALU.add, None, SK, SK)
                P.act(sc[:, 2:4], sc[:, 2:4], AF.Ln, SK, SK)
                P.act(sc[:, 2:4], sc[:, 2:4], AF.Exp, SK, SK, scale=-0.5)
                P.ts("dve", col(15), col(2), 128.0 ** -0.5, None, ALU.mult, None, SK, SK)
                P.act(col(4), psg[:, 128:129], AF.Exp, [("psg",)], SK, scale=-1.0)
                P.ts("dve", col(4), col(4), 1.0, None, ALU.add, None, SK, SK)
                P.op("dve", lambda e, a=col(4): e.reciprocal(a, a), SK, SK)
                P.ts("dve", col(5), psg[:, 129:130], self.sm(f"dtb{i}")[:, 0:1], None, ALU.add, None,
                     [("psg",), ("small",)], SK)
                P.act(col(6), col(5), AF.Abs, SK, SK)
                P.act(col(6), col(6), AF.Exp, SK, SK, scale=-1.0)
                P.act(col(6), col(6), AF.Ln, SK, SK, bias=1.0)
                P.stt("dve", col(7), col(5), 0.0, col(6), ALU.max, ALU.add, SK, SK)
                P.tt("dve", col(8), col(7), negA, ALU.mult, SK + [("negA",)], SK)
                pss = self.ps[3]
                P.mm(pss[:, 256:257], ltri, col(8), True, True, SK + [("small",)], [("pss", 0)])
                P.mm(pss[:, 257:258], bones, col(8), True, True, SK + [("small",)], [("pss", 1)])
                P.copy("dve", sc[:, 9:11], pss[:, 256:258], [("pss", 0), ("pss", 1)], SK)
                P.ts("dve", col(11), col(9), -1.0, None, ALU.mult, None, SK, SK)
                P.act(col(12), col(9), AF.Exp, SK, SK)
                P.act(col(13), col(9), AF.Exp, SK, SK, scale=-1.0, bias=col(10))
                P.tt("dve", col(14), col(4), col(12), ALU.mult, SK, SK)
                gsel = sc[:, 16:18]
                P.ts("dve", gsel, cind, col(8), None, ALU.mult, None, SK + [("small",)], SK)
                P.mm(pss[:, 258:260], self.ones_f[:, :], gsel, True, True, SK + [("onesf",)], [("pss", 2)])
                dS, dSk = dSR.next()
                P.act(dS, pss[:, 258:260], AF.Exp, [("pss", 2)], [dSk])
                Kn, Knk = KnR.next()
                Kb, Kbk = KbR.next()
                Kd0, Kd0k = Kd0R.next()
                Kd1, Kd1k = Kd1R.next()
                Qs, Qsk = QsR.next()
                Qd, Qdk = QdR.next()
                Vb, Vbk = VbR.next()
                kps, qps, vps = pst[:, 128:256], pst[:, 0:128], pst[:, 256:384]
                P.ts("dve", Kn, kps, col(3), None, ALU.mult, None, [("pst", 1)] + SK, [Knk])
                P.ts("dve", Kb, kps, col(3), col(14), ALU.mult, ALU.mult, [("pst", 1)] + SK, [Kbk])
                P.ts("dve", Kd0[0:64, :], kps[0:64, :], sc[0:64, 3:4], sc[0:64, 13:14], ALU.mult, ALU.mult,
                     [("pst", 1)] + SK, [Kd0k])
                P.ts("dve", Kd1[64:128, :], kps[64:128, :], sc[64:128, 3:4], sc[64:128, 13:14], ALU.mult,
                     ALU.mult, [("pst", 1)] + SK, [Kd1k])
                P.ts("dve", Qs, qps, col(15), None, ALU.mult, None, [("pst", 0)] + SK, [Qsk])
                P.ts("dve", Qd, qps, col(15), col(12), ALU.mult, ALU.mult, [("pst", 0)] + SK, [Qdk])
                P.ts("dve", Vb, vps, col(4), None, ALU.mult, None, [("pst", 2)] + SK, [Vbk])
                ps4 = self.ps[4]
                KnT, KnTk = KnTR.next()
                QsT, QsTk = QsTR.next()
                QdT, QdTk = QdTR.next()
                for n_, (src_, sk_, dst_, dk_) in enumerate(((Kn, Knk, KnT, KnTk), (Qs, Qsk, QsT, QsTk),
                                                              (Qd, Qdk, QdT, QdTk))):
                    P.mm(ps4[:, n_ * 128:(n_ + 1) * 128], src_, self.ident_bf[:, :], True, True,
                         [sk_, ("identb",)], [("ps4", n_)])
                    P.copy("act", dst_, ps4[:, n_ * 128:(n_ + 1) * 128], [("ps4", n_)], [dk_])
                ps5 = self.ps[5]
                P.mm(ps5[:, 0:128], KnT, KnT, True, True, [KnTk], [("ps5", 0)])
                P.mm(ps5[:, 128:256], KnT, QsT, True, True, [KnTk, QsTk], [("ps5", 1)])
                dg, dgk = diagR.next()
                P.ts("dve", dg, ident_f, col(9), None, ALU.mult, None, SK + [("small",)], [dgk])
                P.mm(ps5[:, 256:384], self.ones_f[:, :], dg, True, True, [dgk, ("onesf",)], [("ps5", 2)])
                tmp, tmpk = tmpR.next()
                dec, deck = decR.next()
                P.stt("dve", tmp, ps5[:, 256:384], -1.0, mneg, ALU.mult, ALU.add, [("ps5", 2), ("small",)], [tmpk])
                P.act(dec, tmp, AF.Exp, [tmpk] + SK, [deck], bias=col(9))
                tmp2, tmp2k = tmpR.next()
                decT, decTk = decTR.next()
                P.tt("dve", tmp2, ps5[:, 256:384], mnegT, ALU.add, [("ps5", 2), ("small",)], [tmp2k])
                P.act(decT, tmp2, AF.Exp, [tmp2k] + SK, [decTk], bias=col(11))
                X0, X0k = XR.next()
                P.stt("dve", X0, ps5[:, 0:128], col(4), dec, ALU.mult, ALU.mult, [("ps5", 0), deck] + SK, [X0k])
                P.tt("dve", X0, X0, sneg, ALU.mult, [X0k, ("small",)], [X0k])
                qkT, qkTk = qkTR.next()
                P.tt("dve", qkT, ps5[:, 128:256], decT, ALU.mult, [("ps5", 1), decTk], [qkTk])
                ps6 = self.ps[6]
                ps7 = self.ps[7]
                P.mm(ps6[:, 0:128], X0, ident_f, True, True, [X0k, ("small",)], [("ps6", 0)])
                X0T, X0Tk = XTR.next()
                P.copy("act", X0T, ps6[:, 0:128], [("ps6", 0)], [X0Tk])
                PT, PTk = PTR.next()
                P.tt("dve", PT, ps6[:, 0:128], ident_f, ALU.add, [("ps6", 0), ("small",)], [PTk])
                Xp, Xpk, XpT, XpTk = X0, X0k, X0T, X0Tk
                for kk in range(1, 6):
                    Xn, Xnk = XR.next()
                    P.mm(ps6[:, 128:256], XpT, Xp, True, True, [XpTk, Xpk], [("ps6", 1)])
                    P.copy("act", Xn, ps6[:, 128:256], [("ps6", 1)], [Xnk])
                    if kk < 5:
                        XnT, XnTk = XTR.next()
                        P.mm(ps6[:, 256:384], Xp, XpT, True, True, [XpTk, Xpk], [("ps6", 2)])
                        P.copy("act", XnT, ps6[:, 256:384], [("ps6", 2)], [XnTk])
                    P.mm(ps7[:, 0:128], Xn, PT, True, True, [Xnk, PTk], [("ps7", 0)])
                    PTn, PTnk = PTR.next()
                    P.tt("dve", PTn, ps7[:, 0:128], PT, ALU.add, [("ps7", 0), PTk], [PTnk])
                    PT, PTk = PTn, PTnk
                    Xp, Xpk = Xn, Xnk
                    if kk < 5:
                        XpT, XpTk = XnT, XnTk
                PTb, PTbk = PTbR.next()
                P.copy("act", PTb, PT, [PTk], [PTbk])
                wT, wTk = wTR.next()
                u, uk = uR.next()
                P.mm(ps7[:, 128:256], Kb, PTb, True, True, [Kbk, PTbk], [("ps7", 1)])
                P.copy("act", wT, ps7[:, 128:256], [("ps7", 1)], [wTk])
                P.mm(ps7[:, 256:384], PTb, Vb, True, True, [Vbk, PTbk], [("ps7", 2)])
                P.copy("act", u, ps7[:, 256:384], [("ps7", 2)], [uk])
                gsb, gsbk = gsbR.next()
                P.copy("act", gsb, psg[:, 0:128], [("psg",)], [gsbk])
                st1 = P.capture_end()
                P.capture_begin()
                o, ok_ = oR.next()
                ps1 = self.ps[1]
                for cch in range(2):
                    rows = slice(cch * 64, cch * 64 + 64)
                    Kd, Kdk = (Kd0, Kd0k) if cch == 0 else (Kd1, Kd1k)
                    vn, vnk = vnR.next()
                    P.mm(ps1[:, 0:128], wT, S_b, True, True, [wTk, ("S_b",)], [("ps1", 0)])
                    P.tt("dve", vn, u, ps1[:, 0:128], ALU.subtract, [uk, ("ps1", 0)], [vnk])
                    P.mm(ps1[:, 128:256], QdT, S_b, True, False, [QdTk, ("S_b",)], [("ps1", 1)])
                    P.mm(ps1[:, 128:256], qkT, vn, False, True, [qkTk, vnk], [("ps1", 1)])
                    P.copy("act", o[rows, :], ps1[rows, 128:256], [("ps1", 1)], [ok_])
                    P.mm(ps1[:, 256:384], Kd, vn, True, True, [Kdk, vnk], [("ps1", 2)])
                    P.stt("dve", S_f, S_f, dS[:, cch:cch + 1], ps1[:, 256:384], ALU.mult, ALU.add,
                          [("S_f",), dSk, ("ps1", 2)], [("S_f",)])
                    P.copy("act", S_b, S_f, [("S_f",)], [("S_b",)])
                P.act(junk, o, AF.Square, [ok_], [("junk",), sck], accum_out=col(20))
                P.ts("dve", col(21), col(20), 1.0 / 128.0, EPS, ALU.mult, ALU.add, SK, SK)
                P.act(col(21), col(21), AF.Ln, SK, SK)
                P.act(col(21), col(21), AF.Exp, SK, SK, scale=-0.5)
                sg, sgk = sgR.next()
                P.act(sg, gsb, AF.Exp, [gsbk], [sgk], scale=-1.0)
                P.ts("dve", sg, sg, 1.0, None, ALU.add, None, [sgk], [sgk])
                P.op("dve", lambda e, a=sg: e.reciprocal(a, a), [sgk], [sgk])
                P.tt("dve", sg, gsb, sg, ALU.mult, [gsbk, sgk], [sgk])
                y, yk = yR.next()
                P.stt("dve", y, o, col(21), ogn, ALU.mult, ALU.mult, [ok_, ("small",)] + SK, [yk])
                mt, mtk = mtR.next()
                P.tt("dve", mt, y, sg, ALU.mult, [yk, sgk], [mtk])
                P.mm(ps1[:, 384:512], mt, self.ident_bf[:, :], True, True, [mtk, ("identb",)], [("ps1", 3)])
                P.copy("act", stg[:, c0:c0 + 128], ps1[:, 384:512], [("ps1", 3)], [stgk])
                if tl == 3:
                    P.dma("sp", self.mix_loc[:, b * 512:(b + 1) * 512], stg, f"mo{b % 2}", [stgk],
                          [("mix_loc", b)])
                st2 = P.capture_end()
                P.replay(prev_st2, st1)
                prev_st2 = st2
        P.replay(prev_st2, [])
        P.cc("AllGather", [self.mix_loc.ap().opt()], [self.mix_all.ap().opt()], "ag",
             [("mix_loc", b) for b in range(NB)], [("mix_all",)])
        oh = self.sm("onehot")
        mall = self.mix_all.ap().rearrange("(h p) (j r t) -> h p j r t", p=128, j=4, r=NCORES)
        P.barrier()
        self.aoff = mark
        mixsel = self.carve([128, H, T], BF16)
        candR = Rot("cand", [self.carve([128, NCORES, 512], BF16) for _ in range(2)])
        sacc = self.carve([128, 512], F32)
        n_ = 0
        for tb in range(NTB):
            for ch in range(H):
                cand, ck = candR.next()
                P.dma("sp", cand, mall[ch, :, tb, :, :], f"cand{n_ % 2}", [("mix_all",)], [ck])
                n_ += 1
                P.ts("dve", sacc, cand[:, 0, :], oh[:, 0:1], None, ALU.mult, None, [ck, ("small",)], [("sacc",)])
                for r in range(1, NCORES):
                    last = r == NCORES - 1
                    dst = mixsel[:, ch, tb * 512:(tb + 1) * 512] if last else sacc
                    P.stt("dve", dst, cand[:, r, :], oh[:, r:r + 1], sacc, ALU.mult, ALU.add,
                          [ck, ("sacc",), ("small",)], [("mix", ch, tb)] if last else [("sacc",)])
        self.outproj(l, lambda c, tb: (mixsel[:, c, tb * 512:(tb + 1) * 512] if c < H
                                       else mixmem[:, c - H, tb * 512:(tb + 1) * 512]),
                     lambda c, tb: [("mix", c, tb)])

    def rope_load(self, dstC, dstS, c0, n, key):
        P = self.P
        P.dma("sp", dstC[0:64, 0:n], self.rope_dram[0][:, c0:c0 + n], "ropeld", [("rope_dram", 0)], [key])
        P.dma("sp", dstS[0:64, 0:n], self.rope_dram[1][:, c0:c0 + n], "ropeld", [("rope_dram", 1)], [key])

    def kv_build(self):
        P = self.P
        self.arena_reset()
        self.rope_tables()
        self.arena_reset()
        self.norm_x("kv_in_norm")
        wdkv = self.carve([128, KD, 384], BF16)
        P.dma("pool", wdkv, self.wdkv_in.rearrange("(k p) c -> p k c", p=128), "aw_dkv", [], [("wdkv",)])
        kst = self.carve([128, 3, T], BF16)
        P.memset("dve", kst[64:128, 2, :], 0.0, [("kst2",)])
        P.memset("dve", kst[64:65, 2, :], 1.0, [("kst2",)])
        cst = self.carve([128, 16, 256], BF16)
        ckf = self.carve([128, 2, 512], F32)
        CCb = self.carve([128, 512], F32)
        SSb = self.carve([128, 512], F32)
        t1 = self.carve([128, 512], F32)
        t2 = self.carve([128, 512], F32)
        kvl = self.sm("kvlat")
        for tb in range(NTB):
            c0 = tb * 512
            self.rope_load(CCb, SSb, c0, 512, ("ropeb",))
            for c in range(2):
                ps = self.ps[c]
                for k in range(KD):
                    P.mm(ps[:, :], wdkv[:, k, c * 128:(c + 1) * 128], self.xn[:, k, c0:c0 + 512],
                         k == 0, k == KD - 1, [("wdkv",), ("xn", k, tb)], [("ps", c)])
                P.copy("act", ckf[:, c, :], ps[:, :], [("ps", c)], [("ckf", c)])
            for c in range(2):
                ps = self.ps[2 + c]
                for k in range(KD):
                    P.mm(ps[0:64, :], wdkv[:, k, 256 + c * 64:320 + c * 64], self.xn[:, k, c0:c0 + 512],
                         k == 0, k == KD - 1, [("wdkv",), ("xn", k, tb)], [("ps", 2 + c)])
            self.norm_fm(lambda k: ckf[:, k, :], lambda k: [("ckf", k)], 2, 512, 256.0, kvl,
                         lambda k: kst[:, k, c0:c0 + 512], lambda k: [("kst", k, tb)])
            P.tt("dve", t1[0:64, :], self.ps[2][0:64, :], CCb[0:64, :], ALU.mult, [("ps", 2), ("ropeb",)], [("t1",)])
            P.tt("dve", t2[0:64, :], self.ps[3][0:64, :], SSb[0:64, :], ALU.mult, [("ps", 3), ("ropeb",)], [("t2",)])
            P.tt("dve", kst[0:64, 2, c0:c0 + 512], t1[0:64, :], t2[0:64, :], ALU.add, [("t1",), ("t2",)],
                 [("kst", 2, tb)])
            for tl in range(4):
                ti = tb * 4 + tl
                ps = self.ps[4 + (tl % 2)]
                for c in range(2):
                    P.mm(ps[:, c * 128:(c + 1) * 128], kst[:, c, ti * 128:(ti + 1) * 128], self.ident_bf[:, :],
                         True, True, [("kst", c, tb), ("identb",)], [("ps", 4 + (tl % 2))])
                P.copy("act", cst[:, ti, :], ps[:, 0:256], [("ps", 4 + (tl % 2))], [("cst", ti)])
        for c in range(3):
            P.dma("sp", self.kT_loc[c * 128:(c + 1) * 128, :], kst[:, c, :], "kvo",
                  [("kst", c, tb) for tb in range(NTB)] + [("kst2",)], [("kT_loc", c)])
        P.dma("sp", self.c_loc.ap().rearrange("(n p) c -> p n c", p=128), cst, "kvo_c",
              [("cst", ti) for ti in range(16)], [("c_loc",)])
        P.cc("AllGather", [self.kT_loc.ap().opt()], [self.kT_all.ap().opt()], "ag",
             [("kT_loc", c) for c in range(3)], [("kT_all",)])
        P.cc("AllGather", [self.c_loc.ap().opt()], [self.c_all.ap().opt()], "ag", [("c_loc",)], [("c_all",)])

    def b_mixer(self, j):
        P = self.P
        l = 2 + j
        g = f"mx_{l}"
        self.arena_reset()
        self.norm_x(f"mix_norm{l}")
        self.w_gather(f"f2_{l}")
        if l + 1 < DEPTH:
            self.w_gather(f"f1_{l + 1}")
            self.w_gather(f"mx_{l + 1}")
        mixT = self.carve([128, KD, T], BF16)
        cqn = self.carve([128, 2, T], BF16)
        mark = self.aoff
        wbin = self.carve([128, KD, 512], BF16)
        src = self.wview(g)[:, 128 * D + 128 * 512:128 * D + 2 * 128 * 512].rearrange("r (p f) -> p r f", p=128)
        P.dma("sp", wbin, src, "wbin", [("wfull", g)], [("wbin",)])
        qmT = self.carve([128, 2, T], BF16)
        cqf = self.carve([128, 2, 512], F32)
        for tb in range(NTB):
            c0 = tb * 512
            for c in range(4):
                ps = self.ps[c % 2]
                for k in range(KD):
                    P.mm(ps[:, :], wbin[:, k, c * 128:(c + 1) * 128], self.xn[:, k, c0:c0 + 512],
                         k == 0, k == KD - 1, [("wbin",), ("xn", k, tb)], [("ps", c % 2)])
                if c < 2:
                    P.copy("act", cqf[:, c, :], ps[:, :], [("ps", c % 2)], [("cqf", c)])
                else:
                    P.copy("act", qmT[:, c - 2, c0:c0 + 512], ps[:, :], [("ps", c % 2)], [("qm", c - 2, tb)])
            self.norm_fm(lambda k: cqf[:, k, :], lambda k: [("cqf", k)], 2, 512, 256.0, self.sm(f"bqnorm{j}"),
                         lambda k: cqn[:, k, c0:c0 + 512], lambda k: [("cqn", k, tb)])
        self.mem_attn(l, qmT, lambda m2, tb: [("qm", m2, tb)],
                      lambda m2, tb: mixT[:, 6 + m2, tb * 512:(tb + 1) * 512])
        P.barrier()
        self.aoff = mark
        wuq = self.carve([128, 2, 1536], BF16)
        wukT = self.carve([128, H, 256], BF16)
        wuv = self.carve([128, 2, 768], BF16)
        visR = Rot("vis", [self.carve([128, 256], F32) for _ in range(2)])
        P.dma("pool", wuq, self.wuq_in[j].rearrange("(k p) c -> p k c", p=128), "aw_uq", [], [("wuq",)])
        P.dma("pool", wukT, self.wukT_in.rearrange("h p c -> p h c"), "aw_ukT", [], [("wukT",)])
        P.dma("pool", wuv, self.wuv_in.rearrange("(k p) c -> p k c", p=128), "aw_uv", [], [("wuv",)])
        QaR = Rot("Qa", [self.carve([128, 3, 768], BF16) for _ in range(2)])
        for qa, qk_ in [QaR.next() for _ in range(2)]:
            P.memset("dve", qa[64:128, 2, :], 0.0, [qk_])
        qn = Rot("qn", [self.carve([128, 128], BF16) for _ in range(2)])
        CCq = self.carve([128, 128], F32)
        SSq = self.carve([128, 128], F32)
        r1 = self.carve([128, 128], F32)
        r2 = self.carve([128, 128], F32)
        kTR = Rot("kT4", [self.carve([128, 3, 512], BF16) for _ in range(2)])
        c4R = Rot("c4", [self.carve([128, 4, 256], BF16) for _ in range(2)])
        ptR = Rot("pt", [self.carve([128, 768], BF16) for _ in range(3)])
        rl = self.carve([128, H], F32)
        olnR = Rot("oln", [self.carve([128, 256], BF16) for _ in range(2)])
        oltR = Rot("olt", [self.carve([128, 2, 128], BF16) for _ in range(2)])
        sT = [self.ps2[0], self.ps2[1]]
        lsum = self.ps[7][:, 0:H]
        kall = self.kT_all.ap().rearrange("(r c p) t -> r p c t", r=NCORES, c=3)
        call = self.c_all.ap().rearrange("(r j n p) c -> r j p n c", r=NCORES, j=4, n=4)
        bctr = 0
        for i in range(16):
            c0 = i * 128
            qa, qak = QaR.next()
            visb, visk = visR.next()
            P.dma("sp", visb, self.visb_in[:, i * 256:(i + 1) * 256], f"vis{visk[1]}", [], [visk])
            self.rope_load(CCq, SSq, c0, 128, ("ropeq",))
            ps7 = self.ps[7]
            for h in range(H):
                qnt, qnk = qn.next()
                for k2 in range(2):
                    P.mm(ps7[:, 0:128], wuq[:, k2, h * 256:h * 256 + 128], cqn[:, k2, c0:c0 + 128],
                         k2 == 0, k2 == 1, [("wuq",)] + [("cqn", k2, i // 4)], [("ps7", 0)])
                P.copy("act", qnt, ps7[:, 0:128], [("ps7", 0)], [qnk])
                for rr in range(2):
                    for k2 in range(2):
                        P.mm(ps7[0:64, 128 + rr * 128:256 + rr * 128],
                             wuq[:, k2, h * 256 + 128 + rr * 64:h * 256 + 192 + rr * 64],
                             cqn[:, k2, c0:c0 + 128], k2 == 0, k2 == 1,
                             [("wuq",)] + [("cqn", k2, i // 4)], [("ps7", 1 + rr)])
                P.tt("dve", r1[0:64, :], ps7[0:64, 128:256], CCq[0:64, :], ALU.mult, [("ps7", 1), ("ropeq",)], [("r1",)])
                P.stt("dve", r2[0:64, :], ps7[0:64, 256:384], SCALE_B, SSq[0:64, :], ALU.mult, ALU.mult,
                      [("ps7", 2), ("ropeq",)], [("r2",)])
                P.stt("dve", qa[0:64, 2, h * 128:(h + 1) * 128], r1[0:64, :], SCALE_B, r2[0:64, :],
                      ALU.mult, ALU.add, [("r1",), ("r2",)], [qak])
                for m in range(2):
                    P.mm(ps7[:, 384:512], wukT[:, h, m * 128:(m + 1) * 128], qnt, True, True,
                         [("wukT",), qnk], [("ps7", 3)])
                    P.ts("dve", qa[:, m, h * 128:(h + 1) * 128], ps7[:, 384:512], SCALE_B, None, ALU.mult, None,
                         [("ps7", 3)], [qak])
            nkb = 32 * (i // 4 + 1)
            ngr = (nkb + 3) // 4
            for gq in range(ngr):
                kT4, kTk = kTR.next()
                c4, c4k = c4R.next()
                r, jb = gq % NCORES, gq // NCORES
                P.dma("sp", kT4, kall[r, :, :, jb * 512:(jb + 1) * 512], f"kT{kTk[1]}", [("kT_all",)], [kTk])
                P.dma("sp", c4, call[r, jb], f"c4{c4k[1]}", [("c_all",)], [c4k])
                for kk in range(4):
                    kb = gq * 4 + kk
                    if kb >= nkb:
                        break
                    sb = bctr % 2
                    bctr += 1
                    st = sT[sb]
                    for (a0, a1) in ((0, 512), (512, 768)):
                        for c in range(3):
                            rows = slice(0, 128) if c < 2 else slice(0, 64)
                            P.mm(st[:, a0:a1], kT4[rows, c, kk * 128:(kk + 1) * 128], qa[rows, c, a0:a1],
                                 c == 0, c == 2, [kTk, qak], [("sT", sb, a0)])
                    pt, ptk = ptR.next()
                    for qc in range(2):
                        vcol = kb * 2 + qc
                        P.act(pt.rearrange("p (h q) -> p h q", h=H)[:, :, qc * 64:(qc + 1) * 64],
                              st[:, 0:768].rearrange("p (h q) -> p h q", h=H)[:, :, qc * 64:(qc + 1) * 64],
                              AF.Exp, [("sT", sb, 0), ("sT", sb, 512), visk], [ptk],
                              bias=visb[:, vcol:vcol + 1])
                    first, last = kb == 0, kb == nkb - 1
                    for h in range(H):
                        acc = self.ps[4 + h // 2][:, (h % 2) * 256:(h % 2) * 256 + 256]
                        P.mm(acc, pt[:, h * 128:(h + 1) * 128], c4[:, kk, :], first and h % 2 == 0, last,
                             [ptk, c4k], [("olat", h)], skip=True)
                        P.mm(lsum[:, h:h + 1], pt[:, h * 128:(h + 1) * 128], self.ones_bf[:, 0:1],
                             first and h == 0, last, [ptk, ("ones",)], [("lsum",)], skip=True)
            P.op("dve", lambda e, a=rl, b=lsum: e.reciprocal(a, b[:, 0:H]), [("lsum",)], [("rl",)])
            for h in range(H):
                acc = self.ps[4 + h // 2][:, (h % 2) * 256:(h % 2) * 256 + 256]
                oln, olnk = olnR.next()
                P.ts("dve", oln, acc, rl[:, h:h + 1], None, ALU.mult, None, [("olat", h), ("rl",)], [olnk])
                olt, oltk = oltR.next()
                for m in range(2):
                    P.mm(ps7[:, 128 + m * 128:256 + m * 128], oln[:, m * 128:(m + 1) * 128], self.ident_bf[:, :],
                         True, True, [olnk, ("identb",)], [("ps7", 1 + m)])
                    P.copy("act", olt[:, m, :], ps7[:, 128 + m * 128:256 + m * 128], [("ps7", 1 + m)], [oltk])
                for m in range(2):
                    P.mm(ps7[:, 384:512], wuv[:, m, h * 128:(h + 1) * 128], olt[:, m, :], m == 0, m == 1,
                         [("wuv",), oltk], [("ps7", 3)])
                P.copy("act", mixT[:, h, c0:c0 + 128], ps7[:, 384:512], [("ps7", 3)], [("mixq", h, i)])
        P.barrier()
        self.aoff = mark
        self.outproj(l, lambda c, tb: mixT[:, c, tb * 512:(tb + 1) * 512],
                     lambda c, tb: ([("mix", c, tb)] if c >= H else [("mixq", c, 4 * tb + q_) for q_ in range(4)]))

    def final_out(self):
        P = self.P
        self.arena_reset()
        xo = Rot("xo", [self.carve([128, 512], F32) for _ in range(8)])
        for tb in range(NTB):
            c0 = tb * 512
            outs = {}

            def dstf(k):
                t, kk = xo.next()
                outs[k] = (t, kk)
                return t
            self.norm_fm(lambda k: self.xT[:, k, c0:c0 + 512], lambda k: [("x", k, tb)], KD, 512, D,
                         self.sm("final_norm"), dstf, lambda k: [outs[k][1]])
            for k in range(KD):
                P.dma("sp", self.outT[k * 128:(k + 1) * 128, c0:c0 + 512], outs[k][0], f"out{k % 4}",
                      [outs[k][1]], [("out", k, tb)])
        P.wait_all("sp", [("out", k, tb) for k in range(KD) for tb in range(NTB)] + getattr(self, "dbg_keys", []))

    def dump_stage(self, name):
        if not self.dbg:
            return
        t = self.nc.dram_tensor("dbg_" + name, [D, T], F32, kind="ExternalOutput").ap()
        for k in range(KD):
            self.P.dma("sp", t[k * 128:(k + 1) * 128, :], self.xT[:, k, :], f"dbg{k % 4}",
                       [("x", k, tb) for tb in range(NTB)], [("dbgout", name, k)])
        self.dbg_keys = getattr(self, "dbg_keys", []) + [("dbgout", name, k) for k in range(KD)]

    def dump_x(self):
        P = self.P
        for k in range(KD):
            P.dma("sp", self.outT[k * 128:(k + 1) * 128, :], self.xT[:, k, :], f"out{k % 4}",
                  [("x", k, tb) for tb in range(NTB)], [("out", k)])
        P.wait_all("sp", [("out", k) for k in range(KD)])

    def build(self):
        P = self.P
        sa = self.stop_after
        P.dma("sp", self.small[:, :], self.small_in[:, :], "small", [], [("small",)])
        for k in range(KD):
            P.dma("sp", self.xT[:, k, :], self.xT_in[k * 128:(k + 1) * 128, :], f"xin{k}", [],
                  [("x", k, tb) for tb in range(NTB)])
        P.memset("dve", self.ones_bf[:, :], 1.0, [("ones",)])
        P.memset("dve", self.ones_f[:, :], 1.0, [("onesf",)])
        P.copy("dve", self.ident_bf[:, :], self.sm("ident"), [("small",)], [("identb",)])
        for l in range(DEPTH):
            for part in ("f1", "mx", "f2"):
                self.w_cast(f"{part}_{l}")
        self.w_gather("f1_0")
        self.w_gather("mx_0")
        self.aoff = 0
        memf = self.carve([128, KD, 256], F32)
        P.dma("sp", memf, self.memT_in.rearrange("(k p) m -> p k m", p=128), "memin", [], [("memf",)])
        self.norm_fm(lambda k: memf[:, k, :], lambda k: [("memf",)], KD, 256, D, self.sm("mem_norm"),
                     lambda k: self.memn[:, k, :], lambda k: [("memn",)])
        done = False
        for l in range(DEPTH):
            self.ffn(l, 1)
            self.dump_stage(f"x_ffn1_{l}")
            if sa == ("ffn1", l):
                done = True
                break
            if l < 2:
                self.a_mixer(l)
            else:
                if l == 2:
                    pass
                self.b_mixer(l - 2)
            self.dump_stage(f"x_mix_{l}")
            if sa == ("mix", l):
                done = True
                break
            self.ffn(l, 2)
            self.dump_stage(f"x_l_{l}")
            if sa == ("ffn2", l):
                done = True
                break
            if l == 1:
                self.kv_build()
        if done:
            self.arena_reset()
            self.dump_x()
        else:
            self.final_out()
        P.emit()
        return self.nc


def _build(small_off, n_small, stop_after=None, dbg=False):
    b = Builder(small_off, n_small, stop_after, dbg)
    nc = b.build()
    return nc, b


def kernel(_stop_after=None, _dbg=False, **inp):
    sp0, _ = pack_small(inp, 0)
    nc, b = _build(sp0.off, sp0.n, _stop_after, _dbg)
    in_maps = []
    for c in range(NCORES):
        m = host_inputs(inp, c)
        in_maps.append(m)
    res = run_bass_kernel_spmd(nc, in_maps, core_ids=list(range(NCORES)))
    out = np.empty((S, D), np.float32)
    for c in range(NCORES):
        out[_tok_idx(c), :] = np.asarray(res.results[c]["outT"]).T
    out = np.ascontiguousarray(out.reshape(1, S, D))
    if _dbg:
        return out, res.results
    return out
```

```python
import numpy as np
import concourse.bass as bass
import concourse.mybir as mybir
from concourse.bass_utils import run_bass_kernel_spmd

F32 = mybir.dt.float32
BF16 = mybir.dt.bfloat16
AF = mybir.ActivationFunctionType
ALU = mybir.AluOpType

NCORES = 8
S = 16384
D = 1024
T = S // NCORES
NTB = T // 512
KD = D // 128
DEPTH = 4
DFF = 2816
EPS = 1e-6
ENGS = ("pe", "act", "dve", "pool", "sp")


def _bank_of(key):
    n = key[0]
    if n == "ps":
        return key[1]
    if n == "ps1":
        return 1
    if n == "pst":
        return 2
    if n in ("psg", "pss"):
        return 3
    if n == "ps4":
        return 4
    if n == "ps5":
        return 5
    if n == "ps6":
        return 6
    if n in ("ps7", "lsum"):
        return 7
    if n == "olat":
        return 4 + key[1] // 2
    if n == "sT":
        return 2 * key[1] + (0 if key[2] == 0 else 1)
    return None


class Op:
    __slots__ = ("eng", "fn", "idx", "signal", "sigval", "dma_sem", "dma_val", "waits")


class Prog:
    def __init__(self, nc):
        self.nc = nc
        self.ops = {e: [] for e in ENGS}
        self.last_w = {}
        self.readers = {}
        self.waited = {e: {x: -1 for x in ENGS} for e in ENGS}
        self.waited_dma = {e: {} for e in ENGS}
        self.last_touch = {}
        self.dma_cnt = {}
        self.dma_inc = {}
        self.dma_last = {}

    def capture_begin(self):
        self._cap = []

    def capture_end(self):
        c, self._cap = self._cap, None
        return c

    def replay(self, a, b):
        i = j = 0
        while i < len(a) or j < len(b):
            if j >= len(b) or (i < len(a) and i * len(b) <= j * len(a)):
                self.op(*a[i])
                i += 1
            else:
                self.op(*b[j])
                j += 1

    def op(self, eng, fn, reads=(), writes=(), dma=None, sync_same=True):
        if getattr(self, "_cap", None) is not None:
            self._cap.append((eng, fn, list(reads), list(writes), dma, sync_same))
            return None
        o = Op()
        o.eng, o.fn, o.signal, o.sigval = eng, fn, False, 0
        o.idx = len(self.ops[eng])
        o.dma_sem, o.dma_val = None, 0
        if dma is not None:
            inc = 16
            if isinstance(dma, tuple):
                dma, inc = dma
            self.dma_inc[dma] = inc
            self.dma_cnt[dma] = self.dma_cnt.get(dma, 0) + 1
            o.dma_sem, o.dma_val = dma, inc * self.dma_cnt[dma]
            self.dma_last[dma] = o
        deps = []
        for k in reads:
            w = self.last_w.get(k)
            if w is not None:
                deps.append(w)
        for k in writes:
            w = self.last_w.get(k)
            if w is not None:
                deps.append(w)
            deps.extend(self.readers.get(k, ()))
        banks = set()
        for k in list(reads) + list(writes):
            b_ = _bank_of(k)
            if b_ is not None:
                banks.add(b_)
        for b_ in banks:
            t = self.last_touch.get(b_)
            if t is not None and t.eng != eng:
                deps.append(t)
            self.last_touch[b_] = o
        best = {}
        bestd = {}
        for d in deps:
            if d is o:
                continue
            if d.dma_sem is not None:
                if d.dma_val > self.waited_dma[eng].get(d.dma_sem, 0):
                    if d.dma_sem not in bestd or bestd[d.dma_sem].dma_val < d.dma_val:
                        bestd[d.dma_sem] = d
            else:
                if d.eng == eng and not sync_same:
                    continue
                if d.idx > self.waited[eng][d.eng]:
                    if d.eng not in best or best[d.eng].idx < d.idx:
                        best[d.eng] = d
        o.waits = list(best.values()) + list(bestd.values())
        for d in best.values():
            d.signal = True
            self.waited[eng][d.eng] = d.idx
        for d in bestd.values():
            self.waited_dma[eng][d.dma_sem] = d.dma_val
        for k in reads:
            self.readers.setdefault(k, []).append(o)
        for k in writes:
            self.last_w[k] = o
            self.readers[k] = []
        self.ops[eng].append(o)
        return o

    def mm(self, out, lhsT, rhs, start, stop, r, w, skip=False):
        if skip:
            return self.op("pe", lambda e: e.matmul(out, lhsT, rhs, start=start, stop=stop,
                                                    skip_group_check=True), r, w, sync_same=False)
        return self.op("pe", lambda e: e.matmul(out, lhsT, rhs, start=start, stop=stop),
                       r, w, sync_same=False)

    def act(self, out, in_, func, r, w, bias=None, scale=1.0, accum_out=None):
        kw = {}
        if bias is not None:
            kw["bias"] = bias
        if accum_out is not None:
            kw["accum_out"] = accum_out
        return self.op("act", lambda e: e.activation(out, in_, func, scale=scale, **kw), r, w)

    def tt(self, eng, out, in0, in1, op, r, w):
        return self.op(eng, lambda e: e.tensor_tensor(out, in0, in1, op), r, w)

    def ts(self, eng, out, in0, s1, s2, op0, op1, r, w, accum_out=None):
        if op1 is None:
            return self.op(eng, lambda e: e.tensor_scalar(out, in0, s1, None, op0), r, w)
        if accum_out is not None:
            return self.op(eng, lambda e: e.tensor_scalar(out, in0, s1, s2, op0, op1, accum_out), r, w)
        return self.op(eng, lambda e: e.tensor_scalar(out, in0, s1, s2, op0, op1), r, w)

    def stt(self, eng, out, in0, scalar, in1, op0, op1, r, w):
        return self.op(eng, lambda e: e.scalar_tensor_tensor(out, in0, scalar, in1, op0, op1), r, w)

    def copy(self, eng, out, in_, r, w):
        if eng == "act":
            return self.op(eng, lambda e: e.copy(out, in_), r, w)
        return self.op(eng, lambda e: e.tensor_copy(out, in_), r, w)

    def memset(self, eng, ap, val, w):
        return self.op(eng, lambda e: e.memset(ap, val), (), w)

    def dma(self, q, out, in_, sem, r, w):
        return self.op(q, lambda e: e.dma_start(out=out, in_=in_), r, w, dma=sem)

    def wait_all(self, eng, keys):
        return self.op(eng, None, keys, ())

    def cc(self, kind, ins, outs, sem, r, w):
        return self.op("pool", lambda e: e.collective_compute(
            kind, ALU.bypass, replica_groups=[list(range(NCORES))], ins=ins, outs=outs),
            r, w, dma=(sem, 1))

    def barrier(self):
        keys = []
        for e in ENGS:
            last = None
            for o in reversed(self.ops[e]):
                if o.dma_sem is None and o.fn is not None:
                    last = o
                    break
            if last is not None:
                self.last_w[("__bar__", e)] = last
                self.readers[("__bar__", e)] = []
                keys.append(("__bar__", e))
        for n, o in self.dma_last.items():
            self.last_w[("__bard__", n)] = o
            self.readers[("__bard__", n)] = []
            keys.append(("__bard__", n))
        for e in ENGS:
            self.op(e, None, keys, ())

    def emit(self):
        nc = self.nc
        for e in ENGS:
            c = 0
            for o in self.ops[e]:
                if o.signal:
                    c += 1
                o.sigval = c
        import contextlib
        with contextlib.ExitStack() as st:
            esem = {e: st.enter_context(nc.semaphore("s_" + e)) for e in ENGS}
            dsem = {n: st.enter_context(nc.semaphore("d_" + n)) for n in self.dma_cnt}
            block = st.enter_context(nc.Block())

            def run(ename, eng):
                for o in self.ops[ename]:
                    for d in o.waits:
                        if d.dma_sem is not None:
                            eng.wait_ge(dsem[d.dma_sem], d.dma_val)
                        else:
                            eng.wait_ge(esem[d.eng], d.sigval)
                    if o.fn is None:
                        continue
                    ins = o.fn(eng)
                    if o.dma_sem is not None:
                        ins.then_inc(dsem[o.dma_sem], self.dma_inc[o.dma_sem])
                    elif o.signal:
                        ins.then_inc(esem[ename], 1)

            @block.tensor
            def _(eng):
                run("pe", eng)

            @block.scalar
            def _(eng):
                run("act", eng)

            @block.vector
            def _(eng):
                run("dve", eng)

            @block.gpsimd
            def _(eng):
                run("pool", eng)

            @block.sync
            def _(eng):
                run("sp", eng)


H = 6
NEG = -30000.0
SCALE_B = 192.0 ** -0.5
TWO_PI = 6.283185307179586


def _col(v, k):
    return np.ascontiguousarray(np.asarray(v, np.float32).reshape(k, 128).T)


def _bc(v):
    v = np.atleast_1d(np.asarray(v, np.float32))
    return np.ascontiguousarray(np.broadcast_to(v[None, :], (128, v.shape[0])))


class SmallPack:
    def __init__(self):
        self.cols, self.off, self.n = [], {}, 0

    def add(self, name, arr):
        arr = np.asarray(arr, np.float32)
        assert arr.shape[0] == 128, (name, arr.shape)
        self.off[name] = (self.n, arr.shape[1])
        self.cols.append(arr)
        self.n += arr.shape[1]

    def build(self):
        return np.ascontiguousarray(np.concatenate(self.cols, axis=1))


def pack_small(inp, c):
    sp = SmallPack()
    for l in range(DEPTH):
        sp.add(f"ffn1_norm{l}", _col(inp["ffn1_norm"][l], 8))
        sp.add(f"mix_norm{l}", _col(inp["mix_norm"][l], 8))
        sp.add(f"ffn2_norm{l}", _col(inp["ffn2_norm"][l], 8))
    sp.add("kv_in_norm", _col(inp["kv_in_norm"], 8))
    sp.add("final_norm", _col(inp["final_norm"], 8))
    sp.add("mem_norm", _col(inp["mem_norm"], 8))
    h = c % H
    for i in range(2):
        cw = np.asarray(inp["a_conv"][i], np.float32)
        cols = []
        for cc in range(3):
            for j in range(4):
                cols.append(cw[j, cc * 768 + h * 128: cc * 768 + (h + 1) * 128])
        sp.add(f"conv{i}", np.stack(cols, axis=1))
        sp.add(f"alog{i}", _bc(inp["a_A_log"][i][h]))
        sp.add(f"dtb{i}", _bc(inp["a_dt_bias"][i][h]))
        sp.add(f"ognorm{i}", _bc(inp["a_out_norm"][i]))
    for j in range(2):
        sp.add(f"bqnorm{j}", _col(inp["b_q_norm"][j], 2))
    sp.add("kvlat", _col(inp["kv_lat_norm"], 2))
    inv = 10000.0 ** (-np.arange(0, 64, 2, dtype=np.float32) / 64.0)
    invf = np.zeros((128, 1), np.float32)
    invf[0:32, 0] = inv
    invf[32:64, 0] = inv
    sp.add("invf", invf)
    sgn = np.zeros((128, 1), np.float32)
    sgn[0:32] = -1.0
    sgn[32:64] = 1.0
    sp.add("sgn", sgn)
    oh = np.zeros((128, 8), np.float32)
    oh[:, c] = 1.0
    sp.add("onehot", oh)
    p = np.arange(128)
    same = (p[:, None] // 64) == (p[None, :] // 64)
    sp.add("ident", np.eye(128, dtype=np.float32))
    sp.add("ltri", (same & (p[:, None] <= p[None, :])).astype(np.float32))
    sp.add("bones", same.astype(np.float32))
    low = same & (p[:, None] >= p[None, :])
    sp.add("mneg", np.where(low, 0.0, NEG).astype(np.float32))
    sp.add("mnegT", np.where(low.T, 0.0, NEG).astype(np.float32))
    sp.add("sneg", np.where(same & (p[:, None] > p[None, :]), -1.0, 0.0).astype(np.float32))
    cind = np.zeros((128, 2), np.float32)
    cind[0:64, 0] = 1.0
    cind[64:128, 1] = 1.0
    sp.add("cind", cind)
    vis = np.zeros((128, 16, 128, 2), np.float32)
    for i in range(16):
        gi = (NCORES * (i // 4) + c) * 4 + i % 4
        kb = np.arange(128)
        for qc in range(2):
            qchunk = 2 * gi + qc
            vis[0:64, i, :, qc] = np.where(2 * kb <= qchunk, 0.0, NEG)[None, :]
            vis[64:128, i, :, qc] = np.where(2 * kb + 1 <= qchunk, 0.0, NEG)[None, :]
    return sp, np.ascontiguousarray(vis.reshape(128, -1))


def _swap_rope(w):
    return np.concatenate([w[..., 32:64], w[..., 0:32]], axis=-1)


def host_inputs(inp, c):
    f = lambda a: np.ascontiguousarray(np.asarray(a, np.float32))
    m = {}
    x = np.asarray(inp["x"], np.float32)[0]
    m["xT_in"] = f(x[_tok_idx(c), :].T)
    m["memT"] = f(np.asarray(inp["mem"], np.float32)[0].T)
    pos = np.asarray(inp["positions"])[0, _tok_idx(c)].astype(np.int32)
    m["pos"] = np.ascontiguousarray(np.broadcast_to(pos[None, :], (128, T)))
    for l in range(DEPTH):
        for which in (1, 2):
            gu = np.asarray(inp[f"ffn{which}_w_gu"][l], np.float32)[c * 128:(c + 1) * 128, :]
            dn = np.zeros((3072, D), np.float32)
            dn[0:DFF] = np.asarray(inp[f"ffn{which}_w_down"][l], np.float32)
            dn = dn[c * 384:(c + 1) * 384]
            m[f"wp_f{which}_{l}"] = f(np.concatenate([gu.ravel(), dn.ravel()]).reshape(128, -1))
        parts = [np.asarray(inp["w_out"][l], np.float32)[c * 128:(c + 1) * 128].ravel(),
                 np.asarray(inp["w_mem_kv"][l], np.float32)[c * 128:(c + 1) * 128].ravel()]
        if l >= 2:
            parts.append(np.asarray(inp["b_w_in"][l - 2], np.float32)[c * 128:(c + 1) * 128].ravel())
        m[f"wp_mx_{l}"] = f(np.concatenate(parts).reshape(128, -1))
    h = c % H
    aw = []
    for i in range(2):
        w = np.asarray(inp["a_w_in"][i], np.float32)
        aw.append(np.concatenate([
            w[:, h * 128:(h + 1) * 128], w[:, 768 + h * 128:768 + (h + 1) * 128],
            w[:, 1536 + h * 128:1536 + (h + 1) * 128], w[:, 2304 + h * 128:2304 + (h + 1) * 128],
            w[:, 3072 + h:3073 + h], w[:, 3078 + h:3079 + h], w[:, 3084:3340]], axis=1))
    m["awin"] = f(np.stack(aw))
    wq = []
    for j in range(2):
        w = np.asarray(inp["b_w_uq"][j], np.float32)
        cols = []
        for hh in range(H):
            nope = w[:, hh * 192:hh * 192 + 128]
            rope = w[:, hh * 192 + 128:hh * 192 + 192]
            cols += [nope, rope, _swap_rope(rope)]
        wq.append(np.concatenate(cols, axis=1))
    m["wuq"] = f(np.stack(wq))
    wd = np.asarray(inp["w_dkv"], np.float32)
    m["wdkv"] = f(np.concatenate([wd[:, 0:256], wd[:, 256:320], _swap_rope(wd[:, 256:320])], axis=1))
    wk = np.asarray(inp["w_ukv"], np.float32)
    m["wukT"] = f(np.stack([wk[:, hh * 256:hh * 256 + 128].T for hh in range(H)]))
    m["wuv"] = f(np.concatenate([wk[:, hh * 256 + 128:hh * 256 + 256] for hh in range(H)], axis=1))
    sp, vis = pack_small(inp, c)
    m["small"] = sp.build()
    m["visb"] = vis
    return m


def _tok_idx(c):
    return ((np.arange(4)[:, None] * NCORES + c) * 512 + np.arange(512)[None, :]).reshape(-1)


class Rot:
    def __init__(self, name, tiles):
        self.name, self.tiles, self.i = name, tiles, 0

    def next(self):
        j = self.i % len(self.tiles)
        self.i += 1
        return self.tiles[j], (self.name, j)


NF1 = 128 * 5632 + 384 * D
OFF_DN = 128 * 5632


class Builder:
    def __init__(self, small_off, n_small, stop_after=None, dbg=False):
        self.stop_after = stop_after
        nc = bass.Bass("TRN2", target_bir_lowering=False)
        self.nc = nc
        self.P = Prog(nc)
        self.small_off = small_off
        dt = nc.dram_tensor
        ext = lambda name, shape, d=F32: dt(name, shape, d, kind="ExternalInput").ap()
        self.xT_in = ext("xT_in", [D, T])
        self.memT_in = ext("memT", [D, 256])
        self.pos_in = ext("pos", [128, T], mybir.dt.int32)
        self.small_in = ext("small", [128, n_small])
        self.visb_in = ext("visb", [128, 4096])
        self.awin = ext("awin", [2, D, 770])
        self.wuq_in = ext("wuq", [2, 256, 1536])
        self.wdkv_in = ext("wdkv", [D, 384])
        self.wukT_in = ext("wukT", [H, 128, 256])
        self.wuv_in = ext("wuv", [256, 768])
        self.wp = {}
        self.wfull = {}
        self.wsb = {}
        self.gsize = {}
        for l in range(DEPTH):
            for part in ("f1", "mx", "f2"):
                n = NF1 if part != "mx" else (128 * 1024 + 128 * 512 + (128 * 512 if l >= 2 else 0))
                g = f"{part}_{l}"
                self.gsize[g] = n
                self.wp[g] = ext(f"wp_{g}", [128, n // 128])
                self.wsb[g] = dt(f"wsb_{g}", [128, n // 128], BF16)
                self.wfull[g] = dt(f"wfull_{g}", [1024, n // 128], BF16)
        self.outT = dt("outT", [D, T], F32, kind="ExternalOutput").ap()
        self.xn_loc = dt("xn_loc", [D, T], BF16)
        self.xn_all = dt("xn_all", [NCORES * D, T], BF16)
        self.mix_loc = dt("mix_loc", [128, S], BF16)
        self.mix_all = dt("mix_all", [NCORES * 128, S], BF16)
        self.kT_loc = dt("kT_loc", [384, T], BF16)
        self.kT_all = dt("kT_all", [NCORES * 384, T], BF16)
        self.c_loc = dt("c_loc", [T, 256], BF16)
        self.c_all = dt("c_all", [S, 256], BF16)

        A = nc.alloc_sbuf_tensor
        self.xT = A("xT", [128, KD, T], F32)
        self.xn = A("xn", [128, KD, T], BF16)
        self.small = A("small_sb", [128, n_small], F32)
        self.ones_bf = A("ones_bf", [128, 128], BF16)
        self.ones_f = A("ones_f", [128, 128], F32)
        self.ident_bf = A("ident_bf", [128, 128], BF16)
        self.sq = A("sq", [128, KD, 512], BF16)
        self.rstd = A("rstd", [128, 512], F32)
        self.memn = A("memn", [128, KD, 256], BF16)
        self.ARENA = 88 * 1024
        self.arena = A("arena", [128, self.ARENA // 4], F32)
        self.aoff = 0
        self.ps2 = [nc.alloc_psum_tensor(f"pp{i}", [128, 1024], F32) for i in range(4)]
        self.ps = [self.ps2[i // 2][:, (i % 2) * 512:(i % 2 + 1) * 512] for i in range(8)]
        self.ffn_groups = [(g * 512, 4) for g in range(5)] + [(2560, 2)]
        self.dbg_outs = {}
        self.dbg = dbg

    def sm(self, name):
        o, w = self.small_off[name]
        return self.small[:, o:o + w]

    def arena_reset(self):
        self.P.barrier()
        self.aoff = 0

    def carve(self, shape, dtype):
        esz = 2 if dtype == BF16 else 4
        n = 1
        for s_ in shape[1:]:
            n *= s_
        nbytes = (n * esz + 31) // 32 * 32
        assert self.aoff + nbytes <= self.ARENA, (self.aoff, nbytes)
        a = self.arena[:, self.aoff // 4:(self.aoff + nbytes) // 4]
        self.aoff += nbytes
        if dtype != F32:
            a = a.bitcast(dtype)
        a = a[:, 0:n]
        if len(shape) == 3:
            a = a.rearrange("p (a b) -> p a b", a=shape[1])
        elif len(shape) == 4:
            a = a.rearrange("p (a b c) -> p a b c", a=shape[1], b=shape[2])
        return a

    def dump(self, name, ap, shape, keys, dtype=F32):
        if not self.dbg:
            return
        t = self.nc.dram_tensor("dbg_" + name, shape, dtype, kind="ExternalOutput").ap()
        self.P.dma("sp", t, ap, "dbg", keys, [("dbgout", name)])
        self.dbg_outs[name] = t

    def wview(self, g):
        n = self.gsize[g]
        return self.wfull[g].ap().rearrange("(r a) b -> r (a b)", a=128)

    def w_cast(self, g):
        n = self.gsize[g] // 128
        step = (n + 3) // 4
        for a in range(0, n, step):
            b = min(n, a + step)
            self.P.dma("pool", self.wsb[g][:, a:b], self.wp[g][:, a:b], "wc_" + g, [], [("wsb", g, a)])
        self.wsb_keys = getattr(self, "wsb_keys", {})
        self.wsb_keys[g] = [("wsb", g, a) for a in range(0, n, step)]

    def w_gather(self, g):
        self.P.cc("AllGather", [self.wsb[g].ap().opt()], [self.wfull[g].ap().opt()], "wag",
                  self.wsb_keys[g], [("wfull", g)])

    def norm_fm(self, src, skeys, nk, ncol, dn, gain, dst, dkeys, psb=6):
        P = self.P
        psn = self.ps[psb]
        for k in range(nk):
            P.act(self.sq[:, k, 0:ncol], src(k), AF.Square, skeys(k), [("sq", k)])
        for k in range(nk):
            P.mm(psn[:, 0:ncol], self.ones_bf[:, :], self.sq[:, k, 0:ncol], k == 0, k == nk - 1,
                 [("sq", k), ("ones",)], [("ps", psb)])
        P.ts("dve", self.rstd[:, 0:ncol], psn[:, 0:ncol], 1.0 / dn, EPS, ALU.mult, ALU.add,
             [("ps", psb)], [("rstd",)])
        P.act(self.rstd[:, 0:ncol], self.rstd[:, 0:ncol], AF.Ln, [("rstd",)], [("rstd",)])
        P.act(self.rstd[:, 0:ncol], self.rstd[:, 0:ncol], AF.Exp, [("rstd",)], [("rstd",)], scale=-0.5)
        for k in range(nk):
            P.stt("dve", dst(k), src(k), gain[:, k:k + 1], self.rstd[:, 0:ncol],
                  ALU.mult, ALU.mult, skeys(k) + [("rstd",), ("small",)], dkeys(k))

    def norm_x(self, gname):
        for tb in range(NTB):
            c0 = tb * 512
            self.norm_fm(lambda k: self.xT[:, k, c0:c0 + 512], lambda k: [("x", k, tb)], KD, 512, D,
                         self.sm(gname), lambda k: self.xn[:, k, c0:c0 + 512],
                         lambda k: [("xn", k, tb)])

    def ffn_alloc(self):
        self.NSLOT = 2
        self.wg = [self.carve([128, KD, 512], BF16) for _ in range(self.NSLOT)]
        self.wu = [self.carve([128, KD, 512], BF16) for _ in range(self.NSLOT)]
        self.wd = [self.carve([128, 4, D], BF16) for _ in range(self.NSLOT)]
        self.aT = [self.carve([128, 4, T], BF16) for _ in range(2)]
        self.sil = [self.carve([128, 512], F32) for _ in range(2)]
        self.grp_ctr = 0
        self.sil_ctr = 0

    def ffn_load(self, g, gi):
        P = self.P
        f0, nch = self.ffn_groups[gi]
        n = nch * 128
        s = (self.grp_ctr + gi) % self.NSLOT
        v = self.wview(g)
        gu = v[:, 0:OFF_DN].rearrange("r (p f) -> p r f", p=128)
        P.dma("sp", self.wg[s][:, :, 0:n], gu[:, :, f0:f0 + n], f"wg{s}", [("wfull", g)], [("wg", s)])
        P.dma("sp", self.wu[s][:, :, 0:n], gu[:, :, DFF + f0:DFF + f0 + n], f"wu{s}",
              [("wfull", g)], [("wu", s)])
        for jj in range(nch):
            j = f0 // 128 + jj
            r, jr = j // 3, j % 3
            src = v[r, OFF_DN + jr * 128 * D:OFF_DN + (jr + 1) * 128 * D].rearrange("(p d) -> p d", p=128)
            P.dma("sp", self.wd[s][:, jj, :], src, f"wd{s}_{jj}", [("wfull", g)], [("wd", s, jj)])

    def ffn_gu(self, gi):
        P = self.P
        f0, nch = self.ffn_groups[gi]
        s = (self.grp_ctr + gi) % self.NSLOT
        a = (self.grp_ctr + gi) % 2
        for tb in range(NTB):
            c0 = tb * 512
            for j in range(nch):
                pb = 2 * (self.sil_ctr % 2)
                psg, psu = self.ps[pb], self.ps[pb + 1]
                for k in range(KD):
                    P.mm(psg[:, :], self.wg[s][:, k, j * 128:(j + 1) * 128], self.xn[:, k, c0:c0 + 512],
                         k == 0, k == KD - 1, [("wg", s), ("xn", k, tb)], [("ps", pb)])
                for k in range(KD):
                    P.mm(psu[:, :], self.wu[s][:, k, j * 128:(j + 1) * 128], self.xn[:, k, c0:c0 + 512],
                         k == 0, k == KD - 1, [("wu", s), ("xn", k, tb)], [("ps", pb + 1)])
                st = self.sil[self.sil_ctr % 2]
                sk = ("sil", self.sil_ctr % 2)
                P.act(st[:, :], psg[:, :], AF.Silu, [("ps", pb)], [sk])
                P.tt("dve", self.aT[a][:, j, c0:c0 + 512], psu[:, :], st[:, :], ALU.mult,
                     [("ps", pb + 1), sk], [("aT", a, j, tb)])
                self.sil_ctr += 1

    def ffn_down(self, gi):
        P = self.P
        f0, nch = self.ffn_groups[gi]
        s = (self.grp_ctr + gi) % self.NSLOT
        a = (self.grp_ctr + gi) % 2
        for tb in range(NTB):
            c0 = tb * 512
            for i in range(KD):
                pb = 4 + (i % 2)
                psy = self.ps[pb]
                for j in range(nch):
                    P.mm(psy[:, :], self.wd[s][:, j, i * 128:(i + 1) * 128], self.aT[a][:, j, c0:c0 + 512],
                         j == 0, j == nch - 1, [("wd", s, jj) for jj in range(nch)] + [("aT", a, j, tb)],
                         [("ps", pb)])
                P.stt("dve", self.xT[:, i, c0:c0 + 512], psy[:, :], 0.5, self.xT[:, i, c0:c0 + 512],
                      ALU.mult, ALU.add, [("ps", pb), ("x", i, tb)], [("x", i, tb)])

    def ffn(self, l, which):
        self.arena_reset()
        self.ffn_alloc()
        g = f"f{which}_{l}"
        G = len(self.ffn_groups)
        self.ffn_load(g, 0)
        self.ffn_load(g, 1)
        self.norm_x(f"ffn{which}_norm{l}")
        self.ffn_gu(0)
        for gi in range(G):
            if gi + 1 < G:
                self.ffn_gu(gi + 1)
            self.ffn_down(gi)
            if gi + 2 < G:
                self.ffn_load(g, gi + 2)
        self.grp_ctr += G

    def outproj(self, l, mixfn, mkeys):
        P = self.P
        g = f"mx_{l}"
        wout = self.carve([128, KD, D], BF16)
        src = self.wview(g)[:, 0:128 * D].rearrange("r (p f) -> p r f", p=128)
        P.dma("sp", wout, src, "wout", [("wfull", g)], [("wout",)])
        for tb in range(NTB):
            c0 = tb * 512
            for i in range(KD):
                pb = 4 + (i % 2)
                psy = self.ps[pb]
                for c in range(KD):
                    P.mm(psy[:, :], wout[:, c, i * 128:(i + 1) * 128], mixfn(c, tb),
                         c == 0, c == KD - 1, [("wout",)] + mkeys(c, tb), [("ps", pb)])
                P.tt("dve", self.xT[:, i, c0:c0 + 512], psy[:, :], self.xT[:, i, c0:c0 + 512], ALU.add,
                     [("ps", pb), ("x", i, tb)], [("x", i, tb)])

    def mem_attn(self, l, qmT, qkeys, dstfn):
        P = self.P
        g = f"mx_{l}"
        wmkv = self.carve([128, KD, 512], BF16)
        src = self.wview(g)[:, 128 * D:128 * D + 128 * 512].rearrange("r (p f) -> p r f", p=128)
        P.dma("sp", wmkv, src, "wmkv", [("wfull", g)], [("wmkv",)])
        mkT = self.carve([128, 2, 256], BF16)
        vpad = self.carve([128, 4, 2, 128], BF16)
        opad = self.carve([128, 2, 128], BF16)
        P.memset("dve", vpad, 0.0, [("vpad",)])
        P.memset("dve", opad, 0.0, [("opad",)])
        for hp in range(2):
            P.memset("dve", opad[:, hp, hp * 64:(hp + 1) * 64], 1.0, [("opad",)])
        for m2 in range(2):
            ps = self.ps[0]
            for k in range(KD):
                P.mm(ps[:, 0:256], wmkv[:, k, m2 * 128:(m2 + 1) * 128], self.memn[:, k, :],
                     k == 0, k == KD - 1, [("wmkv",), ("memn",)], [("ps", 0)])
            P.ts("dve", mkT[:, m2, :], ps[:, 0:256], 0.125, None, ALU.mult, None, [("ps", 0)], [("mkT",)])
        for mb in range(2):
            ps = self.ps[1]
            for k in range(KD):
                P.mm(ps[:, 0:256], self.memn[:, k, mb * 128:(mb + 1) * 128], wmkv[:, k, 256:512],
                     k == 0, k == KD - 1, [("wmkv",), ("memn",)], [("ps", 1)])
            for h in range(4):
                P.copy("act", vpad[:, h, mb, (h % 2) * 64:(h % 2) * 64 + 64], ps[:, h * 64:(h + 1) * 64],
                       [("ps", 1)], [("vpad",)])
        ptR = Rot("mpt", [self.carve([128, 512], BF16) for _ in range(8)])
        rl = self.carve([128, 512], F32)
        sctr = 0
        for tb in range(NTB):
            c0 = tb * 512
            for m2 in range(2):
                pts = []
                for hp in range(2):
                    for mb in range(2):
                        pb = sctr % 2
                        sctr += 1
                        ps = self.ps[pb]
                        P.mm(ps[:, :], mkT[hp * 64:(hp + 1) * 64, m2, mb * 128:(mb + 1) * 128],
                             qmT[hp * 64:(hp + 1) * 64, m2, c0:c0 + 512], True, True,
                             [("mkT",)] + qkeys(m2, tb), [("ps", pb)])
                        pt, pk = ptR.next()
                        P.act(pt, ps[:, :], AF.Exp, [("ps", pb)], [pk])
                        pts.append((pt, pk, hp, mb))
                pso, psl = self.ps[2], self.ps[3]
                for n_, (pt, pk, hp, mb) in enumerate(pts):
                    P.mm(pso[:, :], vpad[:, 2 * m2 + hp, mb, :], pt, n_ == 0, n_ == 3,
                         [("vpad",), pk], [("ps", 2)])
                for n_, (pt, pk, hp, mb) in enumerate(pts):
                    P.mm(psl[:, :], opad[:, hp, :], pt, n_ == 0, n_ == 3, [("opad",), pk], [("ps", 3)])
                P.op("dve", lambda e, a=rl, b=psl: e.reciprocal(a, b[:, :]), [("ps", 3)], [("mrl",)])
                P.tt("dve", dstfn(m2, tb), pso[:, :], rl, ALU.mult,
                     [("ps", 2), ("mrl",)], [("mix", 6 + m2, tb)])

    def rope_tables(self):
        P = self.P
        self.rope_dram = self.nc.dram_tensor("rope_dram", [2, 64, T], F32)
        CCt = self.carve([128, T], F32)
        SSt = self.carve([128, T], F32)
        posi = self.carve([128, T], mybir.dt.int32)
        xf = self.carve([128, T], F32)
        ki = self.carve([128, T], mybir.dt.int32)
        kf = self.carve([128, T], F32)
        msk = self.carve([128, T], F32)
        P.dma("sp", posi, self.pos_in[:, :], "pos", [], [("posi",)])
        R = slice(0, 64)
        P.copy("dve", xf[R], posi[R], [("posi",)], [("rx",)])
        P.ts("dve", xf[R], xf[R], self.sm("invf")[R, 0:1], 1.0 / TWO_PI, ALU.mult, ALU.mult,
             [("rx",), ("small",)], [("rx",)])

        def wrap(r_key):
            P.copy("dve", ki[R], xf[R], [r_key], [("rk",)])
            P.copy("dve", kf[R], ki[R], [("rk",)], [("rkf",)])
            P.tt("dve", xf[R], xf[R], kf[R], ALU.subtract, [r_key, ("rkf",)], [r_key])
            P.ts("dve", msk[R], xf[R], 0.5, None, ALU.is_gt, None, [r_key], [("rm",)])
            P.tt("dve", xf[R], xf[R], msk[R], ALU.subtract, [r_key, ("rm",)], [r_key])
            P.ts("dve", msk[R], xf[R], -0.5, None, ALU.is_lt, None, [r_key], [("rm",)])
            P.tt("dve", xf[R], xf[R], msk[R], ALU.add, [r_key, ("rm",)], [r_key])

        wrap(("rx",))
        P.act(SSt[R], xf[R], AF.Sin, [("rx",)], [("SS",)], scale=TWO_PI * (1.0 - 1e-6))
        P.ts("dve", SSt[R], SSt[R], self.sm("sgn")[R, 0:1], None, ALU.mult, None,
             [("SS",), ("small",)], [("SS",)])
        P.ts("dve", xf[R], xf[R], 0.25, None, ALU.add, None, [("rx",)], [("rx",)])
        wrap(("rx",))
        P.act(CCt[R], xf[R], AF.Sin, [("rx",)], [("CC",)], scale=TWO_PI * (1.0 - 1e-6))
        P.dma("sp", self.rope_dram[0], CCt[R], "rope0", [("CC",)], [("rope_dram", 0)])
        P.dma("sp", self.rope_dram[1], SSt[R], "rope1", [("SS",)], [("rope_dram", 1)])

    def a_mixer(self, i):
        P = self.P
        l = i
        self.arena_reset()
        self.norm_x(f"mix_norm{l}")
        for k in range(KD):
            P.dma("sp", self.xn_loc[k * 128:(k + 1) * 128, :], self.xn[:, k, :], f"xo{k}",
                  [("xn", k, tb) for tb in range(NTB)], [("xn_loc", k)])
        P.cc("AllGather", [self.xn_loc.ap().opt()], [self.xn_all.ap().opt()], "ag",
             [("xn_loc", k) for k in range(KD)], [("xn_all",)])
        self.w_gather(f"f2_{l}")
        if l + 1 < DEPTH:
            self.w_gather(f"f1_{l + 1}")
            self.w_gather(f"mx_{l + 1}")

        mixmem = self.carve([128, 2, T], BF16)
        wqkv = self.carve([128, KD, 384], BF16)
        wgba = self.carve([128, KD, 130], BF16)
        wqm = self.carve([128, KD, 256], BF16)
        P.dma("pool", wqkv, self.awin[i, :, 0:384].rearrange("(k p) c -> p k c", p=128), "aw_qkv", [], [("wqkv",)])
        P.dma("pool", wgba, self.awin[i, :, 384:514].rearrange("(k p) c -> p k c", p=128), "aw_gba", [], [("wgba",)])
        P.dma("pool", wqm, self.awin[i, :, 514:770].rearrange("(k p) c -> p k c", p=128), "aw_qm", [], [("wqm",)])

        mark = self.aoff
        qmT = self.carve([128, 2, T], BF16)
        for tb in range(NTB):
            c0 = tb * 512
            for m2 in range(2):
                ps = self.ps[4 + m2]
                for k in range(KD):
                    P.mm(ps[:, :], wqm[:, k, m2 * 128:(m2 + 1) * 128], self.xn[:, k, c0:c0 + 512],
                         k == 0, k == KD - 1, [("wqm",), ("xn", k, tb)], [("ps", 4 + m2)])
                P.copy("act", qmT[:, m2, c0:c0 + 512], ps[:, :], [("ps", 4 + m2)], [("qm", m2, tb)])
        self.mem_attn(l, qmT, lambda m2, tb: [("qm", m2, tb)],
                      lambda m2, tb: mixmem[:, m2, tb * 512:(tb + 1) * 512])
        P.barrier()
        self.aoff = mark

        conv = self.sm(f"conv{i}")
        ident_f = self.sm("ident")
        ltri, bones = self.sm("ltri"), self.sm("bones")
        mneg, mnegT, sneg = self.sm("mneg"), self.sm("mnegT"), self.sm("sneg")
        cind = self.sm("cind")
        ogn = self.sm(f"ognorm{i}")
        cv = lambda shape, d: self.carve(shape, d)
        negA = cv([128, 1], F32)
        P.act(negA, self.sm(f"alog{i}"), AF.Exp, [("small",)], [("negA",)])
        P.ts("dve", negA, negA, -1.0, None, ALU.mult, None, [("negA",)], [("negA",)])
        S_f = cv([128, 128], F32)
        S_b = cv([128, 128], BF16)
        P.memset("dve", S_f, 0.0, [("S_f",)])
        P.memset("dve", S_b, 0.0, [("S_b",)])
        pre = [cv([128, 3, 515], BF16) for _ in range(2)]
        P.memset("dve", pre[0][:, :, 0:3], 0.0, [("pre", 0, c) for c in range(3)])
        xbR = Rot("xb", [cv([128, KD, 512], BF16) for _ in range(2)])
        acc = cv([128, 512], F32)
        sgt = cv([128, 512], F32)
        qkvT = cv([128, 3, 512], BF16)
        stageR = Rot("stg", [cv([128, 512], BF16) for _ in range(2)])
        junk = cv([128, 128], BF16)
        scR = Rot("sc", [cv([128, 32], F32) for _ in range(2)])
        dSR = Rot("dS", [cv([128, 2], F32) for _ in range(2)])
        t128 = lambda n, d, nm: Rot(nm, [cv([128, 128], d) for _ in range(n)])
        KbR, VbR, QsTR, QdTR, KnTR = (t128(2, BF16, "Kb"), t128(2, BF16, "Vb"), t128(2, BF16, "QsT"),
                                      t128(2, BF16, "QdT"), t128(2, BF16, "KnT"))
        KnR, QsR, QdR = t128(2, BF16, "Kn"), t128(2, BF16, "Qs"), t128(2, BF16, "Qd")
        Kd0R, Kd1R = t128(2, BF16, "Kd0"), t128(2, BF16, "Kd1")
        for t_, _k in [Kd0R.next() for _ in range(2)]:
            P.memset("dve", t_, 0.0, [_k])
        for t_, _k in [Kd1R.next() for _ in range(2)]:
            P.memset("dve", t_, 0.0, [_k])
        diagR, tmpR, decR, decTR = (t128(2, F32, "diag"), t128(2, F32, "tmp"), t128(2, F32, "dec"),
                                    t128(2, F32, "decT"))
        XR, XTR, PTR = t128(3, F32, "X"), t128(3, F32, "XT"), t128(3, F32, "PT")
        qkTR, PTbR, wTR, uR, vnR = (t128(2, BF16, "qkT"), t128(2, BF16, "PTb"), t128(2, BF16, "wT"),
                                    t128(2, F32, "u"), t128(2, BF16, "vn"))
        oR, sgR, yR, mtR = t128(2, F32, "o"), t128(2, F32, "sg"), t128(2, F32, "y"), t128(2, BF16, "mt")
        gsbR = t128(2, F32, "gsb")
        prev_st2 = []

        xall = self.xn_all.ap().rearrange("(r k p) t -> r p k t", r=NCORES, k=KD)
        NB = S // 512
        for b in range(NB):
            r, t0 = b % NCORES, (b // NCORES) * 512
            xb, xbk = xbR.next()
            P.dma("sp", xb, xall[r, :, :, t0:t0 + 512], f"xb{(b % 2)}", [("xn_all",)], [xbk])
            pr, prn = pre[b % 2], pre[(b + 1) % 2]
            for c in range(3):
                ps = self.ps[0]
                for k in range(KD):
                    P.mm(ps[:, :], wqkv[:, k, c * 128:(c + 1) * 128], xb[:, k, :], k == 0, k == KD - 1,
                         [("wqkv",), xbk], [("ps", 0)])
                P.copy("act", pr[:, c, 3:515], ps[:, :], [("ps", 0)], [("pre", b % 2, c)])
                pk = [("pre", b % 2, c)]
                P.ts("dve", acc, pr[:, c, 0:512], conv[:, 4 * c:4 * c + 1], None, ALU.mult, None,
                     pk + [("small",)], [("acc",)])
                for j in range(1, 4):
                    P.stt("dve", acc, pr[:, c, j:j + 512], conv[:, 4 * c + j:4 * c + j + 1], acc,
                          ALU.mult, ALU.add, pk + [("acc",), ("small",)], [("acc",)])
                P.copy("dve", prn[:, c, 0:3], pr[:, c, 512:515], pk, [("pre", (b + 1) % 2, c)])
                P.act(sgt, acc, AF.Exp, [("acc",)], [("sgt",)], scale=-1.0)
                P.ts("dve", sgt, sgt, 1.0, None, ALU.add, None, [("sgt",)], [("sgt",)])
                P.op("dve", lambda e, a=sgt: e.reciprocal(a, a), [("sgt",)], [("sgt",)])
                P.tt("dve", qkvT[:, c, :], acc, sgt, ALU.mult, [("acc",), ("sgt",)], [("qkvT", c)])
            stg, stgk = stageR.next()
            for tl in range(4):
                c0 = tl * 128
                P.capture_begin()
                sc, sck = scR.next()
                col = lambda n: sc[:, n:n + 1]
                SK = [sck]
                pst = self.ps[2]
                for c in range(3):
                    P.mm(pst[:, c * 128:(c + 1) * 128], qkvT[:, c, c0:c0 + 128], self.ident_bf[:, :],
                         True, True, [("qkvT", c), ("identb",)], [("pst", c)])
                psg = self.ps[3]
                for k in range(KD):
                    P.mm(psg[:, 0:130], xb[:, k, c0:c0 + 128], wgba[:, k, :], k == 0, k == KD - 1,
                         [xbk, ("wgba",)], [("psg",)])
                P.act(junk, pst[:, 0:128], AF.Square, [("pst", 0)], [("junk",), sck], accum_out=col(0))
                P.act(junk, pst[:, 128:256], AF.Square, [("pst", 1)], [("junk",), sck], accum_out=col(1))
                P.ts("dve", sc[:, 2:4], sc[:, 0:2], EPS, None, ALU.add, None, SK, SK)
                P.act(sc[:, 2:4], sc[:, 2:4], AF.Ln, SK, SK)
                P.act(sc[:, 2:4], sc[:, 2:4], AF.Exp, SK, SK, scale=-0.5)
                P.ts("dve", col(15), col(2), 128.0 ** -0.5, None, ALU.mult, None, SK, SK)
                P.act(col(4), psg[:, 128:129], AF.Exp, [("psg",)], SK, scale=-1.0)
                P.ts("dve", col(4), col(4), 1.0, None, ALU.add, None, SK, SK)
                P.op("dve", lambda e, a=col(4): e.reciprocal(a, a), SK, SK)
                P.ts("dve", col(5), psg[:, 129:130], self.sm(f"dtb{i}")[:, 0:1], None, ALU.add, None,
                     [("psg",), ("small",)], SK)
                P.act(col(6), col(5), AF.Abs, SK, SK)
                P.act(col(6), col(6), AF.Exp, SK, SK, scale=-1.0)
                P.act(col(6), col(6), AF.Ln, SK, SK, bias=1.0)
                P.stt("dve", col(7), col(5), 0.0, col(6), ALU.max, ALU.add, SK, SK)
                P.tt("dve", col(8), col(7), negA, ALU.mult, SK + [("negA",)], SK)
                pss = self.ps[3]
                P.mm(pss[:, 256:257], ltri, col(8), True, True, SK + [("small",)], [("pss", 0)])
                P.mm(pss[:, 257:258], bones, col(8), True, True, SK + [("small",)], [("pss", 1)])
                P.copy("dve", sc[:, 9:11], pss[:, 256:258], [("pss", 0), ("pss", 1)], SK)
                P.ts("dve", col(11), col(9), -1.0, None, ALU.mult, None, SK, SK)
                P.act(col(12), col(9), AF.Exp, SK, SK)
                P.act(col(13), col(9), AF.Exp, SK, SK, scale=-1.0, bias=col(10))
                P.tt("dve", col(14), col(4), col(12), ALU.mult, SK, SK)
                gsel = sc[:, 16:18]
                P.ts("dve", gsel, cind, col(8), None, ALU.mult, None, SK + [("small",)], SK)
                P.mm(pss[:, 258:260], self.ones_f[:, :], gsel, True, True, SK + [("onesf",)], [("pss", 2)])
                dS, dSk = dSR.next()
                P.act(dS, pss[:, 258:260], AF.Exp, [("pss", 2)], [dSk])
                Kn, Knk = KnR.next()
                Kb, Kbk = KbR.next()
                Kd0, Kd0k = Kd0R.next()
                Kd1, Kd1k = Kd1R.next()
                Qs, Qsk = QsR.next()
                Qd, Qdk = QdR.next()
                Vb, Vbk = VbR.next()
                kps, qps, vps = pst[:, 128:256], pst[:, 0:128], pst[:, 256:384]
                P.ts("dve", Kn, kps, col(3), None, ALU.mult, None, [("pst", 1)] + SK, [Knk])
                P.ts("dve", Kb, kps, col(3), col(14), ALU.mult, ALU.mult, [("pst", 1)] + SK, [Kbk])
                P.ts("dve", Kd0[0:64, :], kps[0:64, :], sc[0:64, 3:4], sc[0:64, 13:14], ALU.mult, ALU.mult,
                     [("pst", 1)] + SK, [Kd0k])
                P.ts("dve", Kd1[64:128, :], kps[64:128, :], sc[64:128, 3:4], sc[64:128, 13:14], ALU.mult,
                     ALU.mult, [("pst", 1)] + SK, [Kd1k])
                P.ts("dve", Qs, qps, col(15), None, ALU.mult, None, [("pst", 0)] + SK, [Qsk])
                P.ts("dve", Qd, qps, col(15), col(12), ALU.mult, ALU.mult, [("pst", 0)] + SK, [Qdk])
                P.ts("dve", Vb, vps, col(4), None, ALU.mult, None, [("pst", 2)] + SK, [Vbk])
                ps4 = self.ps[4]
                KnT, KnTk = KnTR.next()
                QsT, QsTk = QsTR.next()
                QdT, QdTk = QdTR.next()
                for n_, (src_, sk_, dst_, dk_) in enumerate(((Kn, Knk, KnT, KnTk), (Qs, Qsk, QsT, QsTk),
                                                              (Qd, Qdk, QdT, QdTk))):
                    P.mm(ps4[:, n_ * 128:(n_ + 1) * 128], src_, self.ident_bf[:, :], True, True,
                         [sk_, ("identb",)], [("ps4", n_)])
                    P.copy("act", dst_, ps4[:, n_ * 128:(n_ + 1) * 128], [("ps4", n_)], [dk_])
                ps5 = self.ps[5]
                P.mm(ps5[:, 0:128], KnT, KnT, True, True, [KnTk], [("ps5", 0)])
                P.mm(ps5[:, 128:256], KnT, QsT, True, True, [KnTk, QsTk], [("ps5", 1)])
                dg, dgk = diagR.next()
                P.ts("dve", dg, ident_f, col(9), None, ALU.mult, None, SK + [("small",)], [dgk])
                P.mm(ps5[:, 256:384], self.ones_f[:, :], dg, True, True, [dgk, ("onesf",)], [("ps5", 2)])
                tmp, tmpk = tmpR.next()
                dec, deck = decR.next()
                P.stt("dve", tmp, ps5[:, 256:384], -1.0, mneg, ALU.mult, ALU.add, [("ps5", 2), ("small",)], [tmpk])
                P.act(dec, tmp, AF.Exp, [tmpk] + SK, [deck], bias=col(9))
                tmp2, tmp2k = tmpR.next()
                decT, decTk = decTR.next()
                P.tt("dve", tmp2, ps5[:, 256:384], mnegT, ALU.add, [("ps5", 2), ("small",)], [tmp2k])
                P.act(decT, tmp2, AF.Exp, [tmp2k] + SK, [decTk], bias=col(11))
                X0, X0k = XR.next()
                P.stt("dve", X0, ps5[:, 0:128], col(4), dec, ALU.mult, ALU.mult, [("ps5", 0), deck] + SK, [X0k])
                P.tt("dve", X0, X0, sneg, ALU.mult, [X0k, ("small",)], [X0k])
                qkT, qkTk = qkTR.next()
                P.tt("dve", qkT, ps5[:, 128:256], decT, ALU.mult, [("ps5", 1), decTk], [qkTk])
                ps6 = self.ps[6]
                ps7 = self.ps[7]
                P.mm(ps6[:, 0:128], X0, ident_f, True, True, [X0k, ("small",)], [("ps6", 0)])
                X0T, X0Tk = XTR.next()
                P.copy("act", X0T, ps6[:, 0:128], [("ps6", 0)], [X0Tk])
                PT, PTk = PTR.next()
                P.tt("dve", PT, ps6[:, 0:128], ident_f, ALU.add, [("ps6", 0), ("small",)], [PTk])
                Xp, Xpk, XpT, XpTk = X0, X0k, X0T, X0Tk
                for kk in range(1, 6):
                    Xn, Xnk = XR.next()
                    P.mm(ps6[:, 128:256], XpT, Xp, True, True, [XpTk, Xpk], [("ps6", 1)])
                    P.copy("act", Xn, ps6[:, 128:256], [("ps6", 1)], [Xnk])
                    if kk < 5:
                        XnT, XnTk = XTR.next()
                        P.mm(ps6[:, 256:384], Xp, XpT, True, True, [XpTk, Xpk], [("ps6", 2)])
                        P.copy("act", XnT, ps6[:, 256:384], [("ps6", 2)], [XnTk])
                    P.mm(ps7[:, 0:128], Xn, PT, True, True, [Xnk, PTk], [("ps7", 0)])
                    PTn, PTnk = PTR.next()
                    P.tt("dve", PTn, ps7[:, 0:128], PT, ALU.add, [("ps7", 0), PTk], [PTnk])
                    PT, PTk = PTn, PTnk
                    Xp, Xpk = Xn, Xnk
                    if kk < 5:
                        XpT, XpTk = XnT, XnTk
                PTb, PTbk = PTbR.next()
                P.copy("act", PTb, PT, [PTk], [PTbk])
                wT, wTk = wTR.next()
                u, uk = uR.next()
                P.mm(ps7[:, 128:256], Kb, PTb, True, True, [Kbk, PTbk], [("ps7", 1)])
                P.copy("act", wT, ps7[:, 128:256], [("ps7", 1)], [wTk])
                P.mm(ps7[:, 256:384], PTb, Vb, True, True, [Vbk, PTbk], [("ps7", 2)])
                P.copy("act", u, ps7[:, 256:384], [("ps7", 2)], [uk])
                gsb, gsbk = gsbR.next()
                P.copy("act", gsb, psg[:, 0:128], [("psg",)], [gsbk])
                st1 = P.capture_end()
                P.capture_begin()
                o, ok_ = oR.next()
                ps1 = self.ps[1]
                for cch in range(2):
                    rows = slice(cch * 64, cch * 64 + 64)
                    Kd, Kdk = (Kd0, Kd0k) if cch == 0 else (Kd1, Kd1k)
                    vn, vnk = vnR.next()
                    P.mm(ps1[:, 0:128], wT, S_b, True, True, [wTk, ("S_b",)], [("ps1", 0)])
                    P.tt("dve", vn, u, ps1[:, 0:128], ALU.subtract, [uk, ("ps1", 0)], [vnk])
                    P.mm(ps1[:, 128:256], QdT, S_b, True, False, [QdTk, ("S_b",)], [("ps1", 1)])
                    P.mm(ps1[:, 128:256], qkT, vn, False, True, [qkTk, vnk], [("ps1", 1)])
                    P.copy("act", o[rows, :], ps1[rows, 128:256], [("ps1", 1)], [ok_])
                    P.mm(ps1[:, 256:384], Kd, vn, True, True, [Kdk, vnk], [("ps1", 2)])
                    P.stt("dve", S_f, S_f, dS[:, cch:cch + 1], ps1[:, 256:384], ALU.mult, ALU.add,
                          [("S_f",), dSk, ("ps1", 2)], [("S_f",)])
                    P.copy("act", S_b, S_f, [("S_f",)], [("S_b",)])
                P.act(junk, o, AF.Square, [ok_], [("junk",), sck], accum_out=col(20))
                P.ts("dve", col(21), col(20), 1.0 / 128.0, EPS, ALU.mult, ALU.add, SK, SK)
                P.act(col(21), col(21), AF.Ln, SK, SK)
                P.act(col(21), col(21), AF.Exp, SK, SK, scale=-0.5)
                sg, sgk = sgR.next()
                P.act(sg, gsb, AF.Exp, [gsbk], [sgk], scale=-1.0)
                P.ts("dve", sg, sg, 1.0, None, ALU.add, None, [sgk], [sgk])
                P.op("dve", lambda e, a=sg: e.reciprocal(a, a), [sgk], [sgk])
                P.tt("dve", sg, gsb, sg, ALU.mult, [gsbk, sgk], [sgk])
                y, yk = yR.next()
                P.stt("dve", y, o, col(21), ogn, ALU.mult, ALU.mult, [ok_, ("small",)] + SK, [yk])
                mt, mtk = mtR.next()
                P.tt("dve", mt, y, sg, ALU.mult, [yk, sgk], [mtk])
                P.mm(ps1[:, 384:512], mt, self.ident_bf[:, :], True, True, [mtk, ("identb",)], [("ps1", 3)])
                P.copy("act", stg[:, c0:c0 + 128], ps1[:, 384:512], [("ps1", 3)], [stgk])
                if tl == 3:
                    P.dma("sp", self.mix_loc[:, b * 512:(b + 1) * 512], stg, f"mo{b % 2}", [stgk],
                          [("mix_loc", b)])
                st2 = P.capture_end()
                P.replay(prev_st2, st1)
                prev_st2 = st2
        P.replay(prev_st2, [])
        P.cc("AllGather", [self.mix_loc.ap().opt()], [self.mix_all.ap().opt()], "ag",
             [("mix_loc", b) for b in range(NB)], [("mix_all",)])
        oh = self.sm("onehot")
        mall = self.mix_all.ap().rearrange("(h p) (j r t) -> h p j r t", p=128, j=4, r=NCORES)
        P.barrier()
        self.aoff = mark
        mixsel = self.carve([128, H, T], BF16)
        candR = Rot("cand", [self.carve([128, NCORES, 512], BF16) for _ in range(2)])
        sacc = self.carve([128, 512], F32)
        n_ = 0
        for tb in range(NTB):
            for ch in range(H):
                cand, ck = candR.next()
                P.dma("sp", cand, mall[ch, :, tb, :, :], f"cand{n_ % 2}", [("mix_all",)], [ck])
                n_ += 1
                P.ts("dve", sacc, cand[:, 0, :], oh[:, 0:1], None, ALU.mult, None, [ck, ("small",)], [("sacc",)])
                for r in range(1, NCORES):
                    last = r == NCORES - 1
                    dst = mixsel[:, ch, tb * 512:(tb + 1) * 512] if last else sacc
                    P.stt("dve", dst, cand[:, r, :], oh[:, r:r + 1], sacc, ALU.mult, ALU.add,
                          [ck, ("sacc",), ("small",)], [("mix", ch, tb)] if last else [("sacc",)])
        self.outproj(l, lambda c, tb: (mixsel[:, c, tb * 512:(tb + 1) * 512] if c < H
                                       else mixmem[:, c - H, tb * 512:(tb + 1) * 512]),
                     lambda c, tb: [("mix", c, tb)])

    def rope_load(self, dstC, dstS, c0, n, key):
        P = self.P
        P.dma("sp", dstC[0:64, 0:n], self.rope_dram[0][:, c0:c0 + n], "ropeldC", [("rope_dram", 0)], [key + ("C",)])
        P.dma("sp", dstS[0:64, 0:n], self.rope_dram[1][:, c0:c0 + n], "ropeldS", [("rope_dram", 1)], [key + ("S",)])

    def kv_build(self):
        P = self.P
        self.arena_reset()
        self.rope_tables()
        self.arena_reset()
        self.norm_x("kv_in_norm")
        wdkv = self.carve([128, KD, 384], BF16)
        P.dma("pool", wdkv, self.wdkv_in.rearrange("(k p) c -> p k c", p=128), "aw_dkv", [], [("wdkv",)])
        kst = self.carve([128, 3, T], BF16)
        P.memset("dve", kst[64:128, 2, :], 0.0, [("kst2",)])
        P.memset("dve", kst[64:65, 2, :], 1.0, [("kst2",)])
        cst = self.carve([128, 16, 256], BF16)
        ckf = self.carve([128, 2, 512], F32)
        CCb = self.carve([128, 512], F32)
        SSb = self.carve([128, 512], F32)
        t1 = self.carve([128, 512], F32)
        t2 = self.carve([128, 512], F32)
        kvl = self.sm("kvlat")
        for tb in range(NTB):
            c0 = tb * 512
            self.rope_load(CCb, SSb, c0, 512, ("ropeb",))
            for c in range(2):
                ps = self.ps[c]
                for k in range(KD):
                    P.mm(ps[:, :], wdkv[:, k, c * 128:(c + 1) * 128], self.xn[:, k, c0:c0 + 512],
                         k == 0, k == KD - 1, [("wdkv",), ("xn", k, tb)], [("ps", c)])
                P.copy("act", ckf[:, c, :], ps[:, :], [("ps", c)], [("ckf", c)])
            for c in range(2):
                ps = self.ps[2 + c]
                for k in range(KD):
                    P.mm(ps[0:64, :], wdkv[:, k, 256 + c * 64:320 + c * 64], self.xn[:, k, c0:c0 + 512],
                         k == 0, k == KD - 1, [("wdkv",), ("xn", k, tb)], [("ps", 2 + c)])
            self.norm_fm(lambda k: ckf[:, k, :], lambda k: [("ckf", k)], 2, 512, 256.0, kvl,
                         lambda k: kst[:, k, c0:c0 + 512], lambda k: [("kst", k, tb)])
            P.tt("dve", t1[0:64, :], self.ps[2][0:64, :], CCb[0:64, :], ALU.mult, [("ps", 2), ("ropeb", "C")], [("t1",)])
            P.tt("dve", t2[0:64, :], self.ps[3][0:64, :], SSb[0:64, :], ALU.mult, [("ps", 3), ("ropeb", "S")], [("t2",)])
            P.tt("dve", kst[0:64, 2, c0:c0 + 512], t1[0:64, :], t2[0:64, :], ALU.add, [("t1",), ("t2",)],
                 [("kst", 2, tb)])
            for tl in range(4):
                ti = tb * 4 + tl
                ps = self.ps[4 + (tl % 2)]
                for c in range(2):
                    P.mm(ps[:, c * 128:(c + 1) * 128], kst[:, c, ti * 128:(ti + 1) * 128], self.ident_bf[:, :],
                         True, True, [("kst", c, tb), ("identb",)], [("ps", 4 + (tl % 2))])
                P.copy("act", cst[:, ti, :], ps[:, 0:256], [("ps", 4 + (tl % 2))], [("cst", ti)])
        for c in range(3):
            P.dma("sp", self.kT_loc[c * 128:(c + 1) * 128, :], kst[:, c, :], "kvo",
                  [("kst", c, tb) for tb in range(NTB)] + [("kst2",)], [("kT_loc", c)])
        P.dma("sp", self.c_loc.ap().rearrange("(n p) c -> p n c", p=128), cst, "kvo_c",
              [("cst", ti) for ti in range(16)], [("c_loc",)])
        P.cc("AllGather", [self.kT_loc.ap().opt()], [self.kT_all.ap().opt()], "ag",
             [("kT_loc", c) for c in range(3)], [("kT_all",)])
        P.cc("AllGather", [self.c_loc.ap().opt()], [self.c_all.ap().opt()], "ag", [("c_loc",)], [("c_all",)])

    def b_mixer(self, j):
        P = self.P
        l = 2 + j
        g = f"mx_{l}"
        self.arena_reset()
        self.norm_x(f"mix_norm{l}")
        self.w_gather(f"f2_{l}")
        if l + 1 < DEPTH:
            self.w_gather(f"f1_{l + 1}")
            self.w_gather(f"mx_{l + 1}")
        mixT = self.carve([128, KD, T], BF16)
        cqn = self.carve([128, 2, T], BF16)
        mark = self.aoff
        wbin = self.carve([128, KD, 512], BF16)
        src = self.wview(g)[:, 128 * D + 128 * 512:128 * D + 2 * 128 * 512].rearrange("r (p f) -> p r f", p=128)
        P.dma("sp", wbin, src, "wbin", [("wfull", g)], [("wbin",)])
        qmT = self.carve([128, 2, T], BF16)
        cqf = self.carve([128, 2, 512], F32)
        for tb in range(NTB):
            c0 = tb * 512
            for c in range(4):
                ps = self.ps[c % 2]
                for k in range(KD):
                    P.mm(ps[:, :], wbin[:, k, c * 128:(c + 1) * 128], self.xn[:, k, c0:c0 + 512],
                         k == 0, k == KD - 1, [("wbin",), ("xn", k, tb)], [("ps", c % 2)])
                if c < 2:
                    P.copy("act", cqf[:, c, :], ps[:, :], [("ps", c % 2)], [("cqf", c)])
                else:
                    P.copy("act", qmT[:, c - 2, c0:c0 + 512], ps[:, :], [("ps", c % 2)], [("qm", c - 2, tb)])
            self.norm_fm(lambda k: cqf[:, k, :], lambda k: [("cqf", k)], 2, 512, 256.0, self.sm(f"bqnorm{j}"),
                         lambda k: cqn[:, k, c0:c0 + 512], lambda k: [("cqn", k, tb)])
        self.mem_attn(l, qmT, lambda m2, tb: [("qm", m2, tb)],
                      lambda m2, tb: mixT[:, 6 + m2, tb * 512:(tb + 1) * 512])
        P.barrier()
        self.aoff = mark
        wuq = self.carve([128, 2, 1536], BF16)
        wukT = self.carve([128, H, 256], BF16)
        wuv = self.carve([128, 2, 768], BF16)
        visR = Rot("vis", [self.carve([128, 256], F32) for _ in range(2)])
        P.dma("pool", wuq, self.wuq_in[j].rearrange("(k p) c -> p k c", p=128), "aw_uq", [], [("wuq",)])
        P.dma("pool", wukT, self.wukT_in.rearrange("h p c -> p h c"), "aw_ukT", [], [("wukT",)])
        P.dma("pool", wuv, self.wuv_in.rearrange("(k p) c -> p k c", p=128), "aw_uv", [], [("wuv",)])
        QaR = Rot("Qa", [self.carve([128, 3, 768], BF16) for _ in range(2)])
        for qa, qk_ in [QaR.next() for _ in range(2)]:
            P.memset("dve", qa[64:128, 2, :], 0.0, [qk_])
        qn = Rot("qn", [self.carve([128, 128], BF16) for _ in range(2)])
        CCq = self.carve([128, 128], F32)
        SSq = self.carve([128, 128], F32)
        r1 = self.carve([128, 128], F32)
        r2 = self.carve([128, 128], F32)
        kTR = Rot("kT4", [self.carve([128, 3, 512], BF16) for _ in range(2)])
        c4R = Rot("c4", [self.carve([128, 4, 256], BF16) for _ in range(2)])
        ptR = Rot("pt", [self.carve([128, 768], BF16) for _ in range(3)])
        rl = self.carve([128, H], F32)
        olnR = Rot("oln", [self.carve([128, 256], BF16) for _ in range(2)])
        oltR = Rot("olt", [self.carve([128, 2, 128], BF16) for _ in range(2)])
        sT = [self.ps2[0], self.ps2[1]]
        lsum = self.ps[7][:, 0:H]
        kall = self.kT_all.ap().rearrange("(r c p) t -> r p c t", r=NCORES, c=3)
        call = self.c_all.ap().rearrange("(r j n p) c -> r j p n c", r=NCORES, j=4, n=4)
        bctr = 0
        for i in range(16):
            c0 = i * 128
            qa, qak = QaR.next()
            visb, visk = visR.next()
            P.dma("sp", visb, self.visb_in[:, i * 256:(i + 1) * 256], f"vis{visk[1]}", [], [visk])
            self.rope_load(CCq, SSq, c0, 128, ("ropeq",))
            ps7 = self.ps[7]
            for h in range(H):
                qnt, qnk = qn.next()
                for k2 in range(2):
                    P.mm(ps7[:, 0:128], wuq[:, k2, h * 256:h * 256 + 128], cqn[:, k2, c0:c0 + 128],
                         k2 == 0, k2 == 1, [("wuq",)] + [("cqn", k2, i // 4)], [("ps7", 0)])
                P.copy("act", qnt, ps7[:, 0:128], [("ps7", 0)], [qnk])
                for rr in range(2):
                    for k2 in range(2):
                        P.mm(ps7[0:64, 128 + rr * 128:256 + rr * 128],
                             wuq[:, k2, h * 256 + 128 + rr * 64:h * 256 + 192 + rr * 64],
                             cqn[:, k2, c0:c0 + 128], k2 == 0, k2 == 1,
                             [("wuq",)] + [("cqn", k2, i // 4)], [("ps7", 1 + rr)])
                P.tt("dve", r1[0:64, :], ps7[0:64, 128:256], CCq[0:64, :], ALU.mult, [("ps7", 1), ("ropeq", "C")], [("r1",)])
                P.stt("dve", r2[0:64, :], ps7[0:64, 256:384], SCALE_B, SSq[0:64, :], ALU.mult, ALU.mult,
                      [("ps7", 2), ("ropeq", "S")], [("r2",)])
                P.stt("dve", qa[0:64, 2, h * 128:(h + 1) * 128], r1[0:64, :], SCALE_B, r2[0:64, :],
                      ALU.mult, ALU.add, [("r1",), ("r2",)], [qak])
                for m in range(2):
                    P.mm(ps7[:, 384:512], wukT[:, h, m * 128:(m + 1) * 128], qnt, True, True,
                         [("wukT",), qnk], [("ps7", 3)])
                    P.ts("dve", qa[:, m, h * 128:(h + 1) * 128], ps7[:, 384:512], SCALE_B, None, ALU.mult, None,
                         [("ps7", 3)], [qak])
            nkb = 32 * (i // 4 + 1)
            ngr = (nkb + 3) // 4
            for gq in range(ngr):
                kT4, kTk = kTR.next()
                c4, c4k = c4R.next()
                r, jb = gq % NCORES, gq // NCORES
                P.dma("sp", kT4, kall[r, :, :, jb * 512:(jb + 1) * 512], f"kT{kTk[1]}", [("kT_all",)], [kTk])
                P.dma("sp", c4, call[r, jb], f"c4{c4k[1]}", [("c_all",)], [c4k])
                for kk in range(4):
                    kb = gq * 4 + kk
                    if kb >= nkb:
                        break
                    sb = bctr % 2
                    bctr += 1
                    st = sT[sb]
                    for (a0, a1) in ((0, 512), (512, 768)):
                        for c in range(3):
                            rows = slice(0, 128) if c < 2 else slice(0, 64)
                            P.mm(st[:, a0:a1], kT4[rows, c, kk * 128:(kk + 1) * 128], qa[rows, c, a0:a1],
                                 c == 0, c == 2, [kTk, qak], [("sT", sb, a0)])
                    pt, ptk = ptR.next()
                    for qc in range(2):
                        vcol = kb * 2 + qc
                        P.act(pt.rearrange("p (h q) -> p h q", h=H)[:, :, qc * 64:(qc + 1) * 64],
                              st[:, 0:768].rearrange("p (h q) -> p h q", h=H)[:, :, qc * 64:(qc + 1) * 64],
                              AF.Exp, [("sT", sb, 0), ("sT", sb, 512), visk], [ptk],
                              bias=visb[:, vcol:vcol + 1])
                    first, last = kb == 0, kb == nkb - 1
                    for h in range(H):
                        acc = self.ps[4 + h // 2][:, (h % 2) * 256:(h % 2) * 256 + 256]
                        P.mm(acc, pt[:, h * 128:(h + 1) * 128], c4[:, kk, :], first and h % 2 == 0, last,
                             [ptk, c4k], [("olat", h)], skip=True)
                        P.mm(lsum[:, h:h + 1], pt[:, h * 128:(h + 1) * 128], self.ones_bf[:, 0:1],
                             first and h == 0, last, [ptk, ("ones",)], [("lsum",)], skip=True)
            P.op("dve", lambda e, a=rl, b=lsum: e.reciprocal(a, b[:, 0:H]), [("lsum",)], [("rl",)])
            for h in range(H):
                acc = self.ps[4 + h // 2][:, (h % 2) * 256:(h % 2) * 256 + 256]
                oln, olnk = olnR.next()
                P.ts("dve", oln, acc, rl[:, h:h + 1], None, ALU.mult, None, [("olat", h), ("rl",)], [olnk])
                olt, oltk = oltR.next()
                for m in range(2):
                    P.mm(ps7[:, 128 + m * 128:256 + m * 128], oln[:, m * 128:(m + 1) * 128], self.ident_bf[:, :],
                         True, True, [olnk, ("identb",)], [("ps7", 1 + m)])
                    P.copy("act", olt[:, m, :], ps7[:, 128 + m * 128:256 + m * 128], [("ps7", 1 + m)], [oltk])
                for m in range(2):
                    P.mm(ps7[:, 384:512], wuv[:, m, h * 128:(h + 1) * 128], olt[:, m, :], m == 0, m == 1,
                         [("wuv",), oltk], [("ps7", 3)])
                P.copy("act", mixT[:, h, c0:c0 + 128], ps7[:, 384:512], [("ps7", 3)], [("mixq", h, i)])
        P.barrier()
        self.aoff = mark
        self.outproj(l, lambda c, tb: mixT[:, c, tb * 512:(tb + 1) * 512],
                     lambda c, tb: ([("mix", c, tb)] if c >= H else [("mixq", c, 4 * tb + q_) for q_ in range(4)]))

    def final_out(self):
        P = self.P
        self.arena_reset()
        xo = Rot("xo", [self.carve([128, 512], F32) for _ in range(8)])
        for tb in range(NTB):
            c0 = tb * 512
            outs = {}

            def dstf(k):
                t, kk = xo.next()
                outs[k] = (t, kk)
                return t
            self.norm_fm(lambda k: self.xT[:, k, c0:c0 + 512], lambda k: [("x", k, tb)], KD, 512, D,
                         self.sm("final_norm"), dstf, lambda k: [outs[k][1]])
            for k in range(KD):
                P.dma("sp", self.outT[k * 128:(k + 1) * 128, c0:c0 + 512], outs[k][0], f"out{k % 4}",
                      [outs[k][1]], [("out", k, tb)])
        P.wait_all("sp", [("out", k, tb) for k in range(KD) for tb in range(NTB)] + getattr(self, "dbg_keys", []))

    def dump_stage(self, name):
        if not self.dbg:
            return
        t = self.nc.dram_tensor("dbg_" + name, [D, T], F32, kind="ExternalOutput").ap()
        for k in range(KD):
            self.P.dma("sp", t[k * 128:(k + 1) * 128, :], self.xT[:, k, :], f"dbg{k % 4}",
                       [("x", k, tb) for tb in range(NTB)], [("dbgout", name, k)])
        self.dbg_keys = getattr(self, "dbg_keys", []) + [("dbgout", name, k) for k in range(KD)]

    def dump_x(self):
        P = self.P
        for k in range(KD):
            P.dma("sp", self.outT[k * 128:(k + 1) * 128, :], self.xT[:, k, :], f"out{k % 4}",
                  [("x", k, tb) for tb in range(NTB)], [("out", k)])
        P.wait_all("sp", [("out", k) for k in range(KD)])

    def build(self):
        P = self.P
        sa = self.stop_after
        P.dma("sp", self.small[:, :], self.small_in[:, :], "small", [], [("small",)])
        for k in range(KD):
            P.dma("sp", self.xT[:, k, :], self.xT_in[k * 128:(k + 1) * 128, :], f"xin{k}", [],
                  [("x", k, tb) for tb in range(NTB)])
        P.memset("dve", self.ones_bf[:, :], 1.0, [("ones",)])
        P.memset("dve", self.ones_f[:, :], 1.0, [("onesf",)])
        P.copy("dve", self.ident_bf[:, :], self.sm("ident"), [("small",)], [("identb",)])
        for l in range(DEPTH):
            for part in ("f1", "mx", "f2"):
                self.w_cast(f"{part}_{l}")
        self.w_gather("f1_0")
        self.w_gather("mx_0")
        self.aoff = 0
        memf = self.carve([128, KD, 256], F32)
        P.dma("sp", memf, self.memT_in.rearrange("(k p) m -> p k m", p=128), "memin", [], [("memf",)])
        self.norm_fm(lambda k: memf[:, k, :], lambda k: [("memf",)], KD, 256, D, self.sm("mem_norm"),
                     lambda k: self.memn[:, k, :], lambda k: [("memn",)])
        done = False
        for l in range(DEPTH):
            self.ffn(l, 1)
            self.dump_stage(f"x_ffn1_{l}")
            if sa == ("ffn1", l):
                done = True
                break
            if l < 2:
                self.a_mixer(l)
            else:
                if l == 2:
                    pass
                self.b_mixer(l - 2)
            self.dump_stage(f"x_mix_{l}")
            if sa == ("mix", l):
                done = True
                break
            self.ffn(l, 2)
            self.dump_stage(f"x_l_{l}")
            if sa == ("ffn2", l):
                done = True
                break
            if l == 1:
                self.kv_build()
        if done:
            self.arena_reset()
            self.dump_x()
        else:
            self.final_out()
        P.emit()
        return self.nc


def _build(small_off, n_small, stop_after=None, dbg=False):
    b = Builder(small_off, n_small, stop_after, dbg)
    nc = b.build()
    return nc, b


def kernel(_stop_after=None, _dbg=False, **inp):
    sp0, _ = pack_small(inp, 0)
    nc, b = _build(sp0.off, sp0.n, _stop_after, _dbg)
    in_maps = []
    for c in range(NCORES):
        m = host_inputs(inp, c)
        in_maps.append(m)
    res = run_bass_kernel_spmd(nc, in_maps, core_ids=list(range(NCORES)))
    out = np.empty((S, D), np.float32)
    for c in range(NCORES):
        out[_tok_idx(c), :] = np.asarray(res.results[c]["outT"]).T
    out = np.ascontiguousarray(out.reshape(1, S, D))
    if _dbg:
        return out, res.results
    return out
```
